# Optimizing a Trainium2 kernel written in Bass

```python
import jax, jax.numpy as jnp
from jax import lax
import numpy as np

D_MODEL = 1024
BATCH = 8
SEQ = 4096
DEPTH = 2

HG_EXPAND = 128
HG_HEADS = D_MODEL // HG_EXPAND
HG_DK = HG_EXPAND
HG_DV = D_MODEL // HG_HEADS
HG_QK = HG_HEADS * HG_DK
HG_WIDTH = HG_HEADS * HG_DV
HG_CHUNK = 64
RET_HEADS = D_MODEL // 256
RET_DK = D_MODEL // RET_HEADS
RET_DV = 2 * RET_DK
RET_QK = RET_HEADS * RET_DK
RET_WIDTH = RET_HEADS * RET_DV
RET_CHUNK = 128
ROPE_BASE = 10000.0
D_FF = 4 * D_MODEL
DEEPNORM_ALPHA = (2 * DEPTH) ** 0.25
DEEPNORM_BETA = (8 * DEPTH) ** -0.25
LN_EPS = 1e-5
HEAD_NORM_EPS = 1e-6
ADA_SCALE = 0.1
IN_SIZES = (HG_QK, HG_QK, HG_WIDTH, HG_WIDTH,
            RET_QK, RET_QK, RET_WIDTH, RET_WIDTH,
            D_MODEL, D_MODEL)
IN_WIDTH = sum(IN_SIZES)

kernel_name = "hgrn2_retention_gated_hybrid_deepnorm_adaln"


def _layer_norm(x, g, b):
    x32 = x.astype(jnp.float32)
    mu = jnp.mean(x32, axis=-1, keepdims=True)
    var = jnp.mean(jnp.square(x32 - mu), axis=-1, keepdims=True)
    return ((x32 - mu) * lax.rsqrt(var + LN_EPS)).astype(x.dtype) * g + b


def _head_norm(o, centered):
    if centered:
        o = o - jnp.mean(o, axis=-1, keepdims=True)
    return o * lax.rsqrt(jnp.mean(jnp.square(o), axis=-1, keepdims=True) + HEAD_NORM_EPS)


def _to_chunks(t, chunk):
    b, s, h, d = t.shape
    return t.reshape(b, s // chunk, chunk, h, d).transpose(1, 0, 3, 2, 4)


def _from_chunks(t):
    n, b, h, c, d = t.shape
    return t.transpose(1, 0, 3, 2, 4).reshape(b, n * c, h, d)


def _hgrn2(q, zf, v, g, lb):
    B, S, _ = q.shape
    f32 = jnp.float32
    zf = zf.astype(f32)
    lb = lb.astype(f32)
    q = jax.nn.silu(q.astype(f32)).reshape(B, S, HG_HEADS, HG_DK)
    log_f = jnp.logaddexp(jnp.log(lb), jnp.log1p(-lb) + jax.nn.log_sigmoid(zf))
    k = (1.0 - lb) * jax.nn.sigmoid(-zf)
    log_f = log_f.reshape(B, S, HG_HEADS, HG_DK)
    k = k.reshape(B, S, HG_HEADS, HG_DK)
    v = v.astype(f32).reshape(B, S, HG_HEADS, HG_DV)
    causal = jnp.tril(jnp.ones((HG_CHUNK, HG_CHUNK), dtype=bool))[:, :, None]

    def step(state, blk):
        qc, lfc, kc, vc = blk
        b = jnp.cumsum(lfc, axis=2)
        rel = jnp.where(causal, b[:, :, :, None, :] - b[:, :, None, :, :], -jnp.inf)
        scores = jnp.einsum('bhid,bhijd,bhjd->bhij', qc, jnp.exp(rel), kc)
        b_last = b[:, :, -1:, :]
        o = (jnp.einsum('bhij,bhje->bhie', scores, vc)
             + jnp.einsum('bhid,bhde->bhie', qc * jnp.exp(b), state))
        state = (jnp.exp(b_last[:, :, 0, :, None]) * state
                 + jnp.einsum('bhjd,bhje->bhde', kc * jnp.exp(b_last - b), vc))
        return state, o

    state0 = jnp.zeros((B, HG_HEADS, HG_DK, HG_DV), f32)
    blocks = (_to_chunks(q, HG_CHUNK), _to_chunks(log_f, HG_CHUNK),
              _to_chunks(k, HG_CHUNK), _to_chunks(v, HG_CHUNK))
    _, o = lax.scan(step, state0, blocks)
    o = _head_norm(_from_chunks(o), centered=False).reshape(B, S, HG_WIDTH)
    return (o * jax.nn.silu(g.astype(f32))).astype(g.dtype)


def _rotate(t, cos, sin):
    B, S, H, d = t.shape
    t = t.reshape(B, S, H, d // 2, 2)
    t1, t2 = t[..., 0], t[..., 1]
    return jnp.stack([t1 * cos - t2 * sin, t1 * sin + t2 * cos], axis=-1).reshape(B, S, H, d)


def _retention(q, k, v, g, positions):
    B, S, _ = q.shape
    f32 = jnp.float32
    theta = ROPE_BASE ** (-jnp.linspace(0.0, 1.0, RET_DK // 2, dtype=f32))
    ang = positions.astype(f32)[:, :, None, None] * theta
    cos, sin = jnp.cos(ang), jnp.sin(ang)
    q = _rotate(q.astype(f32).reshape(B, S, RET_HEADS, RET_DK), cos, sin)
    k = _rotate(k.astype(f32).reshape(B, S, RET_HEADS, RET_DK), cos, sin) * (RET_DK ** -0.5)
    v = v.astype(f32).reshape(B, S, RET_HEADS, RET_DV)
    log_gamma = jnp.log1p(-jnp.exp2(-5.0 - jnp.arange(RET_HEADS, dtype=f32)))
    pos = jnp.arange(RET_CHUNK, dtype=f32)
    diff = pos[:, None] - pos[None, :]
    decay_mask = jnp.where(diff >= 0,
                           jnp.exp(log_gamma[:, None, None] * jnp.maximum(diff, 0.0)), 0.0)
    q_decay = jnp.exp(log_gamma[:, None] * (pos + 1.0))[:, :, None]
    k_decay = jnp.exp(log_gamma[:, None] * (RET_CHUNK - 1.0 - pos))[:, :, None]
    chunk_decay = jnp.exp(log_gamma * RET_CHUNK)[:, None, None]

    def step(state, blk):
        qc, kc, vc = blk
        scores = jnp.einsum('bhid,bhjd->bhij', qc, kc) * decay_mask
        o = (jnp.einsum('bhij,bhje->bhie', scores, vc)
             + jnp.einsum('bhid,bhde->bhie', qc * q_decay, state))
        state = chunk_decay * state + jnp.einsum('bhjd,bhje->bhde', kc * k_decay, vc)
        return state, o

    state0 = jnp.zeros((B, RET_HEADS, RET_DK, RET_DV), f32)
    blocks = (_to_chunks(q, RET_CHUNK), _to_chunks(k, RET_CHUNK), _to_chunks(v, RET_CHUNK))
    _, o = lax.scan(step, state0, blocks)
    o = _head_norm(_from_chunks(o), centered=True).reshape(B, S, RET_WIDTH)
    return (o * jax.nn.silu(g.astype(f32))).astype(g.dtype)


def setup_inputs(seed: int = 0) -> dict:
    key = jax.random.key(seed)
    ks = jax.random.split(key, 20)
    f32 = jnp.float32

    def nrm(k, shape, scale):
        return jax.random.normal(k, shape, f32) * scale

    x = nrm(ks[0], (BATCH, SEQ, D_MODEL), 1.0)
    c = nrm(ks[1], (BATCH, D_MODEL), 1.0)
    positions = jnp.broadcast_to(jnp.arange(SEQ, dtype=jnp.int32), (BATCH, SEQ))
    lb_logits = nrm(ks[2], (DEPTH, HG_QK), 0.5)
    w_ada = nrm(ks[3], (DEPTH, D_MODEL, 6 * D_MODEL), ADA_SCALE * D_MODEL ** -0.5)
    b_ada = nrm(ks[4], (DEPTH, 6 * D_MODEL), 0.01)
    w_in = nrm(ks[5], (DEPTH, D_MODEL, IN_WIDTH), D_MODEL ** -0.5)
    w_pa = nrm(ks[6], (DEPTH, HG_WIDTH, D_MODEL), HG_WIDTH ** -0.5)
    w_pb = nrm(ks[7], (DEPTH, RET_WIDTH, D_MODEL), RET_WIDTH ** -0.5)
    w_o = nrm(ks[8], (DEPTH, D_MODEL, D_MODEL), DEEPNORM_BETA * D_MODEL ** -0.5)
    ln1_g = 1.0 + nrm(ks[9], (DEPTH, D_MODEL), 0.02)
    ln1_b = nrm(ks[10], (DEPTH, D_MODEL), 0.02)
    w_up = nrm(ks[11], (DEPTH, D_MODEL, D_FF), D_MODEL ** -0.5)
    b_up = nrm(ks[12], (DEPTH, D_FF), 0.02)
    w_down = nrm(ks[13], (DEPTH, D_FF, D_MODEL), DEEPNORM_BETA * D_FF ** -0.5)
    b_down = nrm(ks[14], (DEPTH, D_MODEL), 0.02)
    ln2_g = 1.0 + nrm(ks[15], (DEPTH, D_MODEL), 0.02)
    ln2_b = nrm(ks[16], (DEPTH, D_MODEL), 0.02)
    return {"x": x, "c": c, "positions": positions, "lb_logits": lb_logits,
            "w_ada": w_ada, "b_ada": b_ada, "w_in": w_in, "w_pa": w_pa, "w_pb": w_pb,
            "w_o": w_o, "ln1_g": ln1_g, "ln1_b": ln1_b, "w_up": w_up, "b_up": b_up,
            "w_down": w_down, "b_down": b_down, "ln2_g": ln2_g, "ln2_b": ln2_b}


def reference(x, c, positions, lb_logits, w_ada, b_ada, w_in, w_pa, w_pb, w_o,
              ln1_g, ln1_b, w_up, b_up, w_down, b_down, ln2_g, ln2_b):
    lb_cum = jnp.cumsum(jax.nn.softmax(lb_logits.astype(jnp.float32), axis=0), axis=0)
    lower_bounds = lb_cum - lb_cum[:1]
    split_at = np.cumsum(IN_SIZES)[:-1].tolist()
    cond = jax.nn.silu(c)
    for l in range(DEPTH):
        mod = cond @ w_ada[l] + b_ada[l]
        shift1, scale1, gate1, shift2, scale2, gate2 = jnp.split(mod[:, None, :], 6, axis=-1)
        u = x * (1.0 + scale1) + shift1
        hq, hf, hi, hg, rq, rk, rv, rg, ga, gb = jnp.split(u @ w_in[l], split_at, axis=-1)
        ya = _hgrn2(hq, hf, hi, hg, lower_bounds[l]) @ w_pa[l]
        yb = _retention(rq, rk, rv, rg, positions) @ w_pb[l]
        y = (jax.nn.sigmoid(ga) * ya + jax.nn.sigmoid(gb) * yb) @ w_o[l]
        x = _layer_norm(DEEPNORM_ALPHA * x + (1.0 + gate1) * y, ln1_g[l], ln1_b[l])
        u = x * (1.0 + scale2) + shift2
        h = jnp.square(jax.nn.relu(u @ w_up[l] + b_up[l]))
        y = h @ w_down[l] + b_down[l]
        x = _layer_norm(DEEPNORM_ALPHA * x + (1.0 + gate2) * y, ln2_g[l], ln2_b[l])
    return x
```

```python
import math
from contextlib import ExitStack
import numpy as np
import concourse.bass as bass
import concourse.mybir as mybir
from concourse.bass_utils import run_bass_kernel_spmd

F32 = mybir.dt.float32
BF16 = mybir.dt.bfloat16
I32 = mybir.dt.int32
AF = mybir.ActivationFunctionType
ALU = mybir.AluOpType
AX = mybir.AxisListType

PE, ACT, DVE, POOL, SP = "tensor", "scalar", "vector", "gpsimd", "sync"
ENGS = (PE, ACT, DVE, POOL, SP)

D = 1024
SEQ = 4096
T = 512
NSUB = 4
DEPTH = 2
ALPHA = (2 * DEPTH) ** 0.25
LN_EPS = 1e-5
HN_EPS = 1e-6
UPL = 48
GAMMAS = [1.0 - 2.0 ** (-5.0 - h) for h in range(4)]


class Op:
    __slots__ = ("eng", "fn", "deps", "need_sig", "sigval", "dma", "dma_val", "waits", "wkeys")

    def __init__(self, eng, fn):
        self.eng = eng
        self.fn = fn
        self.deps = []
        self.need_sig = False
        self.sigval = 0
        self.dma = None
        self.dma_val = 0
        self.waits = {}
        self.wkeys = frozenset()


class Sched:
    def __init__(self):
        self.streams = {e: [] for e in ENGS}
        self.lastw = {}
        self.readers = {}
        self.dma_count = {}
        self.dma_last = {}

    def op(self, eng, fn, reads=(), writes=(), dma=None):
        o = Op(eng, fn)
        reads = tuple(reads)
        writes = tuple(writes)
        deps = set()
        for k in reads:
            w = self.lastw.get(k)
            if w is not None:
                deps.add(w)
        for k in writes:
            w = self.lastw.get(k)
            if w is not None:
                deps.add(w)
            for r in self.readers.get(k, ()):
                deps.add(r)
        if dma is not None:
            prev = self.dma_last.get(dma)
            if prev is not None:
                deps.add(prev)
            self.dma_last[dma] = o
            cnt = self.dma_count.get(dma, 0) + 1
            self.dma_count[dma] = cnt
            o.dma = dma
            o.dma_val = 16 * cnt
        touched = set(reads) | set(writes)
        for d in deps:
            if d is o:
                continue
            if d.dma is None and d.eng == eng:
                if eng == PE:
                    continue
                if not (d.wkeys & touched):
                    continue
            if d.dma is None:
                d.need_sig = True
            o.deps.append(d)
        for k in reads:
            self.readers.setdefault(k, []).append(o)
        for k in writes:
            self.lastw[k] = o
            self.readers[k] = []
        o.wkeys = frozenset(writes)
        self.streams[eng].append(o)
        return o

    def finalize(self):
        for e, st in self.streams.items():
            c = 0
            for o in st:
                if o.dma is None and o.need_sig:
                    c += 1
                    o.sigval = c
        for e, st in self.streams.items():
            known = {}
            for o in st:
                w = {}
                for d in o.deps:
                    if d.dma is not None:
                        key = ("dma", d.dma)
                        val = d.dma_val
                    else:
                        key = ("eng", d.eng)
                        val = d.sigval
                    if val > known.get(key, 0) and val > w.get(key, 0):
                        w[key] = val
                for k, v in w.items():
                    known[k] = v
                o.waits = w

    def emit(self, engobj, eng, engsem, dmasem):
        for o in self.streams[eng]:
            for (kind, name), v in o.waits.items():
                s = engsem[name] if kind == "eng" else dmasem[name]
                engobj.wait_ge(s, v)
            ins = o.fn(engobj)
            if o.dma is not None:
                ins.then_inc(dmasem[o.dma], 16)
            elif o.need_sig:
                ins.then_inc(engsem[o.eng], 1)


def _vec_layout():
    off = {}
    c = 0
    for l in range(DEPTH):
        for nm, n in (("ln1_g", 8), ("ln1_b", 8), ("ln2_g", 8), ("ln2_b", 8), ("b_down", 8), ("b_up", 32), ("b_ada", 48), ("lbl", 8)):
            off[(nm, l)] = c
            c += n
    off["c"] = c
    c += 8
    return off, c


VOFF, NVEC = _vec_layout()
C_IDENT, C_MASKT, C_MH, C_SCANM, C_KDEC, C_QDEC, C_QDEC2, C_THETA, C_NHALF = 0, 128, 256, 768, 1280, 1284, 1288, 1292, 1293
NCONST = 1294 + 2


def build_nc(NT=8, NL=DEPTH, dbg=False):
    nc = bass.Bass("TRN2", target_bir_lowering=False)
    x_d = nc.dram_tensor("x", [SEQ, D], F32, kind="ExternalInput").ap()
    pos_d = nc.dram_tensor("pos", [128, SEQ], I32, kind="ExternalInput").ap()
    wall_d = nc.dram_tensor("wall", [DEPTH * UPL, 128, 4096], F32, kind="ExternalInput").ap()
    wada_d = nc.dram_tensor("wada", [DEPTH * 6, 128, 8192], F32, kind="ExternalInput").ap()
    vec_d = nc.dram_tensor("vecs", [128, NVEC], F32, kind="ExternalInput").ap()
    cst_d = nc.dram_tensor("consts", [128, NCONST], F32, kind="ExternalInput").ap()
    out_d = nc.dram_tensor("out", [SEQ, D], F32, kind="ExternalOutput").ap()
    wscr = nc.dram_tensor("wscr", [DEPTH * UPL, 128, 4096], BF16, kind="Internal").ap()

    S = Sched()
    es = ExitStack()
    with es:
        def sb(name, shape, dt):
            return es.enter_context(nc.sbuf_tensor(name, shape, dt))

        xa = sb("xa", [128, 8, T], F32)
        u = sb("u", [128, 8, T], BF16)
        big = sb("big", [128, 32, T], BF16)
        ring = sb("ring", [128, 4, 4096], BF16)
        FA = sb("FA", [128, 16, T], F32)
        BA = sb("BA", [128, 20, T], BF16)
        lnb = sb("lnb", [128, 4, T], BF16)
        SH = sb("SH", [128, DEPTH * 8, 128], F32)
        SR = sb("SR", [128, DEPTH * 4 * 2, 512], F32)
        HSB = sb("HSB", [128, 4, 128], BF16)
        RSB = sb("RSB", [128, 2, 2, 512], BF16)
        cs = sb("cs", [128, 2, T], F32)
        post = sb("post", [128, T], I32)
        VEC = sb("VEC", [128, NVEC], F32)
        CST = sb("CST", [128, NCONST], F32)
        DV = sb("DV", [128, 512], F32)
        identb = sb("identb", [128, 128], BF16)
        onesb = sb("onesb", [128, 128], BF16)
        maski = sb("maski", [128, 4, 128], I32)
        sm = sb("sm", [128, 128], F32)

        ps = [es.enter_context(nc.psum_tensor("ps%d" % i, [128, 512], F32)) for i in range(8)]

        def F(i):
            return FA[:, i, :]

        def Fk(i):
            return ("F", i)

        def B(i):
            return BA[:, i, :]

        def Bk(i):
            return ("B", i)

        pj = [0]

        def nextproj():
            i = pj[0] % 4
            pj[0] += 1
            return i

        def PK(i):
            return ("ps", i)

        def mm(out, lhsT, rhs, start, stop, reads, writes):
            S.op(PE, lambda e: e.matmul(out, lhsT=lhsT, rhs=rhs, start=start, stop=stop), reads, writes)

        def tr(out, in_, ident, reads, writes):
            S.op(PE, lambda e: e.transpose(out=out, in_=in_, identity=ident), reads, writes)

        def act(out, in_, func, reads, writes, scale=1.0, bias=0.0):
            S.op(ACT, lambda e: e.activation(out=out, in_=in_, func=func, bias=bias, scale=scale), reads, writes)

        def tt(eng, out, in0, in1, op, reads, writes):
            S.op(eng, lambda e: e.tensor_tensor(out=out, in0=in0, in1=in1, op=op), reads, writes)

        def ts(eng, out, in0, s1, s2, op0, op1, reads, writes):
            S.op(eng, lambda e: e.tensor_scalar(out=out, in0=in0, scalar1=s1, scalar2=s2, op0=op0, op1=op1), reads, writes)

        def ts1(eng, out, in0, s1, op0, reads, writes):
            S.op(eng, lambda e: e.tensor_scalar(out=out, in0=in0, scalar1=s1, scalar2=None, op0=op0), reads, writes)

        def stt(out, in0, scalar, in1, op0, op1, reads, writes):
            S.op(DVE, lambda e: e.scalar_tensor_tensor(out=out, in0=in0, scalar=scalar, in1=in1, op0=op0, op1=op1), reads, writes)

        def cp(eng, out, in_, reads, writes):
            S.op(eng, lambda e: e.tensor_copy(out=out, in_=in_), reads, writes)

        def dma(eng, out, in_, reads, writes, ch, **kw):
            S.op(eng, lambda e: e.dma_start(out=out, in_=in_, **kw), reads, writes, dma=ch)

        def dump(name, ap, keys, shape, dt):
            if not dbg:
                return
            d = nc.dram_tensor(name, shape, dt, kind="ExternalOutput").ap()
            dma(POOL, d, ap, keys, [], "dbg")

        ucount = [0]

        def next_unit(uidx):
            slot = ucount[0] % 4
            ucount[0] += 1
            dma(SP, ring[:, slot, :], wscr[uidx], [("wscr", uidx)], [("ring", slot)], "ring%d" % slot)
            return ring[:, slot, :].rearrange("p (k n) -> p k n", n=512), ("ring", slot)

        dma(SP, VEC[:], vec_d, [], ["VEC"], "ldv")
        dma(SP, CST[:], cst_d, [], ["CST"], "ldc")
        for uidx in range(NL * UPL):
            dma(POOL, wscr[uidx], wall_d[uidx], [], [("wscr", uidx)], "wcv%d" % (uidx % 4), max_dma_last_dim=4096)

        cp(DVE, identb[:], CST[:, C_IDENT:C_IDENT + 128], ["CST"], ["identb"])
        S.op(DVE, lambda e: e.memset(onesb[:], 1.0 / 1024.0), [], ["onesb"])
        for c in range(4):
            cp(DVE, maski[:, c, :], CST[:, C_MASKT:C_MASKT + 128], ["CST"], ["maski"])
        S.op(DVE, lambda e: e.memset(SH[:], 0.0), [], [("SH", i) for i in range(DEPTH * 8)])
        S.op(DVE, lambda e: e.memset(SR[:], 0.0), [], [("SR", i) for i in range(DEPTH * 8)])
        identf = CST[:, C_IDENT:C_IDENT + 128]

        DVO = {}
        dvc = [0]

        def dv(name, n=8):
            DVO[name] = dvc[0]
            dvc[0] += n
            return DVO[name]

        def DVs(name, n=8):
            o = DVO[name]
            return DV[:, o:o + n]

        def Vs(name, l, n=8):
            o = VOFF[(name, l)]
            return VEC[:, o:o + n]

        dv("cond")
        dv("tmp")
        cfm = VEC[:, VOFF["c"]:VOFF["c"] + 8]
        act(DVs("tmp"), cfm, AF.Exp, ["VEC"], ["DV"], scale=-1.0)
        act(DVs("tmp"), DVs("tmp"), AF.Ln, ["DV"], ["DV"], bias=1.0)
        act(DVs("tmp"), DVs("tmp"), AF.Exp, ["DV"], ["DV"], scale=-1.0)
        tt(DVE, DVs("cond"), cfm, DVs("tmp"), ALU.mult, ["VEC", "DV"], ["DV"])
        bigf = big[:].rearrange("p a t -> p (a t)").bitcast(F32)
        bigkeys = [("big", i) for i in range(32)]
        for l in range(NL):
            dv(("mod", l), 48)
            for g in range(6):
                dma(SP, bigf, wada_d[l * 6 + g], [], bigkeys, "ada")
                wv = bigf.rearrange("p (k n) -> p k n", n=1024)
                for jb in range(8):
                    jj = g * 8 + jb
                    for kc in range(8):
                        mm(ps[0][:, jj:jj + 1], wv[:, kc, jb * 128:(jb + 1) * 128], DV[:, DVO["cond"] + kc:DVO["cond"] + kc + 1],
                           kc == 0, kc == 7, bigkeys + ["DV"], [PK(0)])
            tt(DVE, DVs(("mod", l), 48), ps[0][:, 0:48], Vs("b_ada", l, 48), ALU.add, [PK(0), "VEC"], ["DV"])

        def MOD(l, which):
            o = DVO[("mod", l)] + which * 8
            return DV[:, o:o + 8]

        dv("lb", 16)
        dv("lnom", 16)
        S.op(DVE, lambda e: e.memset(DVs("lb", 16), 0.0), [], ["DV"])
        if NL > 1:
            lb1 = DV[:, DVO["lb"] + 8:DVO["lb"] + 16]
            tt(DVE, DVs("tmp"), Vs("lbl", 1), Vs("lbl", 0), ALU.subtract, ["VEC"], ["DV"])
            act(DVs("tmp"), DVs("tmp"), AF.Exp, ["DV"], ["DV"], scale=-1.0)
            act(DVs("tmp"), DVs("tmp"), AF.Ln, ["DV"], ["DV"], bias=1.0)
            act(lb1, DVs("tmp"), AF.Exp, ["DV"], ["DV"], scale=-1.0)
        act(DVs("lnom", 16), DVs("lb", 16), AF.Ln, ["DV"], ["DV"], scale=-1.0, bias=1.0)

        dv("one_s", 8)
        for l in range(NL):
            for nm in ("G1h", "G2", "Ax1", "Bx1", "Au1", "Bu1", "Ax2", "Bx2", "Au2", "Bu2"):
                dv((nm, l))
        dv("A0u")
        dv("B0u")
        ts1(DVE, DVs("A0u"), MOD(0, 1), 1.0, ALU.add, ["DV"], ["DV"])
        cp(DVE, DVs("B0u"), MOD(0, 0), ["DV"], ["DV"])
        for l in range(NL):
            ts1(DVE, DVs(("G1h", l)), MOD(l, 2), 1.0, ALU.add, ["DV"], ["DV"])
            ts1(DVE, DVs(("G2", l)), MOD(l, 5), 1.0, ALU.add, ["DV"], ["DV"])
            ts1(DVE, DVs(("Ax1", l)), Vs("ln1_g", l), ALPHA, ALU.mult, ["VEC"], ["DV"])
            tt(DVE, DVs("tmp"), DVs(("G2", l)), Vs("b_down", l), ALU.mult, ["DV", "VEC"], ["DV"])
            stt(DVs(("Bx1", l)), Vs("ln1_b", l), ALPHA, DVs("tmp"), ALU.mult, ALU.add, ["VEC", "DV"], ["DV"])
            ts1(DVE, DVs("one_s"), MOD(l, 4), 1.0, ALU.add, ["DV"], ["DV"])
            tt(DVE, DVs(("Au1", l)), Vs("ln1_g", l), DVs("one_s"), ALU.mult, ["VEC", "DV"], ["DV"])
            tt(DVE, DVs("tmp"), Vs("ln1_b", l), DVs("one_s"), ALU.mult, ["VEC", "DV"], ["DV"])
            tt(DVE, DVs(("Bu1", l)), DVs("tmp"), MOD(l, 3), ALU.add, ["DV"], ["DV"])
            if l < NL - 1:
                ts1(DVE, DVs(("Ax2", l)), Vs("ln2_g", l), ALPHA, ALU.mult, ["VEC"], ["DV"])
                ts1(DVE, DVs(("Bx2", l)), Vs("ln2_b", l), ALPHA, ALU.mult, ["VEC"], ["DV"])
                ts1(DVE, DVs("one_s"), MOD(l + 1, 1), 1.0, ALU.add, ["DV"], ["DV"])
                tt(DVE, DVs(("Au2", l)), Vs("ln2_g", l), DVs("one_s"), ALU.mult, ["VEC", "DV"], ["DV"])
                tt(DVE, DVs("tmp"), Vs("ln2_b", l), DVs("one_s"), ALU.mult, ["VEC", "DV"], ["DV"])
                tt(DVE, DVs(("Bu2", l)), DVs("tmp"), MOD(l + 1, 0), ALU.add, ["DV"], ["DV"])
            else:
                cp(DVE, DVs(("Ax2", l)), Vs("ln2_g", l), ["VEC"], ["DV"])
                cp(DVE, DVs(("Bx2", l)), Vs("ln2_b", l), ["VEC"], ["DV"])
        assert dvc[0] <= 512

        dump("d_dv", DV[:], ["DV"], [128, 512], F32)
        xakeys = [("xa", m) for m in range(8)]
        ukeys = [("u", m) for m in range(8)]

        def ln_accum(m):
            rb = lnb[:, m % 2, :]
            rq = lnb[:, 2 + (m % 2), :]
            act(rb, xa[:, m, :], AF.Identity, [("xa", m)], [("lnb", m % 2)])
            act(rq, xa[:, m, :], AF.Square, [("xa", m)], [("lnb", 2 + m % 2)])
            mm(ps[4][:], onesb[:], rb, m == 0, m == 7, ["onesb", ("lnb", m % 2)], [PK(4)])
            mm(ps[5][:], onesb[:], rq, m == 0, m == 7, ["onesb", ("lnb", 2 + m % 2)], [PK(5)])

        def ln_finish(Ax, Bx, Au, Bu):
            mean_sb, msq, rstd = F(12), F(13), F(14)
            act(mean_sb, ps[4][:], AF.Identity, [PK(4)], [Fk(12)])
            tt(DVE, msq, mean_sb, mean_sb, ALU.mult, [Fk(12)], [Fk(13)])
            tt(DVE, msq, ps[5][:], msq, ALU.subtract, [PK(5), Fk(13)], [Fk(13)])
            act(rstd, msq, AF.Ln, [Fk(13)], [Fk(14)], bias=LN_EPS)
            act(rstd, rstd, AF.Exp, [Fk(14)], [Fk(14)], scale=-0.5)
            for m in range(8):
                xm = xa[:, m, :]
                tt(DVE, xm, xm, mean_sb, ALU.subtract, [("xa", m), Fk(12)], [("xa", m)])
                tt(POOL, xm, xm, rstd, ALU.mult, [("xa", m), Fk(14)], [("xa", m)])
                if Au is not None:
                    act(u[:, m, :], xm, AF.Identity, [("xa", m), "DV"], [("u", m)], scale=Au[:, m:m + 1], bias=Bu[:, m:m + 1])
                act(xm, xm, AF.Identity, [("xa", m), "DV"], [("xa", m)], scale=Ax[:, m:m + 1], bias=Bx[:, m:m + 1])

        for t in range(NT):
            t0 = t * T
            xin = FA[:, 0:8, :].rearrange("p a t -> p (a t)").rearrange("p (s f) -> p s f", f=1024)
            fk8 = [Fk(i) for i in range(8)]
            for s_ in range(NSUB):
                dma(POOL, xin[:, s_, :], x_d[t0 + s_ * 128:t0 + (s_ + 1) * 128, :], [], fk8, "xld%d" % s_)
            dma(POOL, post[:], pos_d[:, t0:t0 + T], [], ["post"], "pld")
            for j in range(8):
                pb = nextproj()
                for s_ in range(NSUB):
                    tr(ps[pb][:, s_ * 128:(s_ + 1) * 128], xin[:, s_, j * 128:(j + 1) * 128], identf, fk8 + ["CST"], [PK(pb)])
                act(xa[:, j, :], ps[pb][:], AF.Identity, [PK(pb)], [("xa", j)], scale=ALPHA)
                act(u[:, j, :], ps[pb][:], AF.Identity, [PK(pb), "DV"], [("u", j)],
                    scale=DV[:, DVO["A0u"] + j:DVO["A0u"] + j + 1], bias=DV[:, DVO["B0u"] + j:DVO["B0u"] + j + 1])

            ang, kf, ki, mk_ = F(8), F(9), F(10), F(11)
            cp(DVE, ang, post[:], ["post"], [Fk(8)])
            ts1(DVE, ang, ang, CST[:, C_THETA:C_THETA + 1], ALU.mult, [Fk(8), "CST"], [Fk(8)])
            TWO_PI = 2.0 * math.pi
            PI_HI = 6.28125
            PI_LO = TWO_PI - PI_HI
            for which in (1, 0):
                src = ang
                if which == 0:
                    ts1(DVE, kf, ang, math.pi / 2.0, ALU.add, [Fk(8)], [Fk(9)])
                    cp(DVE, ang, kf, [Fk(9)], [Fk(8)])
                ts1(DVE, kf, src, 1.0 / TWO_PI, ALU.mult, [Fk(8)], [Fk(9)])
                kiv = ki.bitcast(I32)
                cp(DVE, kiv, kf, [Fk(9)], [Fk(10)])
                cp(DVE, kf, kiv, [Fk(10)], [Fk(9)])
                r_ = F(15)
                stt(r_, kf, -PI_HI, src, ALU.mult, ALU.add, [Fk(9), Fk(8)], [Fk(15)])
                stt(r_, kf, -PI_LO, r_, ALU.mult, ALU.add, [Fk(9), Fk(15)], [Fk(15)])
                ts1(DVE, mk_, r_, math.pi, ALU.is_gt, [Fk(15)], [Fk(11)])
                stt(r_, mk_, -TWO_PI, r_, ALU.mult, ALU.add, [Fk(11), Fk(15)], [Fk(15)])
                ts1(DVE, mk_, r_, -math.pi, ALU.is_lt, [Fk(15)], [Fk(11)])
                stt(r_, mk_, TWO_PI, r_, ALU.mult, ALU.add, [Fk(11), Fk(15)], [Fk(15)])
                ts(DVE, r_, r_, 3.1415925, -3.1415925, ALU.min, ALU.max, [Fk(15)], [Fk(15)])
                act(cs[:, which, :], r_, AF.Sin, [Fk(15)], [("cs", which)])
            cosT, sinT = cs[:, 0, :], cs[:, 1, :]
            if t == 0:
                dump("d_cs", cs[:], [("cs", 0), ("cs", 1)], [128, 2, T], F32)
                dump("d_u0", u[:], ukeys, [128, 8, T], BF16)
                dump("d_xa0", xa[:], xakeys, [128, 8, T], F32)

            for l in range(NL):
                ub = l * UPL
                OGF = big[:, 0:8, :]
                RGOF = big[:, 8:24, :]
                sbuf_s = big[:, 24:32, :]
                for h in range(8):
                    par = h % 2
                    fo = 0 if par == 0 else 8
                    bo = 0 if par == 0 else 7
                    wv, wk = next_unit(ub + h)
                    pf, pq = nextproj(), nextproj()
                    for kc in range(8):
                        mm(ps[pf][:], wv[:, kc, 128:256], u[:, kc, :], kc == 0, kc == 7, [wk, ("u", kc)], [PK(pf)])
                    for kc in range(8):
                        mm(ps[pq][:], wv[:, kc, 0:128], u[:, kc, :], kc == 0, kc == 7, [wk, ("u", kc)], [PK(pq)])
                    pv0, pv1 = nextproj(), nextproj()
                    for s_ in range(NSUB):
                        pv = pv0 if s_ < 2 else pv1
                        for kc in range(8):
                            mm(ps[pv][:, (s_ % 2) * 256:(s_ % 2 + 1) * 256], u[:, kc, s_ * 128:(s_ + 1) * 128], wv[:, kc, 256:512],
                               kc == 0, kc == 7, [wk, ("u", kc)], [PK(pv)])
                    E, L1, L2, Bc, T2, EQ, AK, EG = [F(fo + i) for i in range(8)]
                    kE, kL1, kL2, kB, kT2, kEQ, kAK, kEG = [Fk(fo + i) for i in range(8)]
                    Qp, Kp, KT, V, SG, SCM, OG = [B(bo + i) for i in range(7)]
                    kQp, kKp, kKT, kV, kSG, kSCM, kOG = [Bk(bo + i) for i in range(7)]
                    lbc = DV[:, DVO["lb"] + l * 8 + h:DVO["lb"] + l * 8 + h + 1]
                    lnomc = DV[:, DVO["lnom"] + l * 8 + h:DVO["lnom"] + l * 8 + h + 1]
                    smo = par * 32
                    nbm = sm[:, smo + 0:smo + 4]
                    ebm = sm[:, smo + 4:smo + 8]
                    dl = sm[:, smo + 8:smo + 12]
                    dlm = sm[:, smo + 12:smo + 16]
                    ss = sm[:, smo + 16:smo + 20]
                    rs = sm[:, smo + 20:smo + 24]
                    smk = ("sm", par)
                    act(E, ps[pf][:], AF.Exp, [PK(pf)], [kE], scale=-1.0)
                    act(L1, E, AF.Ln, [kE], [kL1], bias=1.0)
                    act(L2, E, AF.Ln, [kE, "DV"], [kL2], scale=lbc, bias=1.0)
                    tt(POOL, L2, L2, L1, ALU.subtract, [kL2, kL1], [kL2])
                    S.op(DVE, lambda e, Bc=Bc, L2=L2: e.tensor_tensor_scan(out=Bc, data0=CST[:, C_SCANM:C_SCANM + 512], data1=L2, initial=0.0, op0=ALU.mult, op1=ALU.add),
                         ["CST", kL2], [kB])
                    B3 = Bc.rearrange("p (c t) -> p c t", t=128)
                    ts1(DVE, nbm, B3[:, :, 63], -1.0, ALU.mult, [kB], [smk])
                    tt(DVE, T2.rearrange("p (c t) -> p c t", t=128), B3, nbm.unsqueeze(2).to_broadcast([128, 4, 128]), ALU.add, [kB, smk], [kT2])
                    T23 = T2.rearrange("p (c t) -> p c t", t=128)
                    act(ebm, nbm, AF.Exp, [smk], [smk], scale=-1.0)
                    act(dl, B3[:, :, 127], AF.Exp, [kB], [smk])
                    act(dlm, T23[:, :, 127], AF.Exp, [kT2], [smk])
                    act(EQ, ps[pq][:], AF.Exp, [PK(pq)], [kEQ], scale=-1.0)
                    act(EQ, EQ, AF.Ln, [kEQ], [kEQ], bias=1.0)
                    tt(POOL, EQ, T2, EQ, ALU.subtract, [kT2, kEQ], [kEQ])
                    act(EQ, EQ, AF.Exp, [kEQ], [kEQ])
                    tt(DVE, Qp, ps[pq][:], EQ, ALU.mult, [PK(pq), kEQ], [kQp])
                    tt(DVE, AK, ps[pf][:], L1, ALU.add, [PK(pf), kL1], [kAK])
                    tt(POOL, AK, AK, T2, ALU.add, [kAK, kT2], [kAK])
                    act(Kp, AK, AF.Exp, [kAK, "DV"], [kKp], scale=-1.0, bias=lnomc)
                    V3 = V.rearrange("p (s e) -> p s e", e=128)
                    SG3 = SG.rearrange("p (s e) -> p s e", e=128)
                    EG3 = EG.rearrange("p (s e) -> p s e", e=128)
                    for half, pv in ((0, pv0), (1, pv1)):
                        pvv = ps[pv][:].rearrange("p (s c) -> p s c", c=256)
                        cp(DVE, V3[:, half * 2:half * 2 + 2, :], pvv[:, :, 0:128], [PK(pv)], [kV])
                        act(EG3[:, half * 2:half * 2 + 2, :], pvv[:, :, 128:256], AF.Exp, [PK(pv)], [kEG], scale=-1.0)
                    act(EG, EG, AF.Ln, [kEG], [kEG], bias=1.0)
                    act(EG, EG, AF.Exp, [kEG], [kEG], scale=-1.0)
                    for half, pv in ((0, pv0), (1, pv1)):
                        pvv = ps[pv][:].rearrange("p (s c) -> p s c", c=256)
                        tt(DVE, SG3[:, half * 2:half * 2 + 2, :], pvv[:, :, 128:256], EG3[:, half * 2:half * 2 + 2, :], ALU.mult, [PK(pv), kEG], [kSG])
                    p4b = ps[4][:].bitcast(BF16)
                    for c in range(4):
                        tr(p4b[:, c * 128:(c + 1) * 128], Kp[:, c * 128:(c + 1) * 128], identb[:], [kKp, "identb"], [PK(4)])
                    act(KT, p4b[:, 0:512], AF.Identity, [PK(4)], [kKT])
                    KT3 = KT.rearrange("p (c d) -> p c d", d=128)
                    for c in range(4):
                        mm(ps[5][:, c * 128:(c + 1) * 128], Kp[:, c * 128:(c + 1) * 128], Qp[:, c * 128:(c + 1) * 128], True, True, [kKp, kQp], [PK(5)])
                    S.op(POOL, lambda e, SCM=SCM: e.memset(SCM, 0.0), [], [kSCM])
                    S.op(DVE, lambda e, SCM=SCM: e.copy_predicated(out=SCM, mask=maski[:].rearrange("p c i -> p (c i)"), data=ps[5][:]), [PK(5), "maski", kSCM], [kSCM])
                    SCM3 = SCM.rearrange("p (c i) -> p c i", i=128)
                    for c in range(4):
                        mm(ps[6][:, c * 128:(c + 1) * 128], KT3[:, c, :], V3[:, c, :], True, True, [kKT, kV], [PK(6)])
                    Sst = SH[:, l * 8 + h, :]
                    kS = ("SH", l * 8 + h)
                    tmpS = sm
                    for c in range(4):
                        ts1(DVE, HSB[:, c, :], Sst, ebm[:, c:c + 1], ALU.mult, [kS, smk], [("HSB", c)])
                        if True:
                            dst = FA[:, fo + 7, c * 128:(c + 1) * 128]
                            ts1(DVE, dst, ps[6][:, c * 128:(c + 1) * 128], dlm[:, c:c + 1], ALU.mult, [PK(6), smk, kEG], [kEG])
                            stt(Sst, Sst, dl[:, c:c + 1], dst, ALU.mult, ALU.add, [kS, smk, kEG], [kS])
                    for c in range(4):
                        mm(ps[7][:, c * 128:(c + 1) * 128], SCM3[:, c, :], V3[:, c, :], True, False, [kSCM, kV], [PK(7)])
                        mm(ps[7][:, c * 128:(c + 1) * 128], Qp[:, c * 128:(c + 1) * 128], HSB[:, c, :], False, True, [kQp, ("HSB", c)], [PK(7)])
                    act(E, ps[7][:], AF.Square, [PK(7)], [kE])
                    S.op(DVE, lambda e, ss=ss, E=E: e.tensor_reduce(out=ss, in_=E.rearrange("p (c t) -> p c t", t=128), axis=AX.X, op=ALU.add), [kE], [smk])
                    act(rs, ss, AF.Ln, [smk], [smk], scale=1.0 / 128.0, bias=HN_EPS)
                    act(rs, rs, AF.Exp, [smk], [smk], scale=-0.5)
                    OG3 = OG.rearrange("p (c e) -> p c e", e=128)
                    for c in range(4):
                        stt(OG3[:, c, :], ps[7][:, c * 128:(c + 1) * 128], rs[:, c:c + 1], SG3[:, c, :], ALU.mult, ALU.mult, [PK(7), smk, kSG], [kOG])
                    for c in range(4):
                        tr(p4b[:, c * 128:(c + 1) * 128], OG3[:, c, :], identb[:], [kOG, "identb"], [PK(4)])
                    act(OGF[:, h, :], p4b[:, 0:512], AF.Identity, [PK(4)], [("big", h)])

                for h in range(4):
                    gam = GAMMAS[h]
                    cdec = gam ** 128
                    fo = 0 if h % 2 == 0 else 8
                    wvA, wkA = next_unit(ub + 8 + h * 3)
                    QR = [B(14), B(15)]
                    KR = [B(16), B(17)]
                    kQR = [Bk(14), Bk(15)]
                    kKR = [Bk(16), Bk(17)]
                    for qk in range(2):
                        p1, p2 = nextproj(), nextproj()
                        for half, pb in ((0, p1), (1, p2)):
                            cb = qk * 256 + half * 128
                            for kc in range(8):
                                mm(ps[pb][:], wvA[:, kc, cb:cb + 128], u[:, kc, :], kc == 0, kc == 7, [wkA, ("u", kc)], [PK(pb)])
                        t1, t2_, t3, t4 = F(fo + 0), F(fo + 1), F(fo + 2), F(fo + 3)
                        k1, k2, k3, k4 = Fk(fo + 0), Fk(fo + 1), Fk(fo + 2), Fk(fo + 3)
                        R = QR if qk == 0 else KR
                        kR = kQR if qk == 0 else kKR
                        tt(DVE, t1, ps[p1][:], cosT, ALU.mult, [PK(p1), ("cs", 0)], [k1])
                        tt(DVE, t2_, ps[p2][:], sinT, ALU.mult, [PK(p2), ("cs", 1)], [k2])
                        tt(POOL, R[0], t1, t2_, ALU.subtract, [k1, k2], [kR[0]])
                        tt(DVE, t3, ps[p1][:], sinT, ALU.mult, [PK(p1), ("cs", 1)], [k3])
                        tt(DVE, t4, ps[p2][:], cosT, ALU.mult, [PK(p2), ("cs", 0)], [k4])
                        tt(POOL, R[1], t3, t4, ALU.add, [k3, k4], [kR[1]])
                    wvB, wkB = next_unit(ub + 8 + h * 3 + 1)
                    Vr = BA[:, 0:4, :]
                    SRG = BA[:, 4:8, :]
                    for s_ in range(NSUB):
                        pb = nextproj()
                        for kc in range(8):
                            mm(ps[pb][:], u[:, kc, s_ * 128:(s_ + 1) * 128], wvB[:, kc, :], kc == 0, kc == 7, [wkB, ("u", kc)], [PK(pb)])
                        cp(DVE, Vr[:, s_, :], ps[pb][:], [PK(pb)], [Bk(s_)])
                    wvC, wkC = next_unit(ub + 8 + h * 3 + 2)
                    for s_ in range(NSUB):
                        pb = nextproj()
                        for kc in range(8):
                            mm(ps[pb][:], u[:, kc, s_ * 128:(s_ + 1) * 128], wvC[:, kc, :], kc == 0, kc == 7, [wkC, ("u", kc)], [PK(pb)])
                        eg = F(fo + 4 + s_)
                        keg = Fk(fo + 4 + s_)
                        act(eg, ps[pb][:], AF.Exp, [PK(pb)], [keg], scale=-1.0)
                        act(eg, eg, AF.Ln, [keg], [keg], bias=1.0)
                        act(eg, eg, AF.Exp, [keg], [keg], scale=-1.0)
                        tt(DVE, SRG[:, s_, :], ps[pb][:], eg, ALU.mult, [PK(pb), keg], [Bk(4 + s_)])
                    KTr = BA[:, 8:10, :].rearrange("p a t -> p (a t)").rearrange("p (c d) -> p c d", d=256)
                    kKTr = [Bk(8), Bk(9)]
                    p4b = ps[4][:].bitcast(BF16).rearrange("p (c d) -> p c d", d=256)
                    for c in range(4):
                        for dh in range(2):
                            tr(p4b[:, c, dh * 128:(dh + 1) * 128], KR[dh][:, c * 128:(c + 1) * 128], identb[:], [kKR[dh], "identb"], [PK(4)])
                    act(KTr, p4b, AF.Identity, [PK(4), "CST"], kKTr, scale=CST[:, C_KDEC + h:C_KDEC + h + 1])
                    for c in range(4):
                        for dh in range(2):
                            mm(ps[5][:, c * 128:(c + 1) * 128], KR[dh][:, c * 128:(c + 1) * 128], QR[dh][:, c * 128:(c + 1) * 128], dh == 0, dh == 1,
                               [kKR[dh], kQR[dh]], [PK(5)])
                    SCMr = B(10)
                    tt(DVE, SCMr.rearrange("p (c i) -> p c i", i=128), ps[5][:].rearrange("p (c i) -> p c i", i=128),
                       CST[:, C_MH + h * 128:C_MH + (h + 1) * 128].unsqueeze(1).to_broadcast([128, 4, 128]), ALU.mult, [PK(5), "CST"], [Bk(10)])
                    SCMr3 = SCMr.rearrange("p (c i) -> p c i", i=128)
                    sidx = (l * 4 + h) * 2
                    kSR = [("SR", sidx), ("SR", sidx + 1)]
                    stat6 = sm[:, 96:120].rearrange("p (c s) -> p c s", s=6)
                    mv = sm[:, 64:72].rearrange("p (c s) -> p c s", s=2)
                    smr = ("sm", 2)
                    for c in range(4):
                        sbp = c % 2
                        for dh in range(2):
                            act(RSB[:, sbp, dh, :], SR[:, sidx + dh, :], AF.Identity, [kSR[dh]], [("RSB", sbp, dh)])
                        po = nextproj()
                        mm(ps[po][:], SCMr3[:, c, :], Vr[:, c, :], True, False, [Bk(10), Bk(c)], [PK(po)])
                        for dh in range(2):
                            mm(ps[po][:], QR[dh][:, c * 128:(c + 1) * 128], RSB[:, sbp, dh, :], False, dh == 1, [kQR[dh], ("RSB", sbp, dh)], [PK(po)])
                        for dh in range(2):
                            mm(ps[6 + dh][:], KTr[:, c, dh * 128:(dh + 1) * 128], Vr[:, c, :], True, True, kKTr + [Bk(c)], [PK(6 + dh)])
                            stt(SR[:, sidx + dh, :], SR[:, sidx + dh, :], cdec, ps[6 + dh][:], ALU.mult, ALU.add, [kSR[dh], PK(6 + dh)], [kSR[dh]])
                        S.op(DVE, lambda e, c=c, po=po: e.bn_stats(out=stat6[:, c, :], in_=ps[po][:]), [PK(po)], [smr])
                        S.op(DVE, lambda e, c=c: e.bn_aggr(out=mv[:, c, :], in_=stat6[:, c, :]), [smr], [smr])
                        rsp = sm[:, 72 + c:72 + c + 1]
                        nb = sm[:, 80 + c:80 + c + 1]
                        ts(DVE, rsp, mv[:, c, 1:2], CST[:, C_QDEC2 + h:C_QDEC2 + h + 1], HN_EPS, ALU.mult, ALU.add, [smr, "CST"], [smr])
                        act(rsp, rsp, AF.Ln, [smr], [smr])
                        act(rsp, rsp, AF.Exp, [smr], [smr], scale=-0.5)
                        tt(DVE, rsp, rsp, CST[:, C_QDEC + h:C_QDEC + h + 1], ALU.mult, [smr, "CST"], [smr])
                        stt(nb, mv[:, c, 0:1], -1.0, rsp, ALU.mult, ALU.mult, [smr], [smr])
                        ON = B(11 + (c % 2))
                        kON = Bk(11 + (c % 2))
                        act(ON, ps[po][:], AF.Identity, [PK(po), smr], [kON], scale=rsp, bias=nb)
                        OGr = B(18 + (c % 2))
                        kOGr = Bk(18 + (c % 2))
                        tt(POOL, OGr, ON, SRG[:, c, :], ALU.mult, [kON, Bk(4 + c)], [kOGr])
                        pt = nextproj()
                        ptb = ps[pt][:].bitcast(BF16)
                        for ec in range(4):
                            tr(ptb[:, ec * 128:(ec + 1) * 128], OGr[:, ec * 128:(ec + 1) * 128], identb[:], [kOGr, "identb"], [PK(pt)])
                        act(RGOF[:, h * 4:(h + 1) * 4, c * 128:(c + 1) * 128], ptb[:, 0:512].rearrange("p (a i) -> p a i", i=128), AF.Identity,
                            [PK(pt)], [("big", 8 + h * 4 + ec) for ec in range(4)])

                gate_sb = {}
                for gi in range(4):
                    wv, wk = next_unit(ub + 20 + gi)
                    for blk in range(4):
                        m = (gi % 2) * 4 + blk
                        pb = nextproj()
                        for kc in range(8):
                            mm(ps[pb][:], wv[:, kc, blk * 128:(blk + 1) * 128], u[:, kc, :], kc == 0, kc == 7, [wk, ("u", kc)], [PK(pb)])
                        si = (gi // 2) * 8 + m
                        g_ = F(si)
                        act(g_, ps[pb][:], AF.Exp, [PK(pb)], [Fk(si)], scale=-1.0)
                        act(g_, g_, AF.Ln, [Fk(si)], [Fk(si)], bias=1.0)
                        act(g_, g_, AF.Exp, [Fk(si)], [Fk(si)], scale=-1.0)
                        gate_sb[(gi // 2, m)] = si
                for ch in range(2):
                    wv, wk = next_unit(ub + 24 + ch)
                    for blk in range(4):
                        m = ch * 4 + blk
                        pb = nextproj()
                        for kc in range(8):
                            mm(ps[pb][:], wv[:, kc, blk * 128:(blk + 1) * 128], OGF[:, kc, :], kc == 0, kc == 7, [wk, ("big", kc)], [PK(pb)])
                        si = gate_sb[(0, m)]
                        tt(DVE, F(si), ps[pb][:], F(si), ALU.mult, [PK(pb), Fk(si)], [Fk(si)])
                for ch in range(2):
                    units = [next_unit(ub + 26 + ch * 2 + kh) for kh in range(2)]
                    pbs = [nextproj() for _ in range(4)]
                    for kh in range(2):
                        wv, wk = units[kh]
                        for blk in range(4):
                            for kc in range(8):
                                kk = kh * 8 + kc
                                mm(ps[pbs[blk]][:], wv[:, kc, blk * 128:(blk + 1) * 128], RGOF[:, kk, :], kk == 0, kk == 15, [wk, ("big", 8 + kk)], [PK(pbs[blk])])
                    for blk in range(4):
                        m = ch * 4 + blk
                        sa, sb_ = gate_sb[(0, m)], gate_sb[(1, m)]
                        tt(DVE, F(sb_), ps[pbs[blk]][:], F(sb_), ALU.mult, [PK(pbs[blk]), Fk(sb_)], [Fk(sb_)])
                        tt(POOL, sbuf_s[:, m, :], F(sa), F(sb_), ALU.add, [Fk(sa), Fk(sb_)], [("big", 24 + m)])
                if t == 0:
                    dump("d_big%d" % l, big[:], bigkeys, [128, 32, T], BF16)
                G1h = DVs(("G1h", l))
                for ch in range(2):
                    wv, wk = next_unit(ub + 30 + ch)
                    for blk in range(4):
                        m = ch * 4 + blk
                        pb = nextproj()
                        for kc in range(8):
                            mm(ps[pb][:], wv[:, kc, blk * 128:(blk + 1) * 128], sbuf_s[:, kc, :], kc == 0, kc == 7, [wk, ("big", 24 + kc)], [PK(pb)])
                        stt(xa[:, m, :], ps[pb][:], G1h[:, m:m + 1], xa[:, m, :], ALU.mult, ALU.add, [PK(pb), "DV", ("xa", m)], [("xa", m)])
                        ln_accum(m)
                ln_finish(DVs(("Ax1", l)), DVs(("Bx1", l)), DVs(("Au1", l)), DVs(("Bu1", l)))

                if t == 0:
                    dump("d_xa1_%d" % l, xa[:], xakeys, [128, 8, T], F32)
                    dump("d_u1_%d" % l, u[:], ukeys, [128, 8, T], BF16)
                bup = Vs("b_up", l, 32)
                for un in range(8):
                    wv, wk = next_unit(ub + 32 + un)
                    for blk in range(4):
                        j = un * 4 + blk
                        pb = nextproj()
                        for kc in range(8):
                            mm(ps[pb][:], wv[:, kc, blk * 128:(blk + 1) * 128], u[:, kc, :], kc == 0, kc == 7, [wk, ("u", kc)], [PK(pb)])
                        hr = F(j % 4)
                        act(hr, ps[pb][:], AF.Relu, [PK(pb), "VEC"], [Fk(j % 4)], bias=bup[:, j:j + 1])
                        tt(DVE if j % 2 == 0 else POOL, big[:, j, :], hr, hr, ALU.mult, [Fk(j % 4)], [("big", j)])
                G2 = DVs(("G2", l))
                for ch in range(2):
                    units = [next_unit(ub + 40 + ch * 4 + kq) for kq in range(4)]
                    pbs = [0, 1, 2, 3] if ch == 0 else [6, 7, 0, 1]
                    for kq in range(4):
                        wv, wk = units[kq]
                        for blk in range(4):
                            for kc in range(8):
                                kk = kq * 8 + kc
                                mm(ps[pbs[blk]][:], wv[:, kc, blk * 128:(blk + 1) * 128], big[:, kk, :], kk == 0, kk == 31, [wk, ("big", kk)], [PK(pbs[blk])])
                    for blk in range(4):
                        m = ch * 4 + blk
                        stt(xa[:, m, :], ps[pbs[blk]][:], G2[:, m:m + 1], xa[:, m, :], ALU.mult, ALU.add, [PK(pbs[blk]), "DV", ("xa", m)], [("xa", m)])
                        ln_accum(m)
                last = (l == NL - 1)
                ln_finish(DVs(("Ax2", l)), DVs(("Bx2", l)), None if last else DVs(("Au2", l)), None if last else DVs(("Bu2", l)))

            for s_ in range(NSUB):
                st_ = FA[:, (s_ % 2) * 2:(s_ % 2) * 2 + 2, :].rearrange("p a t -> p (a t)")
                stk = [Fk((s_ % 2) * 2), Fk((s_ % 2) * 2 + 1)]
                for jh in range(2):
                    pb = nextproj()
                    for jj in range(4):
                        j = jh * 4 + jj
                        tr(ps[pb][:, jj * 128:(jj + 1) * 128], xa[:, j, s_ * 128:(s_ + 1) * 128], identf, [("xa", j), "CST"], [PK(pb)])
                    act(st_[:, jh * 512:(jh + 1) * 512], ps[pb][:], AF.Identity, [PK(pb)], [stk[jh]])
                dma(POOL, out_d[t0 + s_ * 128:t0 + (s_ + 1) * 128, :], st_, stk, [], "ost%d" % (s_ % 2))

        fin = S.op(POOL, lambda e: e.nop(), [], [])
        for ch in list(S.dma_last):
            if S.dma_last[ch] is not fin:
                fin.deps.append(S.dma_last[ch])

        S.finalize()
        engsem = {e: es.enter_context(nc.semaphore("sem_" + e)) for e in ENGS}
        dmasem = {ch: es.enter_context(nc.semaphore("dsem_" + ch)) for ch in S.dma_count}
        with nc.Block() as block:
            @block.sync
            def _(e):
                S.emit(e, SP, engsem, dmasem)

            @block.tensor
            def _(e):
                S.emit(e, PE, engsem, dmasem)

            @block.scalar
            def _(e):
                S.emit(e, ACT, engsem, dmasem)

            @block.vector
            def _(e):
                S.emit(e, DVE, engsem, dmasem)

            @block.gpsimd
            def _(e):
                S.emit(e, POOL, engsem, dmasem)
    return nc


def _unit(Wm, rows, cols):
    sub = Wm[rows][:, cols]
    return np.ascontiguousarray(sub.reshape(8, 128, 512).transpose(1, 0, 2)).reshape(128, 4096)


def _pack_weights(w_in, w_pa, w_pb, w_o, w_up, w_down):
    units = []
    r1024 = np.arange(1024)
    for l in range(DEPTH):
        Wi = w_in[l]
        for h in range(8):
            cols = np.concatenate([0 + h * 128 + np.arange(128), 1024 + h * 128 + np.arange(128),
                                   2048 + h * 128 + np.arange(128), 3072 + h * 128 + np.arange(128)])
            units.append(_unit(Wi, r1024, cols))
        for h in range(4):
            qb = 4096 + h * 256
            kb = 5120 + h * 256
            ev = np.arange(0, 256, 2)
            od = np.arange(1, 256, 2)
            units.append(_unit(Wi, r1024, np.concatenate([qb + ev, qb + od, kb + ev, kb + od])))
            units.append(_unit(Wi, r1024, 6144 + h * 512 + np.arange(512)))
            units.append(_unit(Wi, r1024, 8192 + h * 512 + np.arange(512)))
        for gi in range(4):
            units.append(_unit(Wi, r1024, 10240 + gi * 512 + np.arange(512)))
        for ch in range(2):
            units.append(_unit(w_pa[l], r1024, ch * 512 + np.arange(512)))
        for ch in range(2):
            for kh in range(2):
                units.append(_unit(w_pb[l], kh * 1024 + r1024, ch * 512 + np.arange(512)))
        for ch in range(2):
            units.append(_unit(w_o[l], r1024, ch * 512 + np.arange(512)))
        for un in range(8):
            units.append(_unit(w_up[l], r1024, un * 512 + np.arange(512)))
        for ch in range(2):
            for kq in range(4):
                units.append(_unit(w_down[l], kq * 1024 + r1024, ch * 512 + np.arange(512)))
    return np.stack(units, axis=0)


def _fm(v, n):
    return np.ascontiguousarray(np.asarray(v, np.float32).reshape(n, 128).T)


def _consts():
    C = np.zeros((128, NCONST), np.float32)
    C[:, C_IDENT:C_IDENT + 128] = np.eye(128, dtype=np.float32)
    j = np.arange(128)[:, None]
    i = np.arange(128)[None, :]
    C[:, C_MASKT:C_MASKT + 128] = (j <= i).astype(np.float32)
    for h in range(4):
        g = np.float64(GAMMAS[h])
        C[:, C_MH + h * 128:C_MH + (h + 1) * 128] = ((j <= i) * (g ** (-(j + 1.0))) / 16.0).astype(np.float32)
        C[:, C_KDEC + h] = (g ** (127.0 - np.arange(128)) / 16.0).astype(np.float32)
        C[:, C_QDEC + h] = (g ** (np.arange(128) + 1.0)).astype(np.float32)
        C[:, C_QDEC2 + h] = (g ** (2.0 * (np.arange(128) + 1.0))).astype(np.float32)
    sc = np.ones(512, np.float32)
    sc[::128] = 0.0
    C[:, C_SCANM:C_SCANM + 512] = sc[None, :]
    C[:, C_THETA] = (10000.0 ** (-np.linspace(0.0, 1.0, 128, dtype=np.float32))).astype(np.float32)
    return C


_NC_CACHE = {}


def kernel(x, c, positions, lb_logits, w_ada, b_ada, w_in, w_pa, w_pb, w_o,
           ln1_g, ln1_b, w_up, b_up, w_down, b_down, ln2_g, ln2_b, _NT=8, _cores=8, _dbg=False):
    x = np.asarray(x, np.float32)
    wall = _pack_weights(*[np.asarray(a, np.float32) for a in (w_in, w_pa, w_pb, w_o, w_up, w_down)])
    w_ada = np.asarray(w_ada, np.float32)
    wada = np.stack([np.ascontiguousarray(w_ada[l][:, g * 1024:(g + 1) * 1024].reshape(8, 128, 1024).transpose(1, 0, 2)).reshape(128, 8192)
                     for l in range(DEPTH) for g in range(6)], axis=0)
    consts = _consts()
    in_maps = []
    for b in range(_cores):
        V = np.zeros((128, NVEC), np.float32)
        for l in range(DEPTH):
            for nm, arr, n in (("ln1_g", ln1_g, 8), ("ln1_b", ln1_b, 8), ("ln2_g", ln2_g, 8), ("ln2_b", ln2_b, 8),
                               ("b_down", b_down, 8), ("b_up", b_up, 32), ("b_ada", b_ada, 48), ("lbl", lb_logits, 8)):
                V[:, VOFF[(nm, l)]:VOFF[(nm, l)] + n] = _fm(np.asarray(arr)[l], n)
        V[:, VOFF["c"]:VOFF["c"] + 8] = _fm(np.asarray(c)[b], 8)
        pos = np.ascontiguousarray(np.broadcast_to(np.asarray(positions)[b].astype(np.int32)[None, :], (128, SEQ)))
        in_maps.append({"x": np.ascontiguousarray(x[b]), "pos": pos, "wall": wall, "wada": wada, "vecs": V, "consts": consts})
    key = (_NT, _dbg)
    if key not in _NC_CACHE:
        _NC_CACHE[key] = build_nc(NT=_NT, dbg=_dbg)
    nc = _NC_CACHE[key]
    res = run_bass_kernel_spmd(nc, in_maps, core_ids=list(range(_cores)))
    out = np.stack([np.asarray(r["out"], np.float32) for r in res.results], axis=0)
    if _dbg:
        return out, res.results[0]
    if _cores < 8:
        return out
    return out.reshape(8, SEQ, D)
```

```python
import math
from contextlib import ExitStack
import numpy as np
import concourse.bass as bass
import concourse.mybir as mybir
from concourse.bass_utils import run_bass_kernel_spmd

F32 = mybir.dt.float32
BF16 = mybir.dt.bfloat16
I32 = mybir.dt.int32
AF = mybir.ActivationFunctionType
ALU = mybir.AluOpType
AX = mybir.AxisListType

PE, ACT, DVE, POOL, SP = "tensor", "scalar", "vector", "gpsimd", "sync"
ENGS = (PE, ACT, DVE, POOL, SP)

D = 1024
SEQ = 4096
T = 512
NSUB = 4
DEPTH = 2
ALPHA = (2 * DEPTH) ** 0.25
LN_EPS = 1e-5
HN_EPS = 1e-6
UPL = 48
GAMMAS = [1.0 - 2.0 ** (-5.0 - h) for h in range(4)]


class Op:
    __slots__ = ("eng", "fn", "deps", "need_sig", "sigval", "dma", "dma_val", "waits", "wkeys", "idx")

    def __init__(self, eng, fn):
        self.eng = eng
        self.fn = fn
        self.deps = []
        self.need_sig = False
        self.sigval = 0
        self.dma = None
        self.dma_val = 0
        self.waits = {}
        self.wkeys = frozenset()


class Sched:
    def __init__(self):
        self.streams = {e: [] for e in ENGS}
        self.lastw = {}
        self.readers = {}
        self.dma_count = {}
        self.dma_last = {}

    def op(self, eng, fn, reads=(), writes=(), dma=None):
        o = Op(eng, fn)
        reads = tuple(reads)
        writes = tuple(writes)
        deps = set()
        for k in reads:
            w = self.lastw.get(k)
            if w is not None:
                deps.add(w)
        for k in writes:
            w = self.lastw.get(k)
            if w is not None:
                deps.add(w)
            for r in self.readers.get(k, ()):
                deps.add(r)
        if dma is not None:
            prev = self.dma_last.get(dma)
            if prev is not None:
                deps.add(prev)
            self.dma_last[dma] = o
            cnt = self.dma_count.get(dma, 0) + 1
            self.dma_count[dma] = cnt
            o.dma = dma
            o.dma_val = 16 * cnt
        touched = set(reads) | set(writes)
        best = {}
        for d in deps:
            if d is o:
                continue
            if d.dma is None and d.eng == eng:
                if eng == PE:
                    continue
                if not (d.wkeys & touched):
                    continue
            key = ("dma", d.dma) if d.dma is not None else ("eng", d.eng)
            cur = best.get(key)
            if cur is None or d.idx > cur.idx:
                best[key] = d
        for d in best.values():
            if d.dma is None:
                d.need_sig = True
            o.deps.append(d)
        o.idx = len(self.streams[eng])
        for k in reads:
            self.readers.setdefault(k, []).append(o)
        for k in writes:
            self.lastw[k] = o
            self.readers[k] = []
        o.wkeys = frozenset(writes)
        self.streams[eng].append(o)
        return o

    def finalize(self):
        for e, st in self.streams.items():
            c = 0
            for o in st:
                if o.dma is None and o.need_sig:
                    c += 1
                    o.sigval = c
        for e, st in self.streams.items():
            known = {}
            for o in st:
                w = {}
                for d in o.deps:
                    if d.dma is not None:
                        key = ("dma", d.dma)
                        val = d.dma_val
                    else:
                        key = ("eng", d.eng)
                        val = d.sigval
                    if val > known.get(key, 0) and val > w.get(key, 0):
                        w[key] = val
                for k, v in w.items():
                    known[k] = v
                o.waits = w

    def emit(self, engobj, eng, engsem, dmasem):
        for o in self.streams[eng]:
            for (kind, name), v in o.waits.items():
                s = engsem[name] if kind == "eng" else dmasem[name]
                engobj.wait_ge(s, v)
            ins = o.fn(engobj)
            if o.dma is not None:
                ins.then_inc(dmasem[o.dma], 16)
            elif o.need_sig:
                ins.then_inc(engsem[o.eng], 1)


def _vec_layout():
    off = {}
    c = 0
    for l in range(DEPTH):
        for nm, n in (("ln1_g", 8), ("ln1_b", 8), ("ln2_g", 8), ("ln2_b", 8), ("b_down", 8), ("b_up", 32), ("b_ada", 48), ("lbl", 8)):
            off[(nm, l)] = c
            c += n
    off["c"] = c
    c += 8
    return off, c


VOFF, NVEC = _vec_layout()
C_IDENT, C_MASKT, C_MH, C_SCANM, C_KSC, C_KDEC, C_QD, C_QD2, C_THETA = 0, 128, 256, 768, 1280, 1296, 1312, 1328, 1344
NCONST = 1346


def build_nc(NT=8, NL=DEPTH, dbg=False):
    nc = bass.Bass("TRN2", target_bir_lowering=False)
    x_d = nc.dram_tensor("x", [SEQ, D], F32, kind="ExternalInput").ap()
    pos_d = nc.dram_tensor("pos", [128, SEQ], I32, kind="ExternalInput").ap()
    wall_d = nc.dram_tensor("wall", [DEPTH * UPL, 128, 4096], F32, kind="ExternalInput").ap()
    wada_d = nc.dram_tensor("wada", [DEPTH * 6, 128, 8192], F32, kind="ExternalInput").ap()
    vec_d = nc.dram_tensor("vecs", [128, NVEC], F32, kind="ExternalInput").ap()
    cst_d = nc.dram_tensor("consts", [128, NCONST], F32, kind="ExternalInput").ap()
    out_d = nc.dram_tensor("out", [SEQ, D], F32, kind="ExternalOutput").ap()
    wscr = nc.dram_tensor("wscr", [DEPTH * UPL, 128, 4096], BF16, kind="Internal").ap()

    S = Sched()
    es = ExitStack()
    with es:
        def sb(name, shape, dt):
            return es.enter_context(nc.sbuf_tensor(name, shape, dt))

        xa = sb("xa", [128, 8, T], F32)
        u = sb("u", [128, 8, T], BF16)
        big = sb("big", [128, 32, T], BF16)
        ring = sb("ring", [128, 4, 4096], BF16)
        FA = sb("FA", [128, 16, T], F32)
        BA = sb("BA", [128, 20, T], BF16)
        lnb = sb("lnb", [128, 4, T], BF16)
        SH = sb("SH", [128, DEPTH * 8, 128], F32)
        SR = sb("SR", [128, DEPTH * 4 * 2, 512], F32)
        HSB = sb("HSB", [128, 4, 128], BF16)
        RSB = sb("RSB", [128, 2, 2, 512], BF16)
        cs = sb("cs", [128, 2, T], F32)
        post = sb("post", [128, T], I32)
        VEC = sb("VEC", [128, NVEC], F32)
        CST = sb("CST", [128, NCONST], F32)
        DV = sb("DV", [128, 512], F32)
        identb = sb("identb", [128, 128], BF16)
        onesb = sb("onesb", [128, 128], BF16)
        maski = sb("maski", [128, 4, 128], I32)
        sm = sb("sm", [128, 128], F32)

        ps = [es.enter_context(nc.psum_tensor("ps%d" % i, [128, 512], F32)) for i in range(8)]

        def F(i):
            return FA[:, i, :]

        def Fk(i):
            return ("F", i)

        def B(i):
            return BA[:, i, :]

        def Bk(i):
            return ("B", i)

        pj = [0]

        def nextproj():
            i = pj[0] % 4
            pj[0] += 1
            return i

        def PK(i):
            return ("ps", i)

        def mm(out, lhsT, rhs, start, stop, reads, writes):
            S.op(PE, lambda e: e.matmul(out, lhsT=lhsT, rhs=rhs, start=start, stop=stop), reads, writes)

        def tr(out, in_, ident, reads, writes):
            S.op(PE, lambda e: e.transpose(out=out, in_=in_, identity=ident), reads, writes)

        def act(out, in_, func, reads, writes, scale=1.0, bias=0.0):
            S.op(ACT, lambda e: e.activation(out=out, in_=in_, func=func, bias=bias, scale=scale), reads, writes)

        def tt(eng, out, in0, in1, op, reads, writes):
            S.op(eng, lambda e: e.tensor_tensor(out=out, in0=in0, in1=in1, op=op), reads, writes)

        def ts(eng, out, in0, s1, s2, op0, op1, reads, writes):
            S.op(eng, lambda e: e.tensor_scalar(out=out, in0=in0, scalar1=s1, scalar2=s2, op0=op0, op1=op1), reads, writes)

        def ts1(eng, out, in0, s1, op0, reads, writes):
            S.op(eng, lambda e: e.tensor_scalar(out=out, in0=in0, scalar1=s1, scalar2=None, op0=op0), reads, writes)

        def stt(out, in0, scalar, in1, op0, op1, reads, writes):
            S.op(DVE, lambda e: e.scalar_tensor_tensor(out=out, in0=in0, scalar=scalar, in1=in1, op0=op0, op1=op1), reads, writes)

        def cp(eng, out, in_, reads, writes):
            S.op(eng, lambda e: e.tensor_copy(out=out, in_=in_), reads, writes)

        def dma(eng, out, in_, reads, writes, ch, **kw):
            S.op(eng, lambda e: e.dma_start(out=out, in_=in_, **kw), reads, writes, dma=ch)

        def dump(name, ap, keys, shape, dt):
            if not dbg:
                return
            d = nc.dram_tensor(name, shape, dt, kind="ExternalOutput").ap()
            dma(POOL, d, ap, keys, [], "dbg")

        ucount = [0]

        def next_unit(uidx):
            slot = ucount[0] % 4
            ucount[0] += 1
            dma(SP, ring[:, slot, :], wscr[uidx], [("wscr", uidx)], [("ring", slot)], "ring%d" % slot)
            return ring[:, slot, :].rearrange("p (k n) -> p k n", n=512), ("ring", slot)

        dma(SP, VEC[:], vec_d, [], ["VEC"], "ldv")
        dma(SP, CST[:], cst_d, [], ["CST"], "ldc")
        for uidx in range(NL * UPL):
            dma(POOL, wscr[uidx], wall_d[uidx], [], [("wscr", uidx)], "wcv%d" % (uidx % 4), max_dma_last_dim=4096)

        cp(DVE, identb[:], CST[:, C_IDENT:C_IDENT + 128], ["CST"], ["identb"])
        S.op(DVE, lambda e: e.memset(onesb[:], 1.0 / 1024.0), [], ["onesb"])
        for c in range(4):
            cp(DVE, maski[:, c, :], CST[:, C_MASKT:C_MASKT + 128], ["CST"], ["maski"])
        S.op(DVE, lambda e: e.memset(SH[:], 0.0), [], [("SH", i) for i in range(DEPTH * 8)])
        S.op(DVE, lambda e: e.memset(SR[:], 0.0), [], [("SR", i) for i in range(DEPTH * 8)])
        identf = CST[:, C_IDENT:C_IDENT + 128]

        DVO = {}
        dvc = [0]

        def dv(name, n=8):
            DVO[name] = dvc[0]
            dvc[0] += n
            return DVO[name]

        def DVs(name, n=8):
            o = DVO[name]
            return DV[:, o:o + n]

        def Vs(name, l, n=8):
            o = VOFF[(name, l)]
            return VEC[:, o:o + n]

        dv("cond")
        dv("tmp")
        cfm = VEC[:, VOFF["c"]:VOFF["c"] + 8]
        act(DVs("tmp"), cfm, AF.Exp, ["VEC"], ["DV"], scale=-1.0)
        act(DVs("tmp"), DVs("tmp"), AF.Ln, ["DV"], ["DV"], bias=1.0)
        act(DVs("tmp"), DVs("tmp"), AF.Exp, ["DV"], ["DV"], scale=-1.0)
        tt(DVE, DVs("cond"), cfm, DVs("tmp"), ALU.mult, ["VEC", "DV"], ["DV"])
        bigf = big[:].rearrange("p a t -> p (a t)").bitcast(F32)
        bigkeys = [("big", i) for i in range(32)]
        for l in range(NL):
            dv(("mod", l), 48)
            for g in range(6):
                dma(SP, bigf, wada_d[l * 6 + g], [], bigkeys, "ada")
                wv = bigf.rearrange("p (k n) -> p k n", n=1024)
                for jb in range(8):
                    jj = g * 8 + jb
                    for kc in range(8):
                        mm(ps[0][:, jj:jj + 1], wv[:, kc, jb * 128:(jb + 1) * 128], DV[:, DVO["cond"] + kc:DVO["cond"] + kc + 1],
                           kc == 0, kc == 7, bigkeys + ["DV"], [PK(0)])
            tt(DVE, DVs(("mod", l), 48), ps[0][:, 0:48], Vs("b_ada", l, 48), ALU.add, [PK(0), "VEC"], ["DV"])

        def MOD(l, which):
            o = DVO[("mod", l)] + which * 8
            return DV[:, o:o + 8]

        dv("lb", 16)
        dv("lnom", 16)
        S.op(DVE, lambda e: e.memset(DVs("lb", 16), 0.0), [], ["DV"])
        if NL > 1:
            lb1 = DV[:, DVO["lb"] + 8:DVO["lb"] + 16]
            tt(DVE, DVs("tmp"), Vs("lbl", 1), Vs("lbl", 0), ALU.subtract, ["VEC"], ["DV"])
            act(DVs("tmp"), DVs("tmp"), AF.Exp, ["DV"], ["DV"], scale=-1.0)
            act(DVs("tmp"), DVs("tmp"), AF.Ln, ["DV"], ["DV"], bias=1.0)
            act(lb1, DVs("tmp"), AF.Exp, ["DV"], ["DV"], scale=-1.0)
        act(DVs("lnom", 16), DVs("lb", 16), AF.Ln, ["DV"], ["DV"], scale=-1.0, bias=1.0)

        dv("one_s", 8)
        for l in range(NL):
            for nm in ("G1h", "G2", "Ax1", "Bx1", "Au1", "Bu1", "Ax2", "Bx2", "Au2", "Bu2"):
                dv((nm, l))
        dv("A0u")
        dv("B0u")
        ts1(DVE, DVs("A0u"), MOD(0, 1), 1.0, ALU.add, ["DV"], ["DV"])
        cp(DVE, DVs("B0u"), MOD(0, 0), ["DV"], ["DV"])
        for l in range(NL):
            ts1(DVE, DVs(("G1h", l)), MOD(l, 2), 1.0, ALU.add, ["DV"], ["DV"])
            ts1(DVE, DVs(("G2", l)), MOD(l, 5), 1.0, ALU.add, ["DV"], ["DV"])
            ts1(DVE, DVs(("Ax1", l)), Vs("ln1_g", l), ALPHA, ALU.mult, ["VEC"], ["DV"])
            tt(DVE, DVs("tmp"), DVs(("G2", l)), Vs("b_down", l), ALU.mult, ["DV", "VEC"], ["DV"])
            stt(DVs(("Bx1", l)), Vs("ln1_b", l), ALPHA, DVs("tmp"), ALU.mult, ALU.add, ["VEC", "DV"], ["DV"])
            ts1(DVE, DVs("one_s"), MOD(l, 4), 1.0, ALU.add, ["DV"], ["DV"])
            tt(DVE, DVs(("Au1", l)), Vs("ln1_g", l), DVs("one_s"), ALU.mult, ["VEC", "DV"], ["DV"])
            tt(DVE, DVs("tmp"), Vs("ln1_b", l), DVs("one_s"), ALU.mult, ["VEC", "DV"], ["DV"])
            tt(DVE, DVs(("Bu1", l)), DVs("tmp"), MOD(l, 3), ALU.add, ["DV"], ["DV"])
            if l < NL - 1:
                ts1(DVE, DVs(("Ax2", l)), Vs("ln2_g", l), ALPHA, ALU.mult, ["VEC"], ["DV"])
                ts1(DVE, DVs(("Bx2", l)), Vs("ln2_b", l), ALPHA, ALU.mult, ["VEC"], ["DV"])
                ts1(DVE, DVs("one_s"), MOD(l + 1, 1), 1.0, ALU.add, ["DV"], ["DV"])
                tt(DVE, DVs(("Au2", l)), Vs("ln2_g", l), DVs("one_s"), ALU.mult, ["VEC", "DV"], ["DV"])
                tt(DVE, DVs("tmp"), Vs("ln2_b", l), DVs("one_s"), ALU.mult, ["VEC", "DV"], ["DV"])
                tt(DVE, DVs(("Bu2", l)), DVs("tmp"), MOD(l + 1, 0), ALU.add, ["DV"], ["DV"])
            else:
                cp(DVE, DVs(("Ax2", l)), Vs("ln2_g", l), ["VEC"], ["DV"])
                cp(DVE, DVs(("Bx2", l)), Vs("ln2_b", l), ["VEC"], ["DV"])
        assert dvc[0] <= 512

        dump("d_dv", DV[:], ["DV"], [128, 512], F32)
        xakeys = [("xa", m) for m in range(8)]
        ukeys = [("u", m) for m in range(8)]

        def ln_accum(m):
            rb = lnb[:, m % 2, :]
            rq = lnb[:, 2 + (m % 2), :]
            act(rb, xa[:, m, :], AF.Identity, [("xa", m)], [("lnb", m % 2)])
            act(rq, xa[:, m, :], AF.Square, [("xa", m)], [("lnb", 2 + m % 2)])
            mm(ps[4][:], onesb[:], rb, m == 0, m == 7, ["onesb", ("lnb", m % 2)], [PK(4)])
            mm(ps[5][:], onesb[:], rq, m == 0, m == 7, ["onesb", ("lnb", 2 + m % 2)], [PK(5)])

        def ln_finish(Ax, Bx, Au, Bu):
            mean_sb, msq, rstd = F(12), F(13), F(14)
            act(mean_sb, ps[4][:], AF.Identity, [PK(4)], [Fk(12)])
            tt(DVE, msq, mean_sb, mean_sb, ALU.mult, [Fk(12)], [Fk(13)])
            tt(DVE, msq, ps[5][:], msq, ALU.subtract, [PK(5), Fk(13)], [Fk(13)])
            act(rstd, msq, AF.Ln, [Fk(13)], [Fk(14)], bias=LN_EPS)
            act(rstd, rstd, AF.Exp, [Fk(14)], [Fk(14)], scale=-0.5)
            for m in range(8):
                xm = xa[:, m, :]
                tt(DVE, xm, xm, mean_sb, ALU.subtract, [("xa", m), Fk(12)], [("xa", m)])
                tt(POOL, xm, xm, rstd, ALU.mult, [("xa", m), Fk(14)], [("xa", m)])
                if Au is not None:
                    act(u[:, m, :], xm, AF.Identity, [("xa", m), "DV"], [("u", m)], scale=Au[:, m:m + 1], bias=Bu[:, m:m + 1])
                act(xm, xm, AF.Identity, [("xa", m), "DV"], [("xa", m)], scale=Ax[:, m:m + 1], bias=Bx[:, m:m + 1])

        for t in range(NT):
            t0 = t * T
            xin = FA[:, 0:8, :].rearrange("p a t -> p (a t)").rearrange("p (s f) -> p s f", f=1024)
            fk8 = [Fk(i) for i in range(8)]
            for s_ in range(NSUB):
                dma(POOL, xin[:, s_, :], x_d[t0 + s_ * 128:t0 + (s_ + 1) * 128, :], [], fk8, "xld%d" % s_)
            dma(POOL, post[:], pos_d[:, t0:t0 + T], [], ["post"], "pld")
            for j in range(8):
                pb = nextproj()
                for s_ in range(NSUB):
                    tr(ps[pb][:, s_ * 128:(s_ + 1) * 128], xin[:, s_, j * 128:(j + 1) * 128], identf, fk8 + ["CST"], [PK(pb)])
                act(xa[:, j, :], ps[pb][:], AF.Identity, [PK(pb)], [("xa", j)], scale=ALPHA)
                act(u[:, j, :], ps[pb][:], AF.Identity, [PK(pb), "DV"], [("u", j)],
                    scale=DV[:, DVO["A0u"] + j:DVO["A0u"] + j + 1], bias=DV[:, DVO["B0u"] + j:DVO["B0u"] + j + 1])

            ang, kf, ki, mk_ = F(8), F(9), F(10), F(11)
            cp(DVE, ang, post[:], ["post"], [Fk(8)])
            ts1(DVE, ang, ang, CST[:, C_THETA:C_THETA + 1], ALU.mult, [Fk(8), "CST"], [Fk(8)])
            TWO_PI = 2.0 * math.pi
            PI_HI = 6.28125
            PI_LO = TWO_PI - PI_HI
            for which in (1, 0):
                src = ang
                if which == 0:
                    ts1(DVE, kf, ang, math.pi / 2.0, ALU.add, [Fk(8)], [Fk(9)])
                    cp(DVE, ang, kf, [Fk(9)], [Fk(8)])
                ts1(DVE, kf, src, 1.0 / TWO_PI, ALU.mult, [Fk(8)], [Fk(9)])
                kiv = ki.bitcast(I32)
                cp(DVE, kiv, kf, [Fk(9)], [Fk(10)])
                cp(DVE, kf, kiv, [Fk(10)], [Fk(9)])
                r_ = F(15)
                stt(r_, kf, -PI_HI, src, ALU.mult, ALU.add, [Fk(9), Fk(8)], [Fk(15)])
                stt(r_, kf, -PI_LO, r_, ALU.mult, ALU.add, [Fk(9), Fk(15)], [Fk(15)])
                ts1(DVE, mk_, r_, math.pi, ALU.is_gt, [Fk(15)], [Fk(11)])
                stt(r_, mk_, -TWO_PI, r_, ALU.mult, ALU.add, [Fk(11), Fk(15)], [Fk(15)])
                ts1(DVE, mk_, r_, -math.pi, ALU.is_lt, [Fk(15)], [Fk(11)])
                stt(r_, mk_, TWO_PI, r_, ALU.mult, ALU.add, [Fk(11), Fk(15)], [Fk(15)])
                ts(DVE, r_, r_, 3.1415925, -3.1415925, ALU.min, ALU.max, [Fk(15)], [Fk(15)])
                act(cs[:, which, :], r_, AF.Sin, [Fk(15)], [("cs", which)])
            cosT, sinT = cs[:, 0, :], cs[:, 1, :]
            if t == 0:
                dump("d_cs", cs[:], [("cs", 0), ("cs", 1)], [128, 2, T], F32)
                dump("d_u0", u[:], ukeys, [128, 8, T], BF16)
                dump("d_xa0", xa[:], xakeys, [128, 8, T], F32)

            for l in range(NL):
                ub = l * UPL
                OGF = big[:, 0:8, :]
                RGOF = big[:, 8:24, :]
                sbuf_s = big[:, 24:32, :]
                for h in range(8):
                    par = h % 2
                    fo = 0 if par == 0 else 8
                    bo = 0 if par == 0 else 7
                    wv, wk = next_unit(ub + h)
                    pf, pq = nextproj(), nextproj()
                    for kc in range(8):
                        mm(ps[pf][:], wv[:, kc, 128:256], u[:, kc, :], kc == 0, kc == 7, [wk, ("u", kc)], [PK(pf)])
                    for kc in range(8):
                        mm(ps[pq][:], wv[:, kc, 0:128], u[:, kc, :], kc == 0, kc == 7, [wk, ("u", kc)], [PK(pq)])
                    pv0, pv1 = nextproj(), nextproj()
                    for s_ in range(NSUB):
                        pv = pv0 if s_ < 2 else pv1
                        for kc in range(8):
                            mm(ps[pv][:, (s_ % 2) * 256:(s_ % 2 + 1) * 256], u[:, kc, s_ * 128:(s_ + 1) * 128], wv[:, kc, 256:512],
                               kc == 0, kc == 7, [wk, ("u", kc)], [PK(pv)])
                    E, L1, L2, Bc, T2, EQ, AK, EG = [F(fo + i) for i in range(8)]
                    kE, kL1, kL2, kB, kT2, kEQ, kAK, kEG = [Fk(fo + i) for i in range(8)]
                    Qp, Kp, KT, V, SG, SCM, OG = [B(bo + i) for i in range(7)]
                    kQp, kKp, kKT, kV, kSG, kSCM, kOG = [Bk(bo + i) for i in range(7)]
                    lbc = DV[:, DVO["lb"] + l * 8 + h:DVO["lb"] + l * 8 + h + 1]
                    lnomc = DV[:, DVO["lnom"] + l * 8 + h:DVO["lnom"] + l * 8 + h + 1]
                    smo = par * 32
                    nbm = sm[:, smo + 0:smo + 4]
                    ebm = sm[:, smo + 4:smo + 8]
                    dl = sm[:, smo + 8:smo + 12]
                    dlm = sm[:, smo + 12:smo + 16]
                    ss = sm[:, smo + 16:smo + 20]
                    rs = sm[:, smo + 20:smo + 24]
                    smk = ("sm", par)
                    act(E, ps[pf][:], AF.Exp, [PK(pf)], [kE], scale=-1.0)
                    act(L1, E, AF.Ln, [kE], [kL1], bias=1.0)
                    act(L2, E, AF.Ln, [kE, "DV"], [kL2], scale=lbc, bias=1.0)
                    tt(POOL, L2, L2, L1, ALU.subtract, [kL2, kL1], [kL2])
                    S.op(DVE, lambda e, Bc=Bc, L2=L2: e.tensor_tensor_scan(out=Bc, data0=CST[:, C_SCANM:C_SCANM + 512], data1=L2, initial=0.0, op0=ALU.mult, op1=ALU.add),
                         ["CST", kL2], [kB])
                    B3 = Bc.rearrange("p (c t) -> p c t", t=128)
                    ts1(DVE, nbm, B3[:, :, 63], -1.0, ALU.mult, [kB], [smk])
                    tt(DVE, T2.rearrange("p (c t) -> p c t", t=128), B3, nbm.unsqueeze(2).to_broadcast([128, 4, 128]), ALU.add, [kB, smk], [kT2])
                    T23 = T2.rearrange("p (c t) -> p c t", t=128)
                    act(ebm, nbm, AF.Exp, [smk], [smk], scale=-1.0)
                    act(dl, B3[:, :, 127], AF.Exp, [kB], [smk])
                    act(dlm, T23[:, :, 127], AF.Exp, [kT2], [smk])
                    act(EQ, ps[pq][:], AF.Exp, [PK(pq)], [kEQ], scale=-1.0)
                    act(EQ, EQ, AF.Ln, [kEQ], [kEQ], bias=1.0)
                    tt(POOL, EQ, T2, EQ, ALU.subtract, [kT2, kEQ], [kEQ])
                    act(EQ, EQ, AF.Exp, [kEQ], [kEQ])
                    tt(DVE, Qp, ps[pq][:], EQ, ALU.mult, [PK(pq), kEQ], [kQp])
                    tt(DVE, AK, ps[pf][:], L1, ALU.add, [PK(pf), kL1], [kAK])
                    tt(POOL, AK, AK, T2, ALU.add, [kAK, kT2], [kAK])
                    act(Kp, AK, AF.Exp, [kAK, "DV"], [kKp], scale=-1.0, bias=lnomc)
                    V3 = V.rearrange("p (s e) -> p s e", e=128)
                    SG3 = SG.rearrange("p (s e) -> p s e", e=128)
                    EG3 = EG.rearrange("p (s e) -> p s e", e=128)
                    for half, pv in ((0, pv0), (1, pv1)):
                        pvv = ps[pv][:].rearrange("p (s c) -> p s c", c=256)
                        cp(DVE, V3[:, half * 2:half * 2 + 2, :], pvv[:, :, 0:128], [PK(pv)], [kV])
                        act(EG3[:, half * 2:half * 2 + 2, :], pvv[:, :, 128:256], AF.Exp, [PK(pv)], [kEG], scale=-1.0)
                    act(EG, EG, AF.Ln, [kEG], [kEG], bias=1.0)
                    act(EG, EG, AF.Exp, [kEG], [kEG], scale=-1.0)
                    for half, pv in ((0, pv0), (1, pv1)):
                        pvv = ps[pv][:].rearrange("p (s c) -> p s c", c=256)
                        tt(DVE, SG3[:, half * 2:half * 2 + 2, :], pvv[:, :, 128:256], EG3[:, half * 2:half * 2 + 2, :], ALU.mult, [PK(pv), kEG], [kSG])
                    p4b = ps[4][:].bitcast(BF16)
                    for c in range(4):
                        tr(p4b[:, c * 128:(c + 1) * 128], Kp[:, c * 128:(c + 1) * 128], identb[:], [kKp, "identb"], [PK(4)])
                    act(KT, p4b[:, 0:512], AF.Identity, [PK(4)], [kKT])
                    KT3 = KT.rearrange("p (c d) -> p c d", d=128)
                    for c in range(4):
                        mm(ps[5][:, c * 128:(c + 1) * 128], Kp[:, c * 128:(c + 1) * 128], Qp[:, c * 128:(c + 1) * 128], True, True, [kKp, kQp], [PK(5)])
                    S.op(POOL, lambda e, SCM=SCM: e.memset(SCM, 0.0), [], [kSCM])
                    S.op(DVE, lambda e, SCM=SCM: e.copy_predicated(out=SCM, mask=maski[:].rearrange("p c i -> p (c i)"), data=ps[5][:]), [PK(5), "maski", kSCM], [kSCM])
                    SCM3 = SCM.rearrange("p (c i) -> p c i", i=128)
                    for c in range(4):
                        mm(ps[6][:, c * 128:(c + 1) * 128], KT3[:, c, :], V3[:, c, :], True, True, [kKT, kV], [PK(6)])
                    Sst = SH[:, l * 8 + h, :]
                    kS = ("SH", l * 8 + h)
                    tmpS = sm
                    for c in range(4):
                        ts1(DVE, HSB[:, c, :], Sst, ebm[:, c:c + 1], ALU.mult, [kS, smk], [("HSB", c)])
                        if True:
                            dst = FA[:, fo + 7, c * 128:(c + 1) * 128]
                            ts1(DVE, dst, ps[6][:, c * 128:(c + 1) * 128], dlm[:, c:c + 1], ALU.mult, [PK(6), smk, kEG], [kEG])
                            stt(Sst, Sst, dl[:, c:c + 1], dst, ALU.mult, ALU.add, [kS, smk, kEG], [kS])
                    for c in range(4):
                        mm(ps[7][:, c * 128:(c + 1) * 128], SCM3[:, c, :], V3[:, c, :], True, False, [kSCM, kV], [PK(7)])
                        mm(ps[7][:, c * 128:(c + 1) * 128], Qp[:, c * 128:(c + 1) * 128], HSB[:, c, :], False, True, [kQp, ("HSB", c)], [PK(7)])
                    act(E, ps[7][:], AF.Square, [PK(7)], [kE])
                    S.op(DVE, lambda e, ss=ss, E=E: e.tensor_reduce(out=ss, in_=E.rearrange("p (c t) -> p c t", t=128), axis=AX.X, op=ALU.add), [kE], [smk])
                    act(rs, ss, AF.Ln, [smk], [smk], scale=1.0 / 128.0, bias=HN_EPS)
                    act(rs, rs, AF.Exp, [smk], [smk], scale=-0.5)
                    OG3 = OG.rearrange("p (c e) -> p c e", e=128)
                    for c in range(4):
                        stt(OG3[:, c, :], ps[7][:, c * 128:(c + 1) * 128], rs[:, c:c + 1], SG3[:, c, :], ALU.mult, ALU.mult, [PK(7), smk, kSG], [kOG])
                    for c in range(4):
                        tr(p4b[:, c * 128:(c + 1) * 128], OG3[:, c, :], identb[:], [kOG, "identb"], [PK(4)])
                    act(OGF[:, h, :], p4b[:, 0:512], AF.Identity, [PK(4)], [("big", h)])

                for h in range(4):
                    gam = GAMMAS[h]
                    fo = 0 if h % 2 == 0 else 8
                    sidx = (l * 4 + h) * 2
                    kSR = [("SR", sidx), ("SR", sidx + 1)]
                    for dh in range(2):
                        act(RSB[:, 0, dh, :], SR[:, sidx + dh, :], AF.Identity, [kSR[dh]], [("RSB", 0, dh)])
                    wvA, wkA = next_unit(ub + 8 + h * 3)
                    QR = [B(14), B(15)]
                    KR = [B(16), B(17)]
                    kQR = [Bk(14), Bk(15)]
                    kKR = [Bk(16), Bk(17)]
                    for qk in range(2):
                        p1, p2 = nextproj(), nextproj()
                        for half, pb in ((0, p1), (1, p2)):
                            cb = qk * 256 + half * 128
                            for kc in range(8):
                                mm(ps[pb][:], wvA[:, kc, cb:cb + 128], u[:, kc, :], kc == 0, kc == 7, [wkA, ("u", kc)], [PK(pb)])
                        t1, t2_, t3, t4 = F(fo + 0), F(fo + 1), F(fo + 2), F(fo + 3)
                        k1, k2, k3, k4 = Fk(fo + 0), Fk(fo + 1), Fk(fo + 2), Fk(fo + 3)
                        R = QR if qk == 0 else KR
                        kR = kQR if qk == 0 else kKR
                        tt(DVE, t1, ps[p1][:], cosT, ALU.mult, [PK(p1), ("cs", 0)], [k1])
                        tt(DVE, t2_, ps[p2][:], sinT, ALU.mult, [PK(p2), ("cs", 1)], [k2])
                        tt(POOL, R[0], t1, t2_, ALU.subtract, [k1, k2], [kR[0]])
                        tt(DVE, t3, ps[p1][:], sinT, ALU.mult, [PK(p1), ("cs", 1)], [k3])
                        tt(DVE, t4, ps[p2][:], cosT, ALU.mult, [PK(p2), ("cs", 0)], [k4])
                        tt(POOL, R[1], t3, t4, ALU.add, [k3, k4], [kR[1]])
                    wvB, wkB = next_unit(ub + 8 + h * 3 + 1)
                    Vr = BA[:, 0:4, :]
                    SRG = BA[:, 4:8, :]
                    for s_ in range(NSUB):
                        pb = nextproj()
                        for kc in range(8):
                            mm(ps[pb][:], u[:, kc, s_ * 128:(s_ + 1) * 128], wvB[:, kc, :], kc == 0, kc == 7, [wkB, ("u", kc)], [PK(pb)])
                        cp(DVE, Vr[:, s_, :], ps[pb][:], [PK(pb)], [Bk(s_)])
                    wvC, wkC = next_unit(ub + 8 + h * 3 + 2)
                    for s_ in range(NSUB):
                        pb = nextproj()
                        for kc in range(8):
                            mm(ps[pb][:], u[:, kc, s_ * 128:(s_ + 1) * 128], wvC[:, kc, :], kc == 0, kc == 7, [wkC, ("u", kc)], [PK(pb)])
                        eg = F(fo + 4 + s_)
                        keg = Fk(fo + 4 + s_)
                        act(eg, ps[pb][:], AF.Exp, [PK(pb)], [keg], scale=-1.0)
                        act(eg, eg, AF.Ln, [keg], [keg], bias=1.0)
                        act(eg, eg, AF.Exp, [keg], [keg], scale=-1.0)
                        tt(DVE, SRG[:, s_, :], ps[pb][:], eg, ALU.mult, [PK(pb), keg], [Bk(4 + s_)])
                    KTr = BA[:, 8:10, :].rearrange("p a t -> p (a t)").rearrange("p (c d) -> p c d", d=256)
                    kKTr = [Bk(8), Bk(9)]
                    p4b = ps[4][:].bitcast(BF16).rearrange("p (c d) -> p c d", d=256)
                    for c in range(4):
                        for dh in range(2):
                            tr(p4b[:, c, dh * 128:(dh + 1) * 128], KR[dh][:, c * 128:(c + 1) * 128], identb[:], [kKR[dh], "identb"], [PK(4)])
                    for c in range(4):
                        act(KTr[:, c, :], p4b[:, c, :], AF.Identity, [PK(4), "CST"], kKTr, scale=CST[:, C_KDEC + h * 4 + c:C_KDEC + h * 4 + c + 1])
                    sbank = {0: (5, 0), 1: (6, 0), 2: (7, 0), 3: (7, 256)}
                    for cq in range(4):
                        bk_, o0 = sbank[cq]
                        for c in range(cq, 4):
                            oc = o0 + (c - cq) * 128
                            for dh in range(2):
                                mm(ps[bk_][:, oc:oc + 128], KR[dh][:, cq * 128:(cq + 1) * 128], QR[dh][:, c * 128:(c + 1) * 128], dh == 0, dh == 1,
                                   [kKR[dh], kQR[dh]], [PK(bk_)])
                    SCB = BA[:, 10:13, :].rearrange("p a t -> p (a t)")
                    kSCB = [Bk(10), Bk(11), Bk(12)]
                    bi0 = {0: 0, 1: 4, 2: 7, 3: 9}
                    Mh_ = CST[:, C_MH + h * 128:C_MH + (h + 1) * 128]
                    for cq in range(4):
                        bk_, o0 = sbank[cq]
                        b0 = bi0[cq] * 128
                        stt(SCB[:, b0:b0 + 128], ps[bk_][:, o0:o0 + 128], float(gam ** (-(cq * 128.0))), Mh_, ALU.mult, ALU.mult, [PK(bk_), "CST"], kSCB)
                        nof = 3 - cq
                        if nof > 0:
                            ts1(DVE, SCB[:, b0 + 128:b0 + 128 + nof * 128], ps[bk_][:, o0 + 128:o0 + 128 + nof * 128],
                                CST[:, C_KSC + h * 4 + cq:C_KSC + h * 4 + cq + 1], ALU.mult, [PK(bk_), "CST"], kSCB)
                    for dh in range(2):
                        for c in range(4):
                            mm(ps[5 + dh][:], KTr[:, c, dh * 128:(dh + 1) * 128], Vr[:, c, :], c == 0, c == 3, kKTr + [Bk(c)], [PK(5 + dh)])
                    for dh in range(2):
                        stt(SR[:, sidx + dh, :], SR[:, sidx + dh, :], float(gam ** 512.0), ps[5 + dh][:], ALU.mult, ALU.add, [kSR[dh], PK(5 + dh)], [kSR[dh]])
                    stat6 = sm[:, 96:120].rearrange("p (c s) -> p c s", s=6)
                    mv = sm[:, 64:72].rearrange("p (c s) -> p c s", s=2)
                    smr = ("sm", 2)
                    OB = [7, 0, 1, 2]
                    for c in range(4):
                        po = OB[c]
                        for cq in range(c + 1):
                            b0 = (bi0[cq] + (c - cq)) * 128
                            mm(ps[po][:], SCB[:, b0:b0 + 128], Vr[:, cq, :], cq == 0, False, kSCB + [Bk(cq)], [PK(po)])
                        for dh in range(2):
                            mm(ps[po][:], QR[dh][:, c * 128:(c + 1) * 128], RSB[:, 0, dh, :], False, dh == 1, [kQR[dh], ("RSB", 0, dh)], [PK(po)])
                        S.op(DVE, lambda e, c=c, po=po: e.bn_stats(out=stat6[:, c, :], in_=ps[po][:]), [PK(po)], [smr])
                        S.op(DVE, lambda e, c=c: e.bn_aggr(out=mv[:, c, :], in_=stat6[:, c, :]), [smr], [smr])
                    rsp4 = sm[:, 72:76]
                    nb4 = sm[:, 80:84]
                    tt(DVE, rsp4, mv[:, :, 1], CST[:, C_QD2 + h * 4:C_QD2 + h * 4 + 4], ALU.mult, [smr, "CST"], [smr])
                    act(rsp4, rsp4, AF.Ln, [smr], [smr], bias=HN_EPS)
                    act(rsp4, rsp4, AF.Exp, [smr], [smr], scale=-0.5)
                    tt(DVE, rsp4, rsp4, CST[:, C_QD + h * 4:C_QD + h * 4 + 4], ALU.mult, [smr, "CST"], [smr])
                    stt(nb4, mv[:, :, 0], -1.0, rsp4, ALU.mult, ALU.mult, [smr], [smr])
                    for c in range(4):
                        po = OB[c]
                        ON = FA[:, fo + 4 + (c % 2), :].bitcast(BF16)[:, 0:512]
                        kON = Fk(fo + 4 + (c % 2))
                        act(ON, ps[po][:], AF.Identity, [PK(po), smr], [kON], scale=rsp4[:, c:c + 1], bias=nb4[:, c:c + 1])
                        OGr = B(18 + (c % 2))
                        kOGr = Bk(18 + (c % 2))
                        tt(POOL, OGr, ON, SRG[:, c, :], ALU.mult, [kON, Bk(4 + c)], [kOGr])
                        pt = 4 if c % 2 == 0 else 3
                        ptb = ps[pt][:].bitcast(BF16)
                        for ec in range(4):
                            tr(ptb[:, ec * 128:(ec + 1) * 128], OGr[:, ec * 128:(ec + 1) * 128], identb[:], [kOGr, "identb"], [PK(pt)])
                        act(RGOF[:, h * 4:(h + 1) * 4, c * 128:(c + 1) * 128], ptb[:, 0:512].rearrange("p (a i) -> p a i", i=128), AF.Identity,
                            [PK(pt)], [("big", 8 + h * 4 + ec) for ec in range(4)])

                gate_sb = {}
                for gi in range(4):
                    wv, wk = next_unit(ub + 20 + gi)
                    for blk in range(4):
                        m = (gi % 2) * 4 + blk
                        pb = nextproj()
                        for kc in range(8):
                            mm(ps[pb][:], wv[:, kc, blk * 128:(blk + 1) * 128], u[:, kc, :], kc == 0, kc == 7, [wk, ("u", kc)], [PK(pb)])
                        si = (gi // 2) * 8 + m
                        g_ = F(si)
                        act(g_, ps[pb][:], AF.Exp, [PK(pb)], [Fk(si)], scale=-1.0)
                        act(g_, g_, AF.Ln, [Fk(si)], [Fk(si)], bias=1.0)
                        act(g_, g_, AF.Exp, [Fk(si)], [Fk(si)], scale=-1.0)
                        gate_sb[(gi // 2, m)] = si
                for ch in range(2):
                    wv, wk = next_unit(ub + 24 + ch)
                    for blk in range(4):
                        m = ch * 4 + blk
                        pb = nextproj()
                        for kc in range(8):
                            mm(ps[pb][:], wv[:, kc, blk * 128:(blk + 1) * 128], OGF[:, kc, :], kc == 0, kc == 7, [wk, ("big", kc)], [PK(pb)])
                        si = gate_sb[(0, m)]
                        tt(DVE, F(si), ps[pb][:], F(si), ALU.mult, [PK(pb), Fk(si)], [Fk(si)])
                for ch in range(2):
                    units = [next_unit(ub + 26 + ch * 2 + kh) for kh in range(2)]
                    pbs = [nextproj() for _ in range(4)]
                    for kh in range(2):
                        wv, wk = units[kh]
                        for blk in range(4):
                            for kc in range(8):
                                kk = kh * 8 + kc
                                mm(ps[pbs[blk]][:], wv[:, kc, blk * 128:(blk + 1) * 128], RGOF[:, kk, :], kk == 0, kk == 15, [wk, ("big", 8 + kk)], [PK(pbs[blk])])
                    for blk in range(4):
                        m = ch * 4 + blk
                        sa, sb_ = gate_sb[(0, m)], gate_sb[(1, m)]
                        tt(DVE, F(sb_), ps[pbs[blk]][:], F(sb_), ALU.mult, [PK(pbs[blk]), Fk(sb_)], [Fk(sb_)])
                        tt(POOL, sbuf_s[:, m, :], F(sa), F(sb_), ALU.add, [Fk(sa), Fk(sb_)], [("big", 24 + m)])
                if t == 0:
                    dump("d_big%d" % l, big[:], bigkeys, [128, 32, T], BF16)
                G1h = DVs(("G1h", l))
                for ch in range(2):
                    wv, wk = next_unit(ub + 30 + ch)
                    for blk in range(4):
                        m = ch * 4 + blk
                        pb = nextproj()
                        for kc in range(8):
                            mm(ps[pb][:], wv[:, kc, blk * 128:(blk + 1) * 128], sbuf_s[:, kc, :], kc == 0, kc == 7, [wk, ("big", 24 + kc)], [PK(pb)])
                        stt(xa[:, m, :], ps[pb][:], G1h[:, m:m + 1], xa[:, m, :], ALU.mult, ALU.add, [PK(pb), "DV", ("xa", m)], [("xa", m)])
                        ln_accum(m)
                ln_finish(DVs(("Ax1", l)), DVs(("Bx1", l)), DVs(("Au1", l)), DVs(("Bu1", l)))

                if t == 0:
                    dump("d_xa1_%d" % l, xa[:], xakeys, [128, 8, T], F32)
                    dump("d_u1_%d" % l, u[:], ukeys, [128, 8, T], BF16)
                bup = Vs("b_up", l, 32)
                for un in range(8):
                    wv, wk = next_unit(ub + 32 + un)
                    for blk in range(4):
                        j = un * 4 + blk
                        pb = nextproj()
                        for kc in range(8):
                            mm(ps[pb][:], wv[:, kc, blk * 128:(blk + 1) * 128], u[:, kc, :], kc == 0, kc == 7, [wk, ("u", kc)], [PK(pb)])
                        hr = F(j % 4)
                        act(hr, ps[pb][:], AF.Relu, [PK(pb), "VEC"], [Fk(j % 4)], bias=bup[:, j:j + 1])
                        tt(DVE if j % 2 == 0 else POOL, big[:, j, :], hr, hr, ALU.mult, [Fk(j % 4)], [("big", j)])
                G2 = DVs(("G2", l))
                for ch in range(2):
                    units = [next_unit(ub + 40 + ch * 4 + kq) for kq in range(4)]
                    pbs = [0, 1, 2, 3] if ch == 0 else [6, 7, 0, 1]
                    for kq in range(4):
                        wv, wk = units[kq]
                        for blk in range(4):
                            for kc in range(8):
                                kk = kq * 8 + kc
                                mm(ps[pbs[blk]][:], wv[:, kc, blk * 128:(blk + 1) * 128], big[:, kk, :], kk == 0, kk == 31, [wk, ("big", kk)], [PK(pbs[blk])])
                    for blk in range(4):
                        m = ch * 4 + blk
                        stt(xa[:, m, :], ps[pbs[blk]][:], G2[:, m:m + 1], xa[:, m, :], ALU.mult, ALU.add, [PK(pbs[blk]), "DV", ("xa", m)], [("xa", m)])
                        ln_accum(m)
                last = (l == NL - 1)
                ln_finish(DVs(("Ax2", l)), DVs(("Bx2", l)), None if last else DVs(("Au2", l)), None if last else DVs(("Bu2", l)))

            for s_ in range(NSUB):
                st_ = FA[:, (s_ % 2) * 2:(s_ % 2) * 2 + 2, :].rearrange("p a t -> p (a t)")
                stk = [Fk((s_ % 2) * 2), Fk((s_ % 2) * 2 + 1)]
                for jh in range(2):
                    pb = nextproj()
                    for jj in range(4):
                        j = jh * 4 + jj
                        tr(ps[pb][:, jj * 128:(jj + 1) * 128], xa[:, j, s_ * 128:(s_ + 1) * 128], identf, [("xa", j), "CST"], [PK(pb)])
                    act(st_[:, jh * 512:(jh + 1) * 512], ps[pb][:], AF.Identity, [PK(pb)], [stk[jh]])
                dma(POOL, out_d[t0 + s_ * 128:t0 + (s_ + 1) * 128, :], st_, stk, [], "ost%d" % (s_ % 2))

        fin = S.op(POOL, lambda e: e.nop(), [], [])
        for ch in list(S.dma_last):
            if S.dma_last[ch] is not fin:
                fin.deps.append(S.dma_last[ch])

        S.finalize()
        engsem = {e: es.enter_context(nc.semaphore("sem_" + e)) for e in ENGS}
        dmasem = {ch: es.enter_context(nc.semaphore("dsem_" + ch)) for ch in S.dma_count}
        with nc.Block() as block:
            @block.sync
            def _(e):
                S.emit(e, SP, engsem, dmasem)

            @block.tensor
            def _(e):
                S.emit(e, PE, engsem, dmasem)

            @block.scalar
            def _(e):
                S.emit(e, ACT, engsem, dmasem)

            @block.vector
            def _(e):
                S.emit(e, DVE, engsem, dmasem)

            @block.gpsimd
            def _(e):
                S.emit(e, POOL, engsem, dmasem)
    return nc


def _unit(Wm, rows, cols):
    sub = Wm[rows][:, cols]
    return np.ascontiguousarray(sub.reshape(8, 128, 512).transpose(1, 0, 2)).reshape(128, 4096)


def _pack_weights(w_in, w_pa, w_pb, w_o, w_up, w_down):
    units = []
    r1024 = np.arange(1024)
    for l in range(DEPTH):
        Wi = w_in[l]
        for h in range(8):
            cols = np.concatenate([0 + h * 128 + np.arange(128), 1024 + h * 128 + np.arange(128),
                                   2048 + h * 128 + np.arange(128), 3072 + h * 128 + np.arange(128)])
            units.append(_unit(Wi, r1024, cols))
        for h in range(4):
            qb = 4096 + h * 256
            kb = 5120 + h * 256
            ev = np.arange(0, 256, 2)
            od = np.arange(1, 256, 2)
            units.append(_unit(Wi, r1024, np.concatenate([qb + ev, qb + od, kb + ev, kb + od])))
            units.append(_unit(Wi, r1024, 6144 + h * 512 + np.arange(512)))
            units.append(_unit(Wi, r1024, 8192 + h * 512 + np.arange(512)))
        for gi in range(4):
            units.append(_unit(Wi, r1024, 10240 + gi * 512 + np.arange(512)))
        for ch in range(2):
            units.append(_unit(w_pa[l], r1024, ch * 512 + np.arange(512)))
        for ch in range(2):
            for kh in range(2):
                units.append(_unit(w_pb[l], kh * 1024 + r1024, ch * 512 + np.arange(512)))
        for ch in range(2):
            units.append(_unit(w_o[l], r1024, ch * 512 + np.arange(512)))
        for un in range(8):
            units.append(_unit(w_up[l], r1024, un * 512 + np.arange(512)))
        for ch in range(2):
            for kq in range(4):
                units.append(_unit(w_down[l], kq * 1024 + r1024, ch * 512 + np.arange(512)))
    return np.stack(units, axis=0)


def _fm(v, n):
    return np.ascontiguousarray(np.asarray(v, np.float32).reshape(n, 128).T)


def _consts():
    C = np.zeros((128, NCONST), np.float32)
    C[:, C_IDENT:C_IDENT + 128] = np.eye(128, dtype=np.float32)
    j = np.arange(128)[:, None]
    i = np.arange(128)[None, :]
    C[:, C_MASKT:C_MASKT + 128] = (j <= i).astype(np.float32)
    jj = np.arange(128, dtype=np.float64)
    for h in range(4):
        g = np.float64(GAMMAS[h])
        C[:, C_MH + h * 128:C_MH + (h + 1) * 128] = ((j <= i) * (g ** (-(j + 1.0))) / 16.0).astype(np.float32)
        for c in range(4):
            C[:, C_KSC + h * 4 + c] = (g ** (-(c * 128.0 + jj + 1.0)) / 16.0).astype(np.float32)
            C[:, C_KDEC + h * 4 + c] = (g ** (511.0 - c * 128.0 - jj) / 16.0).astype(np.float32)
            C[:, C_QD + h * 4 + c] = (g ** (c * 128.0 + jj + 1.0)).astype(np.float32)
            C[:, C_QD2 + h * 4 + c] = (g ** (2.0 * (c * 128.0 + jj + 1.0))).astype(np.float32)
    sc = np.ones(512, np.float32)
    sc[::128] = 0.0
    C[:, C_SCANM:C_SCANM + 512] = sc[None, :]
    C[:, C_THETA] = (10000.0 ** (-np.linspace(0.0, 1.0, 128, dtype=np.float32))).astype(np.float32)
    return C


_NC_CACHE = {}


def kernel(x, c, positions, lb_logits, w_ada, b_ada, w_in, w_pa, w_pb, w_o,
           ln1_g, ln1_b, w_up, b_up, w_down, b_down, ln2_g, ln2_b, _NT=8, _cores=8, _dbg=False):
    x = np.asarray(x, np.float32)
    wall = _pack_weights(*[np.asarray(a, np.float32) for a in (w_in, w_pa, w_pb, w_o, w_up, w_down)])
    w_ada = np.asarray(w_ada, np.float32)
    wada = np.stack([np.ascontiguousarray(w_ada[l][:, g * 1024:(g + 1) * 1024].reshape(8, 128, 1024).transpose(1, 0, 2)).reshape(128, 8192)
                     for l in range(DEPTH) for g in range(6)], axis=0)
    consts = _consts()
    in_maps = []
    for b in range(_cores):
        V = np.zeros((128, NVEC), np.float32)
        for l in range(DEPTH):
            for nm, arr, n in (("ln1_g", ln1_g, 8), ("ln1_b", ln1_b, 8), ("ln2_g", ln2_g, 8), ("ln2_b", ln2_b, 8),
                               ("b_down", b_down, 8), ("b_up", b_up, 32), ("b_ada", b_ada, 48), ("lbl", lb_logits, 8)):
                V[:, VOFF[(nm, l)]:VOFF[(nm, l)] + n] = _fm(np.asarray(arr)[l], n)
        V[:, VOFF["c"]:VOFF["c"] + 8] = _fm(np.asarray(c)[b], 8)
        pos = np.ascontiguousarray(np.broadcast_to(np.asarray(positions)[b].astype(np.int32)[None, :], (128, SEQ)))
        in_maps.append({"x": np.ascontiguousarray(x[b]), "pos": pos, "wall": wall, "wada": wada, "vecs": V, "consts": consts})
    key = (_NT, _dbg)
    if key not in _NC_CACHE:
        _NC_CACHE[key] = build_nc(NT=_NT, dbg=_dbg)
    nc = _NC_CACHE[key]
    res = run_bass_kernel_spmd(nc, in_maps, core_ids=list(range(_cores)))
    out = np.stack([np.asarray(r["out"], np.float32) for r in res.results], axis=0)
    if _dbg:
        return out, res.results[0]
    if _cores < 8:
        return out
    return out.reshape(8, SEQ, D)
```

```python
import math
from contextlib import ExitStack
import numpy as np
import concourse.bass as bass
import concourse.mybir as mybir
from concourse.bass_utils import run_bass_kernel_spmd

F32 = mybir.dt.float32
BF16 = mybir.dt.bfloat16
I32 = mybir.dt.int32
AF = mybir.ActivationFunctionType
ALU = mybir.AluOpType
AX = mybir.AxisListType

PE, ACT, DVE, POOL, SP = "tensor", "scalar", "vector", "gpsimd", "sync"
ENGS = (PE, ACT, DVE, POOL, SP)

D = 1024
SEQ = 4096
T = 512
NSUB = 4
DEPTH = 2
ALPHA = (2 * DEPTH) ** 0.25
LN_EPS = 1e-5
HN_EPS = 1e-6
UPL = 48
GAMMAS = [1.0 - 2.0 ** (-5.0 - h) for h in range(4)]


class Op:
    __slots__ = ("eng", "fn", "deps", "need_sig", "sigval", "dma", "dma_val", "waits", "wkeys", "gidx")

    def __init__(self, eng, fn):
        self.eng = eng
        self.fn = fn
        self.deps = []
        self.need_sig = False
        self.sigval = 0
        self.dma = None
        self.dma_val = 0
        self.waits = {}
        self.wkeys = frozenset()


class Sched:
    def __init__(self):
        self.streams = {e: [] for e in ENGS}
        self.lastw = {}
        self.readers = {}
        self.dma_count = {}
        self.dma_last = {}

    def op(self, eng, fn, reads=(), writes=(), dma=None):
        o = Op(eng, fn)
        reads = tuple(reads)
        writes = tuple(writes)
        deps = set()
        for k in reads:
            w = self.lastw.get(k)
            if w is not None:
                deps.add(w)
        for k in writes:
            w = self.lastw.get(k)
            if w is not None:
                deps.add(w)
            for r in self.readers.get(k, ()):
                deps.add(r)
        if dma is not None:
            prev = self.dma_last.get(dma)
            if prev is not None:
                deps.add(prev)
            self.dma_last[dma] = o
            cnt = self.dma_count.get(dma, 0) + 1
            self.dma_count[dma] = cnt
            o.dma = dma
            o.dma_val = 16 * cnt
        touched = set(reads) | set(writes)
        for d in deps:
            if d is o:
                continue
            if d.dma is None and d.eng == eng:
                if eng == PE:
                    continue
                if not (d.wkeys & touched):
                    continue
            if d.dma is None:
                d.need_sig = True
            o.deps.append(d)
        for k in reads:
            self.readers.setdefault(k, []).append(o)
        for k in writes:
            self.lastw[k] = o
            self.readers[k] = []
        o.wkeys = frozenset(writes)
        self.streams[eng].append(o)
        self.gcount = getattr(self, "gcount", 0) + 1
        o.gidx = self.gcount
        return o

    def finalize(self):
        for e, st in self.streams.items():
            c = 0
            for o in st:
                if o.dma is None and o.need_sig:
                    c += 1
                    o.sigval = c
        for e, st in self.streams.items():
            known = {}
            for o in st:
                w = {}
                for d in o.deps:
                    if d.dma is not None:
                        key = ("dma", d.dma)
                        val = d.dma_val
                    else:
                        key = ("eng", d.eng)
                        val = d.sigval
                    if val > known.get(key, 0) and val > w.get(key, 0):
                        w[key] = val
                for k, v in w.items():
                    known[k] = v
                o.waits = w

    def emit(self, engobj, eng, engsem, dmasem):
        for o in self.streams[eng]:
            for (kind, name), v in o.waits.items():
                s = engsem[name] if kind == "eng" else dmasem[name]
                engobj.wait_ge(s, v)
            ins = o.fn(engobj)
            if o.dma is not None:
                ins.then_inc(dmasem[o.dma], 16)
            elif o.need_sig:
                ins.then_inc(engsem[o.eng], 1)


def _vec_layout():
    off = {}
    c = 0
    for l in range(DEPTH):
        for nm, n in (("ln1_g", 8), ("ln1_b", 8), ("ln2_g", 8), ("ln2_b", 8), ("b_down", 8), ("b_up", 32), ("b_ada", 48), ("lbl", 8)):
            off[(nm, l)] = c
            c += n
    off["c"] = c
    c += 8
    return off, c


VOFF, NVEC = _vec_layout()
C_IDENT, C_MASKT, C_MH, C_SCANM, C_KSC, C_KDEC, C_QD, C_QD2, C_THETA = 0, 128, 256, 768, 1280, 1296, 1312, 1328, 1344
NCONST = 1346


def build_nc(NT=8, NL=DEPTH, dbg=False):
    nc = bass.Bass("TRN2", target_bir_lowering=False)
    x_d = nc.dram_tensor("x", [SEQ, D], F32, kind="ExternalInput").ap()
    pos_d = nc.dram_tensor("pos", [128, SEQ], I32, kind="ExternalInput").ap()
    wall_d = nc.dram_tensor("wall", [DEPTH * UPL, 128, 4096], F32, kind="ExternalInput").ap()
    wada_d = nc.dram_tensor("wada", [DEPTH * 6, 128, 8192], F32, kind="ExternalInput").ap()
    vec_d = nc.dram_tensor("vecs", [128, NVEC], F32, kind="ExternalInput").ap()
    cst_d = nc.dram_tensor("consts", [128, NCONST], F32, kind="ExternalInput").ap()
    out_d = nc.dram_tensor("out", [SEQ, D], F32, kind="ExternalOutput").ap()
    wscr = nc.dram_tensor("wscr", [DEPTH * UPL, 128, 4096], BF16, kind="Internal").ap()

    S = Sched()
    es = ExitStack()
    with es:
        def sb(name, shape, dt):
            return es.enter_context(nc.sbuf_tensor(name, shape, dt))

        xa = sb("xa", [128, 8, T], F32)
        u = sb("u", [128, 8, T], BF16)
        big = sb("big", [128, 32, T], BF16)
        ring = sb("ring", [128, 4, 4096], BF16)
        FA = sb("FA", [128, 16, T], F32)
        BA = sb("BA", [128, 20, T], BF16)
        lnb = sb("lnb", [128, 4, T], BF16)
        SH = sb("SH", [128, DEPTH * 8, 128], F32)
        SR = sb("SR", [128, DEPTH * 4 * 2, 512], F32)
        HSB = sb("HSB", [128, 4, 128], BF16)
        RSB = sb("RSB", [128, 2, 2, 512], BF16)
        cs = sb("cs", [128, 2, T], F32)
        post = sb("post", [128, T], I32)
        VEC = sb("VEC", [128, NVEC], F32)
        CST = sb("CST", [128, NCONST], F32)
        DV = sb("DV", [128, 512], F32)
        identb = sb("identb", [128, 128], BF16)
        onesb = sb("onesb", [128, 128], BF16)
        maski = sb("maski", [128, 4, 128], I32)
        sm = sb("sm", [128, 128], F32)

        ps = [es.enter_context(nc.psum_tensor("ps%d" % i, [128, 512], F32)) for i in range(8)]

        def F(i):
            return FA[:, i, :]

        def Fk(i):
            return ("F", i)

        def B(i):
            return BA[:, i, :]

        def Bk(i):
            return ("B", i)

        pj = [0]

        def nextproj():
            i = pj[0] % 4
            pj[0] += 1
            return i

        def PK(i):
            return ("ps", i)

        def mm(out, lhsT, rhs, start, stop, reads, writes):
            S.op(PE, lambda e: e.matmul(out, lhsT=lhsT, rhs=rhs, start=start, stop=stop), reads, writes)

        def tr(out, in_, ident, reads, writes):
            S.op(PE, lambda e: e.transpose(out=out, in_=in_, identity=ident), reads, writes)

        def act(out, in_, func, reads, writes, scale=1.0, bias=0.0):
            S.op(ACT, lambda e: e.activation(out=out, in_=in_, func=func, bias=bias, scale=scale), reads, writes)

        def tt(eng, out, in0, in1, op, reads, writes):
            S.op(eng, lambda e: e.tensor_tensor(out=out, in0=in0, in1=in1, op=op), reads, writes)

        def ts(eng, out, in0, s1, s2, op0, op1, reads, writes):
            S.op(eng, lambda e: e.tensor_scalar(out=out, in0=in0, scalar1=s1, scalar2=s2, op0=op0, op1=op1), reads, writes)

        def ts1(eng, out, in0, s1, op0, reads, writes):
            S.op(eng, lambda e: e.tensor_scalar(out=out, in0=in0, scalar1=s1, scalar2=None, op0=op0), reads, writes)

        def stt(out, in0, scalar, in1, op0, op1, reads, writes):
            S.op(DVE, lambda e: e.scalar_tensor_tensor(out=out, in0=in0, scalar=scalar, in1=in1, op0=op0, op1=op1), reads, writes)

        def cp(eng, out, in_, reads, writes):
            S.op(eng, lambda e: e.tensor_copy(out=out, in_=in_), reads, writes)

        def dma(eng, out, in_, reads, writes, ch, **kw):
            S.op(eng, lambda e: e.dma_start(out=out, in_=in_, **kw), reads, writes, dma=ch)

        def dump(name, ap, keys, shape, dt):
            if not dbg:
                return
            d = nc.dram_tensor(name, shape, dt, kind="ExternalOutput").ap()
            dma(POOL, d, ap, keys, [], "dbg")

        ucount = [0]

        def next_unit(uidx):
            slot = ucount[0] % 4
            ucount[0] += 1
            dma(SP, ring[:, slot, :], wscr[uidx], [("wscr", uidx)], [("ring", slot)], "ring%d" % slot)
            return ring[:, slot, :].rearrange("p (k n) -> p k n", n=512), ("ring", slot)

        dma(SP, VEC[:], vec_d, [], ["VEC"], "ldv")
        dma(SP, CST[:], cst_d, [], ["CST"], "ldc")
        for uidx in range(NL * UPL):
            dma(POOL, wscr[uidx], wall_d[uidx], [], [("wscr", uidx)], "wcv%d" % (uidx % 4), max_dma_last_dim=4096)

        cp(DVE, identb[:], CST[:, C_IDENT:C_IDENT + 128], ["CST"], ["identb"])
        S.op(DVE, lambda e: e.memset(onesb[:], 1.0 / 1024.0), [], ["onesb"])
        for c in range(4):
            cp(DVE, maski[:, c, :], CST[:, C_MASKT:C_MASKT + 128], ["CST"], ["maski"])
        S.op(DVE, lambda e: e.memset(SH[:], 0.0), [], [("SH", i) for i in range(DEPTH * 8)])
        S.op(DVE, lambda e: e.memset(SR[:], 0.0), [], [("SR", i) for i in range(DEPTH * 8)])
        identf = CST[:, C_IDENT:C_IDENT + 128]
        act(ps[5][:], CST[:, C_SCANM:C_SCANM + 512], AF.Identity, ["CST"], [PK(5)], scale=0.0)

        DVO = {}
        dvc = [0]

        def dv(name, n=8):
            DVO[name] = dvc[0]
            dvc[0] += n
            return DVO[name]

        def DVs(name, n=8):
            o = DVO[name]
            return DV[:, o:o + n]

        def Vs(name, l, n=8):
            o = VOFF[(name, l)]
            return VEC[:, o:o + n]

        dv("cond")
        dv("tmp")
        cfm = VEC[:, VOFF["c"]:VOFF["c"] + 8]
        act(DVs("tmp"), cfm, AF.Exp, ["VEC"], ["DV"], scale=-1.0)
        act(DVs("tmp"), DVs("tmp"), AF.Ln, ["DV"], ["DV"], bias=1.0)
        act(DVs("tmp"), DVs("tmp"), AF.Exp, ["DV"], ["DV"], scale=-1.0)
        tt(DVE, DVs("cond"), cfm, DVs("tmp"), ALU.mult, ["VEC", "DV"], ["DV"])
        bigf = big[:].rearrange("p a t -> p (a t)").bitcast(F32)
        bigkeys = [("big", i) for i in range(32)]
        faf = FA[:].rearrange("p a t -> p (a t)")
        fakeys = [Fk(i) for i in range(16)]
        for l in range(NL):
            dv(("mod", l), 48)
            for g in range(6):
                abuf, akeys = (bigf, bigkeys) if g % 2 == 0 else (faf, fakeys)
                dma(SP, abuf, wada_d[l * 6 + g], [], akeys, "ada%d" % (g % 2))
                wv = abuf.rearrange("p (k n) -> p k n", n=1024)
                for jb in range(8):
                    jj = g * 8 + jb
                    for kc in range(8):
                        mm(ps[0][:, jj:jj + 1], wv[:, kc, jb * 128:(jb + 1) * 128], DV[:, DVO["cond"] + kc:DVO["cond"] + kc + 1],
                           kc == 0, kc == 7, akeys + ["DV"], [PK(0)])
            tt(DVE, DVs(("mod", l), 48), ps[0][:, 0:48], Vs("b_ada", l, 48), ALU.add, [PK(0), "VEC"], ["DV"])

        def MOD(l, which):
            o = DVO[("mod", l)] + which * 8
            return DV[:, o:o + 8]

        dv("lb", 16)
        dv("lnom", 16)
        S.op(DVE, lambda e: e.memset(DVs("lb", 16), 0.0), [], ["DV"])
        if NL > 1:
            lb1 = DV[:, DVO["lb"] + 8:DVO["lb"] + 16]
            tt(DVE, DVs("tmp"), Vs("lbl", 1), Vs("lbl", 0), ALU.subtract, ["VEC"], ["DV"])
            act(DVs("tmp"), DVs("tmp"), AF.Exp, ["DV"], ["DV"], scale=-1.0)
            act(DVs("tmp"), DVs("tmp"), AF.Ln, ["DV"], ["DV"], bias=1.0)
            act(lb1, DVs("tmp"), AF.Exp, ["DV"], ["DV"], scale=-1.0)
        act(DVs("lnom", 16), DVs("lb", 16), AF.Ln, ["DV"], ["DV"], scale=-1.0, bias=1.0)

        dv("one_s", 8)
        for l in range(NL):
            for nm in ("G1h", "G2", "Ax1", "Bx1", "Au1", "Bu1", "Ax2", "Bx2", "Au2", "Bu2"):
                dv((nm, l))
        dv("A0u")
        dv("B0u")
        ts1(DVE, DVs("A0u"), MOD(0, 1), 1.0, ALU.add, ["DV"], ["DV"])
        cp(DVE, DVs("B0u"), MOD(0, 0), ["DV"], ["DV"])
        for l in range(NL):
            ts1(DVE, DVs(("G1h", l)), MOD(l, 2), 1.0, ALU.add, ["DV"], ["DV"])
            ts1(DVE, DVs(("G2", l)), MOD(l, 5), 1.0, ALU.add, ["DV"], ["DV"])
            ts1(DVE, DVs(("Ax1", l)), Vs("ln1_g", l), ALPHA, ALU.mult, ["VEC"], ["DV"])
            tt(DVE, DVs("tmp"), DVs(("G2", l)), Vs("b_down", l), ALU.mult, ["DV", "VEC"], ["DV"])
            stt(DVs(("Bx1", l)), Vs("ln1_b", l), ALPHA, DVs("tmp"), ALU.mult, ALU.add, ["VEC", "DV"], ["DV"])
            ts1(DVE, DVs("one_s"), MOD(l, 4), 1.0, ALU.add, ["DV"], ["DV"])
            tt(DVE, DVs(("Au1", l)), Vs("ln1_g", l), DVs("one_s"), ALU.mult, ["VEC", "DV"], ["DV"])
            tt(DVE, DVs("tmp"), Vs("ln1_b", l), DVs("one_s"), ALU.mult, ["VEC", "DV"], ["DV"])
            tt(DVE, DVs(("Bu1", l)), DVs("tmp"), MOD(l, 3), ALU.add, ["DV"], ["DV"])
            if l < NL - 1:
                ts1(DVE, DVs(("Ax2", l)), Vs("ln2_g", l), ALPHA, ALU.mult, ["VEC"], ["DV"])
                ts1(DVE, DVs(("Bx2", l)), Vs("ln2_b", l), ALPHA, ALU.mult, ["VEC"], ["DV"])
                ts1(DVE, DVs("one_s"), MOD(l + 1, 1), 1.0, ALU.add, ["DV"], ["DV"])
                tt(DVE, DVs(("Au2", l)), Vs("ln2_g", l), DVs("one_s"), ALU.mult, ["VEC", "DV"], ["DV"])
                tt(DVE, DVs("tmp"), Vs("ln2_b", l), DVs("one_s"), ALU.mult, ["VEC", "DV"], ["DV"])
                tt(DVE, DVs(("Bu2", l)), DVs("tmp"), MOD(l + 1, 0), ALU.add, ["DV"], ["DV"])
            else:
                cp(DVE, DVs(("Ax2", l)), Vs("ln2_g", l), ["VEC"], ["DV"])
                cp(DVE, DVs(("Bx2", l)), Vs("ln2_b", l), ["VEC"], ["DV"])
        assert dvc[0] <= 512

        dump("d_dv", DV[:], ["DV"], [128, 512], F32)
        xakeys = [("xa", m) for m in range(8)]
        ukeys = [("u", m) for m in range(8)]

        def ln_accum(m):
            rb = lnb[:, m % 2, :]
            rq = lnb[:, 2 + (m % 2), :]
            act(rb, xa[:, m, :], AF.Identity, [("xa", m)], [("lnb", m % 2)])
            act(rq, xa[:, m, :], AF.Square, [("xa", m)], [("lnb", 2 + m % 2)])
            mm(ps[4][:], onesb[:], rb, m == 0, m == 7, ["onesb", ("lnb", m % 2)], [PK(4)])
            mm(ps[5][:], onesb[:], rq, m == 0, m == 7, ["onesb", ("lnb", 2 + m % 2)], [PK(5)])

        def ln_finish(Ax, Bx, Au, Bu):
            mean_sb, msq, rstd = F(12), F(13), F(14)
            act(mean_sb, ps[4][:], AF.Identity, [PK(4)], [Fk(12)])
            tt(DVE, msq, mean_sb, mean_sb, ALU.mult, [Fk(12)], [Fk(13)])
            tt(DVE, msq, ps[5][:], msq, ALU.subtract, [PK(5), Fk(13)], [Fk(13)])
            act(rstd, msq, AF.Ln, [Fk(13)], [Fk(14)], bias=LN_EPS)
            act(rstd, rstd, AF.Exp, [Fk(14)], [Fk(14)], scale=-0.5)
            for m in range(8):
                xm = xa[:, m, :]
                tt(DVE, xm, xm, mean_sb, ALU.subtract, [("xa", m), Fk(12)], [("xa", m)])
                tt(POOL, xm, xm, rstd, ALU.mult, [("xa", m), Fk(14)], [("xa", m)])
                if Au is not None:
                    act(u[:, m, :], xm, AF.Identity, [("xa", m), "DV"], [("u", m)], scale=Au[:, m:m + 1], bias=Bu[:, m:m + 1])
                act(xm, xm, AF.Identity, [("xa", m), "DV"], [("xa", m)], scale=Ax[:, m:m + 1], bias=Bx[:, m:m + 1])

        for t in range(NT):
            t0 = t * T
            xin = FA[:, 0:8, :].rearrange("p a t -> p (a t)").rearrange("p (s f) -> p s f", f=1024)
            fk8 = [Fk(i) for i in range(8)]
            for s_ in range(NSUB):
                dma(POOL, xin[:, s_, :], x_d[t0 + s_ * 128:t0 + (s_ + 1) * 128, :], [], fk8, "xld%d" % s_)
            dma(POOL, post[:], pos_d[:, t0:t0 + T], [], ["post"], "pld")
            for j in range(8):
                pb = nextproj()
                for s_ in range(NSUB):
                    tr(ps[pb][:, s_ * 128:(s_ + 1) * 128], xin[:, s_, j * 128:(j + 1) * 128], identf, fk8 + ["CST"], [PK(pb)])
                act(xa[:, j, :], ps[pb][:], AF.Identity, [PK(pb)], [("xa", j)], scale=ALPHA)
                act(u[:, j, :], ps[pb][:], AF.Identity, [PK(pb), "DV"], [("u", j)],
                    scale=DV[:, DVO["A0u"] + j:DVO["A0u"] + j + 1], bias=DV[:, DVO["B0u"] + j:DVO["B0u"] + j + 1])

            ang, kf, ki, mk_ = F(8), F(9), F(10), F(11)
            cp(DVE, ang, post[:], ["post"], [Fk(8)])
            ts1(DVE, ang, ang, CST[:, C_THETA:C_THETA + 1], ALU.mult, [Fk(8), "CST"], [Fk(8)])
            TWO_PI = 2.0 * math.pi
            PI_HI = 6.28125
            PI_LO = TWO_PI - PI_HI
            for which in (1, 0):
                src = ang
                if which == 0:
                    ts1(DVE, kf, ang, math.pi / 2.0, ALU.add, [Fk(8)], [Fk(9)])
                    cp(DVE, ang, kf, [Fk(9)], [Fk(8)])
                ts1(DVE, kf, src, 1.0 / TWO_PI, ALU.mult, [Fk(8)], [Fk(9)])
                kiv = ki.bitcast(I32)
                cp(DVE, kiv, kf, [Fk(9)], [Fk(10)])
                cp(DVE, kf, kiv, [Fk(10)], [Fk(9)])
                r_ = F(15)
                stt(r_, kf, -PI_HI, src, ALU.mult, ALU.add, [Fk(9), Fk(8)], [Fk(15)])
                stt(r_, kf, -PI_LO, r_, ALU.mult, ALU.add, [Fk(9), Fk(15)], [Fk(15)])
                ts1(DVE, mk_, r_, math.pi, ALU.is_gt, [Fk(15)], [Fk(11)])
                stt(r_, mk_, -TWO_PI, r_, ALU.mult, ALU.add, [Fk(11), Fk(15)], [Fk(15)])
                ts1(DVE, mk_, r_, -math.pi, ALU.is_lt, [Fk(15)], [Fk(11)])
                stt(r_, mk_, TWO_PI, r_, ALU.mult, ALU.add, [Fk(11), Fk(15)], [Fk(15)])
                ts(DVE, r_, r_, 3.1415925, -3.1415925, ALU.min, ALU.max, [Fk(15)], [Fk(15)])
                act(cs[:, which, :], r_, AF.Sin, [Fk(15)], [("cs", which)])
            cosT, sinT = cs[:, 0, :], cs[:, 1, :]
            if t == 0:
                dump("d_cs", cs[:], [("cs", 0), ("cs", 1)], [128, 2, T], F32)
                dump("d_u0", u[:], ukeys, [128, 8, T], BF16)
                dump("d_xa0", xa[:], xakeys, [128, 8, T], F32)

            for l in range(NL):
                ub = l * UPL
                OGF = big[:, 0:8, :]
                RGOF = big[:, 8:24, :]
                sbuf_s = big[:, 24:32, :]
                def hg_names(h):
                    par = h % 2
                    fo, bo, smo = par * 8, par * 8, par * 32
                    Fs = [F(fo + i) for i in range(8)]
                    Fks = [Fk(fo + i) for i in range(8)]
                    Bs = [B(bo + i) for i in range(8)]
                    Bks = [Bk(bo + i) for i in range(8)]
                    return par, fo, Fs, Fks, Bs, Bks, smo

                def hg_proj(h):
                    wv, wk = next_unit(ub + h)
                    pf, pq = nextproj(), nextproj()
                    for kc in range(8):
                        mm(ps[pf][:], wv[:, kc, 128:256], u[:, kc, :], kc == 0, kc == 7, [wk, ("u", kc)], [PK(pf)])
                    for kc in range(8):
                        mm(ps[pq][:], wv[:, kc, 0:128], u[:, kc, :], kc == 0, kc == 7, [wk, ("u", kc)], [PK(pq)])
                    pv0, pv1 = nextproj(), nextproj()
                    for s_ in range(NSUB):
                        pv = pv0 if s_ < 2 else pv1
                        for kc in range(8):
                            mm(ps[pv][:, (s_ % 2) * 256:(s_ % 2 + 1) * 256], u[:, kc, s_ * 128:(s_ + 1) * 128], wv[:, kc, 256:512],
                               kc == 0, kc == 7, [wk, ("u", kc)], [PK(pv)])
                    return pf, pq, pv0, pv1

                def hg_wave1(h, banks):
                    pf, pq, pv0, pv1 = banks
                    par, fo, Fs, Fks, Bs, Bks, smo = hg_names(h)
                    E, L1, L2, NZ, EQ, QC, BT, EG = Fs
                    kE, kL1, kL2, kNZ, kEQ, kQC, kBT, kEG = Fks
                    V, GC = Bs[3], Bs[7]
                    kV, kGC = Bks[3], Bks[7]
                    lbc = DV[:, DVO["lb"] + l * 8 + h:DVO["lb"] + l * 8 + h + 1]
                    V3 = V.rearrange("p (s e) -> p s e", e=128)
                    GC3 = GC.rearrange("p (s e) -> p s e", e=128)
                    EG3 = EG.rearrange("p (s e) -> p s e", e=128)
                    act(E, ps[pf][:], AF.Exp, [PK(pf)], [kE], scale=-1.0)
                    act(EQ, ps[pq][:], AF.Exp, [PK(pq)], [kEQ], scale=-1.0)
                    act(QC, ps[pq][:], AF.Identity, [PK(pq)], [kQC])
                    for half, pv in ((0, pv0), (1, pv1)):
                        pvv = ps[pv][:].rearrange("p (s c) -> p s c", c=256)
                        act(V3[:, half * 2:half * 2 + 2, :], pvv[:, :, 0:128], AF.Identity, [PK(pv)], [kV])
                        act(EG3[:, half * 2:half * 2 + 2, :], pvv[:, :, 128:256], AF.Exp, [PK(pv)], [kEG], scale=-1.0)
                        act(GC3[:, half * 2:half * 2 + 2, :], pvv[:, :, 128:256], AF.Identity, [PK(pv)], [kGC])
                    act(L1, E, AF.Ln, [kE], [kL1], bias=1.0)
                    act(L2, E, AF.Ln, [kE, "DV"], [kL2], scale=lbc, bias=1.0)
                    act(NZ, E, AF.Ln, [kE], [kNZ])
                    act(EQ, EQ, AF.Ln, [kEQ], [kEQ], bias=1.0)
                    act(EG, EG, AF.Ln, [kEG], [kEG], bias=1.0)
                    act(EG, EG, AF.Exp, [kEG], [kEG], scale=-1.0)

                def hg_mid(h):
                    par, fo, Fs, Fks, Bs, Bks, smo = hg_names(h)
                    E, L1, L2, NZ, EQ, QC, BT, EG = Fs
                    kE, kL1, kL2, kNZ, kEQ, kQC, kBT, kEG = Fks
                    SG, GC = Bs[4], Bs[7]
                    kSG, kGC = Bks[4], Bks[7]
                    smk = ("sm", par)
                    nbm = sm[:, smo + 0:smo + 4]
                    bl = sm[:, smo + 24:smo + 28]
                    tt(POOL, L2, L2, L1, ALU.subtract, [kL2, kL1], [kL2])
                    tt(POOL, NZ, L1, NZ, ALU.subtract, [kL1, kNZ], [kNZ])
                    S.op(DVE, lambda e, BT=BT, L2=L2: e.tensor_tensor_scan(out=BT, data0=CST[:, C_SCANM:C_SCANM + 512], data1=L2, initial=0.0, op0=ALU.mult, op1=ALU.add),
                         ["CST", kL2], [kBT])
                    BT3 = BT.rearrange("p (c t) -> p c t", t=128)
                    ts1(DVE, nbm, BT3[:, :, 63], -1.0, ALU.mult, [kBT], [smk])
                    tt(DVE, BT3, BT3, nbm.unsqueeze(2).to_broadcast([128, 4, 128]), ALU.add, [kBT, smk], [kBT])
                    tt(DVE, bl, BT3[:, :, 127], nbm, ALU.subtract, [kBT, smk], [smk])
                    tt(POOL, EQ, BT, EQ, ALU.subtract, [kBT, kEQ], [kEQ])
                    tt(POOL, NZ, NZ, BT, ALU.add, [kNZ, kBT], [kNZ])
                    tt(DVE, SG, GC, EG, ALU.mult, [kGC, kEG], [kSG])

                def hg_wave2(h):
                    par, fo, Fs, Fks, Bs, Bks, smo = hg_names(h)
                    E, L1, L2, NZ, EQ, QC, BT, EG = Fs
                    kE, kL1, kL2, kNZ, kEQ, kQC, kBT, kEG = Fks
                    Kp, kKp = Bs[1], Bks[1]
                    smk = ("sm", par)
                    lnomc = DV[:, DVO["lnom"] + l * 8 + h:DVO["lnom"] + l * 8 + h + 1]
                    BT3 = BT.rearrange("p (c t) -> p c t", t=128)
                    act(EQ, EQ, AF.Exp, [kEQ], [kEQ])
                    act(Kp, NZ, AF.Exp, [kNZ, "DV"], [kKp], scale=-1.0, bias=lnomc)
                    act(sm[:, smo + 4:smo + 8], sm[:, smo + 0:smo + 4], AF.Exp, [smk], [smk], scale=-1.0)
                    act(sm[:, smo + 12:smo + 16], BT3[:, :, 127], AF.Exp, [kBT], [smk])
                    act(sm[:, smo + 8:smo + 12], sm[:, smo + 24:smo + 28], AF.Exp, [smk], [smk])

                def hg_post(h):
                    par, fo, Fs, Fks, Bs, Bks, smo = hg_names(h)
                    E, L1, L2, NZ, EQ, QC, BT, EG = Fs
                    kE, kL1, kL2, kNZ, kEQ, kQC, kBT, kEG = Fks
                    Qp, Kp, KT, V, SG, SCM, OG, GC = Bs
                    kQp, kKp, kKT, kV, kSG, kSCM, kOG, kGC = Bks
                    smk = ("sm", par)
                    ebm = sm[:, smo + 4:smo + 8]
                    dl = sm[:, smo + 8:smo + 12]
                    dlm = sm[:, smo + 12:smo + 16]
                    ss = sm[:, smo + 16:smo + 20]
                    rs = sm[:, smo + 20:smo + 24]
                    V3 = V.rearrange("p (s e) -> p s e", e=128)
                    SG3 = SG.rearrange("p (s e) -> p s e", e=128)
                    tt(DVE, Qp, QC, EQ, ALU.mult, [kQC, kEQ], [kQp])
                    p4b = ps[4][:].bitcast(BF16)
                    for c in range(4):
                        tr(p4b[:, c * 128:(c + 1) * 128], Kp[:, c * 128:(c + 1) * 128], identb[:], [kKp, "identb"], [PK(4)])
                    act(KT, p4b[:, 0:512], AF.Identity, [PK(4)], [kKT])
                    KT3 = KT.rearrange("p (c d) -> p c d", d=128)
                    for c in range(4):
                        c0 = c * 128
                        mm(ps[5][0:64, c0:c0 + 128], Kp[:, c0:c0 + 64], Qp[:, c0:c0 + 128], True, True, [kKp, kQp], [PK(5)])
                        mm(ps[5][64:128, c0 + 64:c0 + 128], Kp[:, c0 + 64:c0 + 128], Qp[:, c0 + 64:c0 + 128], True, True, [kKp, kQp], [PK(5)])
                    S.op(POOL, lambda e, SCM=SCM: e.memset(SCM, 0.0), [], [kSCM])
                    S.op(DVE, lambda e, SCM=SCM: e.copy_predicated(out=SCM, mask=maski[:].rearrange("p c i -> p (c i)"), data=ps[5][:]), [PK(5), "maski", kSCM], [kSCM])
                    SCM3 = SCM.rearrange("p (c i) -> p c i", i=128)
                    for c in range(4):
                        mm(ps[6][:, c * 128:(c + 1) * 128], KT3[:, c, :], V3[:, c, :], True, True, [kKT, kV], [PK(6)])
                    Sst = SH[:, l * 8 + h, :]
                    kS = ("SH", l * 8 + h)
                    for c in range(4):
                        ts1(DVE, HSB[:, c, :], Sst, ebm[:, c:c + 1], ALU.mult, [kS, smk], [("HSB", c)])
                        dst = FA[:, fo + 7, c * 128:(c + 1) * 128]
                        ts1(DVE, dst, ps[6][:, c * 128:(c + 1) * 128], dlm[:, c:c + 1], ALU.mult, [PK(6), smk, kEG], [kEG])
                        stt(Sst, Sst, dl[:, c:c + 1], dst, ALU.mult, ALU.add, [kS, smk, kEG], [kS])
                    for c in range(4):
                        mm(ps[7][:, c * 128:(c + 1) * 128], SCM3[:, c, :], V3[:, c, :], True, False, [kSCM, kV], [PK(7)])
                        mm(ps[7][:, c * 128:(c + 1) * 128], Qp[:, c * 128:(c + 1) * 128], HSB[:, c, :], False, True, [kQp, ("HSB", c)], [PK(7)])
                    act(E, ps[7][:], AF.Square, [PK(7)], [kE])
                    S.op(DVE, lambda e, ss=ss, E=E: e.tensor_reduce(out=ss, in_=E.rearrange("p (c t) -> p c t", t=128), axis=AX.X, op=ALU.add), [kE], [smk])
                    act(rs, ss, AF.Ln, [smk], [smk], scale=1.0 / 128.0, bias=HN_EPS)
                    act(rs, rs, AF.Exp, [smk], [smk], scale=-0.5)
                    OG3 = OG.rearrange("p (c e) -> p c e", e=128)
                    for c in range(4):
                        stt(OG3[:, c, :], ps[7][:, c * 128:(c + 1) * 128], rs[:, c:c + 1], SG3[:, c, :], ALU.mult, ALU.mult, [PK(7), smk, kSG], [kOG])
                    for c in range(4):
                        tr(p4b[:, c * 128:(c + 1) * 128], OG3[:, c, :], identb[:], [kOG, "identb"], [PK(4)])
                    act(OGF[:, h, :], p4b[:, 0:512], AF.Identity, [PK(4)], [("big", h)])

                hbanks = hg_proj(0)
                hg_wave1(0, hbanks)
                for h in range(8):
                    if h + 1 < 8:
                        hbanks = hg_proj(h + 1)
                    hg_mid(h)
                    if h + 1 < 8:
                        hg_wave1(h + 1, hbanks)
                    hg_wave2(h)
                    hg_post(h)

                for h in range(4):
                    gam = GAMMAS[h]
                    fo = 0 if h % 2 == 0 else 8
                    sidx = (l * 4 + h) * 2
                    kSR = [("SR", sidx), ("SR", sidx + 1)]
                    for dh in range(2):
                        act(RSB[:, 0, dh, :], SR[:, sidx + dh, :], AF.Identity, [kSR[dh]], [("RSB", 0, dh)])
                    wvA, wkA = next_unit(ub + 8 + h * 3)
                    QR = [B(14), B(15)]
                    KR = [B(16), B(17)]
                    kQR = [Bk(14), Bk(15)]
                    kKR = [Bk(16), Bk(17)]
                    for qk in range(2):
                        p1, p2 = nextproj(), nextproj()
                        for half, pb in ((0, p1), (1, p2)):
                            cb = qk * 256 + half * 128
                            for kc in range(8):
                                mm(ps[pb][:], wvA[:, kc, cb:cb + 128], u[:, kc, :], kc == 0, kc == 7, [wkA, ("u", kc)], [PK(pb)])
                        t1, t2_, t3, t4 = F(fo + 0), F(fo + 1), F(fo + 2), F(fo + 3)
                        k1, k2, k3, k4 = Fk(fo + 0), Fk(fo + 1), Fk(fo + 2), Fk(fo + 3)
                        R = QR if qk == 0 else KR
                        kR = kQR if qk == 0 else kKR
                        tt(DVE, t1, ps[p1][:], cosT, ALU.mult, [PK(p1), ("cs", 0)], [k1])
                        tt(DVE, t2_, ps[p2][:], sinT, ALU.mult, [PK(p2), ("cs", 1)], [k2])
                        tt(POOL, R[0], t1, t2_, ALU.subtract, [k1, k2], [kR[0]])
                        tt(DVE, t3, ps[p1][:], sinT, ALU.mult, [PK(p1), ("cs", 1)], [k3])
                        tt(DVE, t4, ps[p2][:], cosT, ALU.mult, [PK(p2), ("cs", 0)], [k4])
                        tt(POOL, R[1], t3, t4, ALU.add, [k3, k4], [kR[1]])
                    wvB, wkB = next_unit(ub + 8 + h * 3 + 1)
                    Vr = BA[:, 0:4, :]
                    SRG = BA[:, 4:8, :]
                    for s_ in range(NSUB):
                        pb = nextproj()
                        for kc in range(8):
                            mm(ps[pb][:], u[:, kc, s_ * 128:(s_ + 1) * 128], wvB[:, kc, :], kc == 0, kc == 7, [wkB, ("u", kc)], [PK(pb)])
                        cp(DVE, Vr[:, s_, :], ps[pb][:], [PK(pb)], [Bk(s_)])
                    wvC, wkC = next_unit(ub + 8 + h * 3 + 2)
                    for s_ in range(NSUB):
                        pb = nextproj()
                        for kc in range(8):
                            mm(ps[pb][:], u[:, kc, s_ * 128:(s_ + 1) * 128], wvC[:, kc, :], kc == 0, kc == 7, [wkC, ("u", kc)], [PK(pb)])
                        eg = F(fo + 4 + s_)
                        keg = Fk(fo + 4 + s_)
                        act(eg, ps[pb][:], AF.Exp, [PK(pb)], [keg], scale=-1.0)
                        act(eg, eg, AF.Ln, [keg], [keg], bias=1.0)
                        act(eg, eg, AF.Exp, [keg], [keg], scale=-1.0)
                        tt(DVE, SRG[:, s_, :], ps[pb][:], eg, ALU.mult, [PK(pb), keg], [Bk(4 + s_)])
                    KTr = BA[:, 8:10, :].rearrange("p a t -> p (a t)").rearrange("p (c d) -> p c d", d=256)
                    kKTr = [Bk(8), Bk(9)]
                    p4b = ps[4][:].bitcast(BF16).rearrange("p (c d) -> p c d", d=256)
                    for c in range(4):
                        for dh in range(2):
                            tr(p4b[:, c, dh * 128:(dh + 1) * 128], KR[dh][:, c * 128:(c + 1) * 128], identb[:], [kKR[dh], "identb"], [PK(4)])
                    for c in range(4):
                        act(KTr[:, c, :], p4b[:, c, :], AF.Identity, [PK(4), "CST"], kKTr, scale=CST[:, C_KDEC + h * 4 + c:C_KDEC + h * 4 + c + 1])
                    sbank = {0: (5, 0), 1: (6, 0), 2: (7, 0), 3: (7, 256)}
                    for cq in range(4):
                        bk_, o0 = sbank[cq]
                        for c in range(cq, 4):
                            oc = o0 + (c - cq) * 128
                            for dh in range(2):
                                mm(ps[bk_][:, oc:oc + 128], KR[dh][:, cq * 128:(cq + 1) * 128], QR[dh][:, c * 128:(c + 1) * 128], dh == 0, dh == 1,
                                   [kKR[dh], kQR[dh]], [PK(bk_)])
                    SCB = BA[:, 10:13, :].rearrange("p a t -> p (a t)")
                    kSCB = [Bk(10), Bk(11), Bk(12)]
                    bi0 = {0: 0, 1: 4, 2: 7, 3: 9}
                    Mh_ = CST[:, C_MH + h * 128:C_MH + (h + 1) * 128]
                    for cq in range(4):
                        bk_, o0 = sbank[cq]
                        b0 = bi0[cq] * 128
                        stt(SCB[:, b0:b0 + 128], ps[bk_][:, o0:o0 + 128], float(gam ** (-(cq * 128.0))), Mh_, ALU.mult, ALU.mult, [PK(bk_), "CST"], kSCB)
                        nof = 3 - cq
                        if nof > 0:
                            ts1(DVE, SCB[:, b0 + 128:b0 + 128 + nof * 128], ps[bk_][:, o0 + 128:o0 + 128 + nof * 128],
                                CST[:, C_KSC + h * 4 + cq:C_KSC + h * 4 + cq + 1], ALU.mult, [PK(bk_), "CST"], kSCB)
                    for dh in range(2):
                        for c in range(4):
                            mm(ps[5 + dh][:], KTr[:, c, dh * 128:(dh + 1) * 128], Vr[:, c, :], c == 0, c == 3, kKTr + [Bk(c)], [PK(5 + dh)])
                    for dh in range(2):
                        stt(SR[:, sidx + dh, :], SR[:, sidx + dh, :], float(gam ** 512.0), ps[5 + dh][:], ALU.mult, ALU.add, [kSR[dh], PK(5 + dh)], [kSR[dh]])
                    stat6 = sm[:, 96:120].rearrange("p (c s) -> p c s", s=6)
                    mv = sm[:, 64:72].rearrange("p (c s) -> p c s", s=2)
                    smr = ("sm", 2)
                    OB = [7, 0, 1, 2]
                    for c in range(4):
                        po = OB[c]
                        for cq in range(c + 1):
                            b0 = (bi0[cq] + (c - cq)) * 128
                            mm(ps[po][:], SCB[:, b0:b0 + 128], Vr[:, cq, :], cq == 0, False, kSCB + [Bk(cq)], [PK(po)])
                        for dh in range(2):
                            mm(ps[po][:], QR[dh][:, c * 128:(c + 1) * 128], RSB[:, 0, dh, :], False, dh == 1, [kQR[dh], ("RSB", 0, dh)], [PK(po)])
                        S.op(DVE, lambda e, c=c, po=po: e.bn_stats(out=stat6[:, c, :], in_=ps[po][:]), [PK(po)], [smr])
                        S.op(DVE, lambda e, c=c: e.bn_aggr(out=mv[:, c, :], in_=stat6[:, c, :]), [smr], [smr])
                    rsp4 = sm[:, 72:76]
                    nb4 = sm[:, 80:84]
                    tt(DVE, rsp4, mv[:, :, 1], CST[:, C_QD2 + h * 4:C_QD2 + h * 4 + 4], ALU.mult, [smr, "CST"], [smr])
                    act(rsp4, rsp4, AF.Ln, [smr], [smr], bias=HN_EPS)
                    act(rsp4, rsp4, AF.Exp, [smr], [smr], scale=-0.5)
                    tt(DVE, rsp4, rsp4, CST[:, C_QD + h * 4:C_QD + h * 4 + 4], ALU.mult, [smr, "CST"], [smr])
                    stt(nb4, mv[:, :, 0], -1.0, rsp4, ALU.mult, ALU.mult, [smr], [smr])
                    for c in range(4):
                        po = OB[c]
                        ON = FA[:, fo + 4 + (c % 2), :].bitcast(BF16)[:, 0:512]
                        kON = Fk(fo + 4 + (c % 2))
                        act(ON, ps[po][:], AF.Identity, [PK(po), smr], [kON], scale=rsp4[:, c:c + 1], bias=nb4[:, c:c + 1])
                        OGr = B(18 + (c % 2))
                        kOGr = Bk(18 + (c % 2))
                        tt(POOL, OGr, ON, SRG[:, c, :], ALU.mult, [kON, Bk(4 + c)], [kOGr])
                        pt = 4 if c % 2 == 0 else 3
                        ptb = ps[pt][:].bitcast(BF16)
                        for ec in range(4):
                            tr(ptb[:, ec * 128:(ec + 1) * 128], OGr[:, ec * 128:(ec + 1) * 128], identb[:], [kOGr, "identb"], [PK(pt)])
                        act(RGOF[:, h * 4:(h + 1) * 4, c * 128:(c + 1) * 128], ptb[:, 0:512].rearrange("p (a i) -> p a i", i=128), AF.Identity,
                            [PK(pt)], [("big", 8 + h * 4 + ec) for ec in range(4)])

                gate_sb = {}
                for gi in range(4):
                    wv, wk = next_unit(ub + 20 + gi)
                    for blk in range(4):
                        m = (gi % 2) * 4 + blk
                        pb = nextproj()
                        for kc in range(8):
                            mm(ps[pb][:], wv[:, kc, blk * 128:(blk + 1) * 128], u[:, kc, :], kc == 0, kc == 7, [wk, ("u", kc)], [PK(pb)])
                        si = (gi // 2) * 8 + m
                        g_ = F(si)
                        act(g_, ps[pb][:], AF.Exp, [PK(pb)], [Fk(si)], scale=-1.0)
                        act(g_, g_, AF.Ln, [Fk(si)], [Fk(si)], bias=1.0)
                        act(g_, g_, AF.Exp, [Fk(si)], [Fk(si)], scale=-1.0)
                        gate_sb[(gi // 2, m)] = si
                for ch in range(2):
                    wv, wk = next_unit(ub + 24 + ch)
                    for blk in range(4):
                        m = ch * 4 + blk
                        pb = nextproj()
                        for kc in range(8):
                            mm(ps[pb][:], wv[:, kc, blk * 128:(blk + 1) * 128], OGF[:, kc, :], kc == 0, kc == 7, [wk, ("big", kc)], [PK(pb)])
                        si = gate_sb[(0, m)]
                        tt(DVE, F(si), ps[pb][:], F(si), ALU.mult, [PK(pb), Fk(si)], [Fk(si)])
                for ch in range(2):
                    units = [next_unit(ub + 26 + ch * 2 + kh) for kh in range(2)]
                    pbs = [nextproj() for _ in range(4)]
                    for kh in range(2):
                        wv, wk = units[kh]
                        for blk in range(4):
                            for kc in range(8):
                                kk = kh * 8 + kc
                                mm(ps[pbs[blk]][:], wv[:, kc, blk * 128:(blk + 1) * 128], RGOF[:, kk, :], kk == 0, kk == 15, [wk, ("big", 8 + kk)], [PK(pbs[blk])])
                    for blk in range(4):
                        m = ch * 4 + blk
                        sa, sb_ = gate_sb[(0, m)], gate_sb[(1, m)]
                        tt(DVE, F(sb_), ps[pbs[blk]][:], F(sb_), ALU.mult, [PK(pbs[blk]), Fk(sb_)], [Fk(sb_)])
                        tt(POOL, sbuf_s[:, m, :], F(sa), F(sb_), ALU.add, [Fk(sa), Fk(sb_)], [("big", 24 + m)])
                if t == 0:
                    dump("d_big%d" % l, big[:], bigkeys, [128, 32, T], BF16)
                G1h = DVs(("G1h", l))
                for ch in range(2):
                    wv, wk = next_unit(ub + 30 + ch)
                    for blk in range(4):
                        m = ch * 4 + blk
                        pb = nextproj()
                        for kc in range(8):
                            mm(ps[pb][:], wv[:, kc, blk * 128:(blk + 1) * 128], sbuf_s[:, kc, :], kc == 0, kc == 7, [wk, ("big", 24 + kc)], [PK(pb)])
                        stt(xa[:, m, :], ps[pb][:], G1h[:, m:m + 1], xa[:, m, :], ALU.mult, ALU.add, [PK(pb), "DV", ("xa", m)], [("xa", m)])
                        ln_accum(m)
                ln_finish(DVs(("Ax1", l)), DVs(("Bx1", l)), DVs(("Au1", l)), DVs(("Bu1", l)))

                if t == 0:
                    dump("d_xa1_%d" % l, xa[:], xakeys, [128, 8, T], F32)
                    dump("d_u1_%d" % l, u[:], ukeys, [128, 8, T], BF16)
                bup = Vs("b_up", l, 32)
                for un in range(8):
                    wv, wk = next_unit(ub + 32 + un)
                    for blk in range(4):
                        j = un * 4 + blk
                        pb = nextproj()
                        for kc in range(8):
                            mm(ps[pb][:], wv[:, kc, blk * 128:(blk + 1) * 128], u[:, kc, :], kc == 0, kc == 7, [wk, ("u", kc)], [PK(pb)])
                        hr = F(j % 4)
                        act(hr, ps[pb][:], AF.Relu, [PK(pb), "VEC"], [Fk(j % 4)], bias=bup[:, j:j + 1])
                        tt(DVE if j % 2 == 0 else POOL, big[:, j, :], hr, hr, ALU.mult, [Fk(j % 4)], [("big", j)])
                G2 = DVs(("G2", l))
                for ch in range(2):
                    units = [next_unit(ub + 40 + ch * 4 + kq) for kq in range(4)]
                    pbs = [0, 1, 2, 3] if ch == 0 else [6, 7, 0, 1]
                    for kq in range(4):
                        wv, wk = units[kq]
                        for blk in range(4):
                            for kc in range(8):
                                kk = kq * 8 + kc
                                mm(ps[pbs[blk]][:], wv[:, kc, blk * 128:(blk + 1) * 128], big[:, kk, :], kk == 0, kk == 31, [wk, ("big", kk)], [PK(pbs[blk])])
                    for blk in range(4):
                        m = ch * 4 + blk
                        stt(xa[:, m, :], ps[pbs[blk]][:], G2[:, m:m + 1], xa[:, m, :], ALU.mult, ALU.add, [PK(pbs[blk]), "DV", ("xa", m)], [("xa", m)])
                        ln_accum(m)
                last = (l == NL - 1)
                ln_finish(DVs(("Ax2", l)), DVs(("Bx2", l)), None if last else DVs(("Au2", l)), None if last else DVs(("Bu2", l)))

            for s_ in range(NSUB):
                st_ = FA[:, (s_ % 2) * 2:(s_ % 2) * 2 + 2, :].rearrange("p a t -> p (a t)")
                stk = [Fk((s_ % 2) * 2), Fk((s_ % 2) * 2 + 1)]
                for jh in range(2):
                    pb = nextproj()
                    for jj in range(4):
                        j = jh * 4 + jj
                        tr(ps[pb][:, jj * 128:(jj + 1) * 128], xa[:, j, s_ * 128:(s_ + 1) * 128], identf, [("xa", j), "CST"], [PK(pb)])
                    act(st_[:, jh * 512:(jh + 1) * 512], ps[pb][:], AF.Identity, [PK(pb)], [stk[jh]])
                dma(POOL, out_d[t0 + s_ * 128:t0 + (s_ + 1) * 128, :], st_, stk, [], "ost%d" % (s_ % 2))

        fin = S.op(POOL, lambda e: e.nop(), [], [])
        for ch in list(S.dma_last):
            if S.dma_last[ch] is not fin:
                fin.deps.append(S.dma_last[ch])

        S.finalize()
        engsem = {e: es.enter_context(nc.semaphore("sem_" + e)) for e in ENGS}
        dmasem = {ch: es.enter_context(nc.semaphore("dsem_" + ch)) for ch in S.dma_count}
        with nc.Block() as block:
            @block.sync
            def _(e):
                S.emit(e, SP, engsem, dmasem)

            @block.tensor
            def _(e):
                S.emit(e, PE, engsem, dmasem)

            @block.scalar
            def _(e):
                S.emit(e, ACT, engsem, dmasem)

            @block.vector
            def _(e):
                S.emit(e, DVE, engsem, dmasem)

            @block.gpsimd
            def _(e):
                S.emit(e, POOL, engsem, dmasem)
    return nc


def _unit(Wm, rows, cols):
    sub = Wm[rows][:, cols]
    return np.ascontiguousarray(sub.reshape(8, 128, 512).transpose(1, 0, 2)).reshape(128, 4096)


def _pack_weights(w_in, w_pa, w_pb, w_o, w_up, w_down):
    units = []
    r1024 = np.arange(1024)
    for l in range(DEPTH):
        Wi = w_in[l]
        for h in range(8):
            cols = np.concatenate([0 + h * 128 + np.arange(128), 1024 + h * 128 + np.arange(128),
                                   2048 + h * 128 + np.arange(128), 3072 + h * 128 + np.arange(128)])
            units.append(_unit(Wi, r1024, cols))
        for h in range(4):
            qb = 4096 + h * 256
            kb = 5120 + h * 256
            ev = np.arange(0, 256, 2)
            od = np.arange(1, 256, 2)
            units.append(_unit(Wi, r1024, np.concatenate([qb + ev, qb + od, kb + ev, kb + od])))
            units.append(_unit(Wi, r1024, 6144 + h * 512 + np.arange(512)))
            units.append(_unit(Wi, r1024, 8192 + h * 512 + np.arange(512)))
        for gi in range(4):
            units.append(_unit(Wi, r1024, 10240 + gi * 512 + np.arange(512)))
        for ch in range(2):
            units.append(_unit(w_pa[l], r1024, ch * 512 + np.arange(512)))
        for ch in range(2):
            for kh in range(2):
                units.append(_unit(w_pb[l], kh * 1024 + r1024, ch * 512 + np.arange(512)))
        for ch in range(2):
            units.append(_unit(w_o[l], r1024, ch * 512 + np.arange(512)))
        for un in range(8):
            units.append(_unit(w_up[l], r1024, un * 512 + np.arange(512)))
        for ch in range(2):
            for kq in range(4):
                units.append(_unit(w_down[l], kq * 1024 + r1024, ch * 512 + np.arange(512)))
    return np.stack(units, axis=0)


def _fm(v, n):
    return np.ascontiguousarray(np.asarray(v, np.float32).reshape(n, 128).T)


def _consts():
    C = np.zeros((128, NCONST), np.float32)
    C[:, C_IDENT:C_IDENT + 128] = np.eye(128, dtype=np.float32)
    j = np.arange(128)[:, None]
    i = np.arange(128)[None, :]
    C[:, C_MASKT:C_MASKT + 128] = (j <= i).astype(np.float32)
    jj = np.arange(128, dtype=np.float64)
    for h in range(4):
        g = np.float64(GAMMAS[h])
        C[:, C_MH + h * 128:C_MH + (h + 1) * 128] = ((j <= i) * (g ** (-(j + 1.0))) / 16.0).astype(np.float32)
        for c in range(4):
            C[:, C_KSC + h * 4 + c] = (g ** (-(c * 128.0 + jj + 1.0)) / 16.0).astype(np.float32)
            C[:, C_KDEC + h * 4 + c] = (g ** (511.0 - c * 128.0 - jj) / 16.0).astype(np.float32)
            C[:, C_QD + h * 4 + c] = (g ** (c * 128.0 + jj + 1.0)).astype(np.float32)
            C[:, C_QD2 + h * 4 + c] = (g ** (2.0 * (c * 128.0 + jj + 1.0))).astype(np.float32)
    sc = np.ones(512, np.float32)
    sc[::128] = 0.0
    C[:, C_SCANM:C_SCANM + 512] = sc[None, :]
    C[:, C_THETA] = (10000.0 ** (-np.linspace(0.0, 1.0, 128, dtype=np.float32))).astype(np.float32)
    return C


_NC_CACHE = {}


def kernel(x, c, positions, lb_logits, w_ada, b_ada, w_in, w_pa, w_pb, w_o,
           ln1_g, ln1_b, w_up, b_up, w_down, b_down, ln2_g, ln2_b, _NT=8, _cores=8, _dbg=False):
    x = np.asarray(x, np.float32)
    wall = _pack_weights(*[np.asarray(a, np.float32) for a in (w_in, w_pa, w_pb, w_o, w_up, w_down)])
    w_ada = np.asarray(w_ada, np.float32)
    wada = np.stack([np.ascontiguousarray(w_ada[l][:, g * 1024:(g + 1) * 1024].reshape(8, 128, 1024).transpose(1, 0, 2)).reshape(128, 8192)
                     for l in range(DEPTH) for g in range(6)], axis=0)
    consts = _consts()
    in_maps = []
    for b in range(_cores):
        V = np.zeros((128, NVEC), np.float32)
        for l in range(DEPTH):
            for nm, arr, n in (("ln1_g", ln1_g, 8), ("ln1_b", ln1_b, 8), ("ln2_g", ln2_g, 8), ("ln2_b", ln2_b, 8),
                               ("b_down", b_down, 8), ("b_up", b_up, 32), ("b_ada", b_ada, 48), ("lbl", lb_logits, 8)):
                V[:, VOFF[(nm, l)]:VOFF[(nm, l)] + n] = _fm(np.asarray(arr)[l], n)
        V[:, VOFF["c"]:VOFF["c"] + 8] = _fm(np.asarray(c)[b], 8)
        pos = np.ascontiguousarray(np.broadcast_to(np.asarray(positions)[b].astype(np.int32)[None, :], (128, SEQ)))
        in_maps.append({"x": np.ascontiguousarray(x[b]), "pos": pos, "wall": wall, "wada": wada, "vecs": V, "consts": consts})
    key = (_NT, _dbg)
    if key not in _NC_CACHE:
        _NC_CACHE[key] = build_nc(NT=_NT, dbg=_dbg)
    nc = _NC_CACHE[key]
    res = run_bass_kernel_spmd(nc, in_maps, core_ids=list(range(_cores)))
    out = np.stack([np.asarray(r["out"], np.float32) for r in res.results], axis=0)
    if _dbg:
        return out, res.results[0]
    if _cores < 8:
        return out
    return out.reshape(8, SEQ, D)
```

```python
import math
from contextlib import ExitStack
import numpy as np
import concourse.bass as bass
import concourse.mybir as mybir
from concourse.bass_utils import run_bass_kernel_spmd

F32 = mybir.dt.float32
BF16 = mybir.dt.bfloat16
I32 = mybir.dt.int32
AF = mybir.ActivationFunctionType
ALU = mybir.AluOpType
AX = mybir.AxisListType

PE, ACT, DVE, POOL, SP = "tensor", "scalar", "vector", "gpsimd", "sync"
ENGS = (PE, ACT, DVE, POOL, SP)

D = 1024
SEQ = 4096
T = 512
NSUB = 4
DEPTH = 2
ALPHA = (2 * DEPTH) ** 0.25
LN_EPS = 1e-5
HN_EPS = 1e-6
UPL = 48
GAMMAS = [1.0 - 2.0 ** (-5.0 - h) for h in range(4)]


class Op:
    __slots__ = ("eng", "fn", "deps", "need_sig", "sigval", "dma", "dma_val", "waits", "wkeys", "gidx")

    def __init__(self, eng, fn):
        self.eng = eng
        self.fn = fn
        self.deps = []
        self.need_sig = False
        self.sigval = 0
        self.dma = None
        self.dma_val = 0
        self.waits = {}
        self.wkeys = frozenset()


class Sched:
    def __init__(self):
        self.streams = {e: [] for e in ENGS}
        self.lastw = {}
        self.readers = {}
        self.dma_count = {}
        self.dma_last = {}

    def op(self, eng, fn, reads=(), writes=(), dma=None):
        o = Op(eng, fn)
        reads = tuple(reads)
        writes = tuple(writes)
        deps = set()
        for k in reads:
            w = self.lastw.get(k)
            if w is not None:
                deps.add(w)
        for k in writes:
            w = self.lastw.get(k)
            if w is not None:
                deps.add(w)
            for r in self.readers.get(k, ()):
                deps.add(r)
        if dma is not None:
            prev = self.dma_last.get(dma)
            if prev is not None:
                deps.add(prev)
            self.dma_last[dma] = o
            cnt = self.dma_count.get(dma, 0) + 1
            self.dma_count[dma] = cnt
            o.dma = dma
            o.dma_val = 16 * cnt
        touched = set(reads) | set(writes)
        for d in deps:
            if d is o:
                continue
            if d.dma is None and d.eng == eng:
                if eng == PE:
                    continue
                if not (d.wkeys & touched):
                    continue
            if d.dma is None:
                d.need_sig = True
            o.deps.append(d)
        for k in reads:
            self.readers.setdefault(k, []).append(o)
        for k in writes:
            self.lastw[k] = o
            self.readers[k] = []
        o.wkeys = frozenset(writes)
        self.streams[eng].append(o)
        self.gcount = getattr(self, "gcount", 0) + 1
        o.gidx = self.gcount
        return o

    def finalize(self):
        for e, st in self.streams.items():
            c = 0
            for o in st:
                if o.dma is None and o.need_sig:
                    c += 1
                    o.sigval = c
        for e, st in self.streams.items():
            known = {}
            for o in st:
                w = {}
                for d in o.deps:
                    if d.dma is not None:
                        key = ("dma", d.dma)
                        val = d.dma_val
                    else:
                        key = ("eng", d.eng)
                        val = d.sigval
                    if val > known.get(key, 0) and val > w.get(key, 0):
                        w[key] = val
                for k, v in w.items():
                    known[k] = v
                o.waits = w

    def emit(self, engobj, eng, engsem, dmasem):
        for o in self.streams[eng]:
            for (kind, name), v in o.waits.items():
                s = engsem[name] if kind == "eng" else dmasem[name]
                engobj.wait_ge(s, v)
            ins = o.fn(engobj)
            if o.dma is not None:
                ins.then_inc(dmasem[o.dma], 16)
            elif o.need_sig:
                ins.then_inc(engsem[o.eng], 1)


def _vec_layout():
    off = {}
    c = 0
    for l in range(DEPTH):
        for nm, n in (("ln1_g", 8), ("ln1_b", 8), ("ln2_g", 8), ("ln2_b", 8), ("b_down", 8), ("b_up", 32), ("b_ada", 48), ("lbl", 8)):
            off[(nm, l)] = c
            c += n
    off["c"] = c
    c += 8
    return off, c


VOFF, NVEC = _vec_layout()
C_IDENT, C_MASKT, C_MH, C_SCANM, C_KSC, C_KDEC, C_QD, C_QD2, C_THETA = 0, 128, 256, 768, 1280, 1296, 1312, 1328, 1344
NCONST = 1346


def build_nc(NT=8, NL=DEPTH, dbg=False):
    nc = bass.Bass("TRN2", target_bir_lowering=False)
    x_d = nc.dram_tensor("x", [SEQ, D], F32, kind="ExternalInput").ap()
    pos_d = nc.dram_tensor("pos", [128, SEQ], I32, kind="ExternalInput").ap()
    wall_d = nc.dram_tensor("wall", [DEPTH * UPL, 128, 4096], F32, kind="ExternalInput").ap()
    wada_d = nc.dram_tensor("wada", [DEPTH * 6, 128, 8192], F32, kind="ExternalInput").ap()
    vec_d = nc.dram_tensor("vecs", [128, NVEC], F32, kind="ExternalInput").ap()
    cst_d = nc.dram_tensor("consts", [128, NCONST], F32, kind="ExternalInput").ap()
    out_d = nc.dram_tensor("out", [SEQ, D], F32, kind="ExternalOutput").ap()
    wscr = nc.dram_tensor("wscr", [DEPTH * UPL, 128, 4096], BF16, kind="Internal").ap()

    S = Sched()
    es = ExitStack()
    with es:
        def sb(name, shape, dt):
            return es.enter_context(nc.sbuf_tensor(name, shape, dt))

        xa = sb("xa", [128, 8, T], F32)
        u = sb("u", [128, 8, T], BF16)
        big = sb("big", [128, 32, T], BF16)
        ring = sb("ring", [128, 4, 4096], BF16)
        FA = sb("FA", [128, 16, T], F32)
        BA = sb("BA", [128, 20, T], BF16)
        lnb = sb("lnb", [128, 4, T], BF16)
        SH = sb("SH", [128, DEPTH * 8, 128], F32)
        SR = sb("SR", [128, DEPTH * 4 * 2, 512], F32)
        HSB = sb("HSB", [128, 4, 128], BF16)
        RSB = sb("RSB", [128, 2, 2, 512], BF16)
        cs = sb("cs", [128, 2, T], F32)
        post = sb("post", [128, T], I32)
        VEC = sb("VEC", [128, NVEC], F32)
        CST = sb("CST", [128, NCONST], F32)
        DV = sb("DV", [128, 512], F32)
        identb = sb("identb", [128, 128], BF16)
        onesb = sb("onesb", [128, 128], BF16)
        maski = sb("maski", [128, 4, 128], I32)
        sm = sb("sm", [128, 128], F32)

        ps = [es.enter_context(nc.psum_tensor("ps%d" % i, [128, 512], F32)) for i in range(8)]

        def F(i):
            return FA[:, i, :]

        def Fk(i):
            return ("F", i)

        def B(i):
            return BA[:, i, :]

        def Bk(i):
            return ("B", i)

        pj = [0]

        def nextproj():
            i = pj[0] % 4
            pj[0] += 1
            return i

        def PK(i):
            return ("ps", i)

        def mm(out, lhsT, rhs, start, stop, reads, writes):
            S.op(PE, lambda e: e.matmul(out, lhsT=lhsT, rhs=rhs, start=start, stop=stop), reads, writes)

        def tr(out, in_, ident, reads, writes):
            S.op(PE, lambda e: e.transpose(out=out, in_=in_, identity=ident), reads, writes)

        def act(out, in_, func, reads, writes, scale=1.0, bias=0.0):
            S.op(ACT, lambda e: e.activation(out=out, in_=in_, func=func, bias=bias, scale=scale), reads, writes)

        def tt(eng, out, in0, in1, op, reads, writes):
            S.op(eng, lambda e: e.tensor_tensor(out=out, in0=in0, in1=in1, op=op), reads, writes)

        def ts(eng, out, in0, s1, s2, op0, op1, reads, writes):
            S.op(eng, lambda e: e.tensor_scalar(out=out, in0=in0, scalar1=s1, scalar2=s2, op0=op0, op1=op1), reads, writes)

        def ts1(eng, out, in0, s1, op0, reads, writes):
            S.op(eng, lambda e: e.tensor_scalar(out=out, in0=in0, scalar1=s1, scalar2=None, op0=op0), reads, writes)

        def stt(out, in0, scalar, in1, op0, op1, reads, writes):
            S.op(DVE, lambda e: e.scalar_tensor_tensor(out=out, in0=in0, scalar=scalar, in1=in1, op0=op0, op1=op1), reads, writes)

        def cp(eng, out, in_, reads, writes):
            S.op(eng, lambda e: e.tensor_copy(out=out, in_=in_), reads, writes)

        def dma(eng, out, in_, reads, writes, ch, **kw):
            S.op(eng, lambda e: e.dma_start(out=out, in_=in_, **kw), reads, writes, dma=ch)

        def dump(name, ap, keys, shape, dt):
            if not dbg:
                return
            d = nc.dram_tensor(name, shape, dt, kind="ExternalOutput").ap()
            dma(POOL, d, ap, keys, [], "dbg")

        ucount = [0]

        CONV_AHEAD = 6
        conv_done = [0]

        def convert_upto(n):
            while conv_done[0] < min(n, NL * UPL):
                k = conv_done[0]
                dma(POOL, wscr[k], wall_d[k], [], [("wscr", k)], "wcv%d" % (k % 4), max_dma_last_dim=4096)
                conv_done[0] += 1

        def next_unit(uidx):
            convert_upto(uidx + 1 + CONV_AHEAD)
            slot = ucount[0] % 4
            ucount[0] += 1
            dma(SP, ring[:, slot, :], wscr[uidx], [("wscr", uidx)], [("ring", slot)], "ring%d" % slot)
            return ring[:, slot, :].rearrange("p (k n) -> p k n", n=512), ("ring", slot)

        dma(SP, VEC[:], vec_d, [], ["VEC"], "ldv")
        dma(SP, CST[:], cst_d, [], ["CST"], "ldc")
        convert_upto(CONV_AHEAD)

        cp(DVE, identb[:], CST[:, C_IDENT:C_IDENT + 128], ["CST"], ["identb"])
        S.op(DVE, lambda e: e.memset(onesb[:], 1.0 / 1024.0), [], ["onesb"])
        for c in range(4):
            cp(DVE, maski[:, c, :], CST[:, C_MASKT:C_MASKT + 128], ["CST"], ["maski"])
        S.op(DVE, lambda e: e.memset(SH[:], 0.0), [], [("SH", i) for i in range(DEPTH * 8)])
        S.op(DVE, lambda e: e.memset(SR[:], 0.0), [], [("SR", i) for i in range(DEPTH * 8)])
        identf = CST[:, C_IDENT:C_IDENT + 128]
        act(ps[5][:], CST[:, C_SCANM:C_SCANM + 512], AF.Identity, ["CST"], [PK(5)], scale=0.0)

        DVO = {}
        dvc = [0]

        def dv(name, n=8):
            DVO[name] = dvc[0]
            dvc[0] += n
            return DVO[name]

        def DVs(name, n=8):
            o = DVO[name]
            return DV[:, o:o + n]

        def Vs(name, l, n=8):
            o = VOFF[(name, l)]
            return VEC[:, o:o + n]

        dv("cond")
        dv("tmp")
        cfm = VEC[:, VOFF["c"]:VOFF["c"] + 8]
        act(DVs("tmp"), cfm, AF.Exp, ["VEC"], ["DV"], scale=-1.0)
        act(DVs("tmp"), DVs("tmp"), AF.Ln, ["DV"], ["DV"], bias=1.0)
        act(DVs("tmp"), DVs("tmp"), AF.Exp, ["DV"], ["DV"], scale=-1.0)
        tt(DVE, DVs("cond"), cfm, DVs("tmp"), ALU.mult, ["VEC", "DV"], ["DV"])
        bigf = big[:].rearrange("p a t -> p (a t)").bitcast(F32)
        bigkeys = [("big", i) for i in range(32)]
        faf = FA[:].rearrange("p a t -> p (a t)")
        fakeys = [Fk(i) for i in range(16)]
        for l in range(NL):
            dv(("mod", l), 48)
            for g in range(6):
                abuf, akeys = (bigf, bigkeys) if g % 2 == 0 else (faf, fakeys)
                dma(SP, abuf, wada_d[l * 6 + g], [], akeys, "ada%d" % (g % 2))
                wv = abuf.rearrange("p (k n) -> p k n", n=1024)
                for jb in range(8):
                    jj = g * 8 + jb
                    for kc in range(8):
                        mm(ps[0][:, jj:jj + 1], wv[:, kc, jb * 128:(jb + 1) * 128], DV[:, DVO["cond"] + kc:DVO["cond"] + kc + 1],
                           kc == 0, kc == 7, akeys + ["DV"], [PK(0)])
            tt(DVE, DVs(("mod", l), 48), ps[0][:, 0:48], Vs("b_ada", l, 48), ALU.add, [PK(0), "VEC"], ["DV"])

        def MOD(l, which):
            o = DVO[("mod", l)] + which * 8
            return DV[:, o:o + 8]

        dv("lb", 16)
        dv("lnom", 16)
        S.op(DVE, lambda e: e.memset(DVs("lb", 16), 0.0), [], ["DV"])
        if NL > 1:
            lb1 = DV[:, DVO["lb"] + 8:DVO["lb"] + 16]
            tt(DVE, DVs("tmp"), Vs("lbl", 1), Vs("lbl", 0), ALU.subtract, ["VEC"], ["DV"])
            act(DVs("tmp"), DVs("tmp"), AF.Exp, ["DV"], ["DV"], scale=-1.0)
            act(DVs("tmp"), DVs("tmp"), AF.Ln, ["DV"], ["DV"], bias=1.0)
            act(lb1, DVs("tmp"), AF.Exp, ["DV"], ["DV"], scale=-1.0)
        act(DVs("lnom", 16), DVs("lb", 16), AF.Ln, ["DV"], ["DV"], scale=-1.0, bias=1.0)

        dv("one_s", 8)
        for l in range(NL):
            for nm in ("G1h", "G2", "Ax1", "Bx1", "Au1", "Bu1", "Ax2", "Bx2", "Au2", "Bu2"):
                dv((nm, l))
        dv("A0u")
        dv("B0u")
        ts1(DVE, DVs("A0u"), MOD(0, 1), 1.0, ALU.add, ["DV"], ["DV"])
        cp(DVE, DVs("B0u"), MOD(0, 0), ["DV"], ["DV"])
        for l in range(NL):
            ts1(DVE, DVs(("G1h", l)), MOD(l, 2), 1.0, ALU.add, ["DV"], ["DV"])
            ts1(DVE, DVs(("G2", l)), MOD(l, 5), 1.0, ALU.add, ["DV"], ["DV"])
            ts1(DVE, DVs(("Ax1", l)), Vs("ln1_g", l), ALPHA, ALU.mult, ["VEC"], ["DV"])
            tt(DVE, DVs("tmp"), DVs(("G2", l)), Vs("b_down", l), ALU.mult, ["DV", "VEC"], ["DV"])
            stt(DVs(("Bx1", l)), Vs("ln1_b", l), ALPHA, DVs("tmp"), ALU.mult, ALU.add, ["VEC", "DV"], ["DV"])
            ts1(DVE, DVs("one_s"), MOD(l, 4), 1.0, ALU.add, ["DV"], ["DV"])
            tt(DVE, DVs(("Au1", l)), Vs("ln1_g", l), DVs("one_s"), ALU.mult, ["VEC", "DV"], ["DV"])
            tt(DVE, DVs("tmp"), Vs("ln1_b", l), DVs("one_s"), ALU.mult, ["VEC", "DV"], ["DV"])
            tt(DVE, DVs(("Bu1", l)), DVs("tmp"), MOD(l, 3), ALU.add, ["DV"], ["DV"])
            if l < NL - 1:
                ts1(DVE, DVs(("Ax2", l)), Vs("ln2_g", l), ALPHA, ALU.mult, ["VEC"], ["DV"])
                ts1(DVE, DVs(("Bx2", l)), Vs("ln2_b", l), ALPHA, ALU.mult, ["VEC"], ["DV"])
                ts1(DVE, DVs("one_s"), MOD(l + 1, 1), 1.0, ALU.add, ["DV"], ["DV"])
                tt(DVE, DVs(("Au2", l)), Vs("ln2_g", l), DVs("one_s"), ALU.mult, ["VEC", "DV"], ["DV"])
                tt(DVE, DVs("tmp"), Vs("ln2_b", l), DVs("one_s"), ALU.mult, ["VEC", "DV"], ["DV"])
                tt(DVE, DVs(("Bu2", l)), DVs("tmp"), MOD(l + 1, 0), ALU.add, ["DV"], ["DV"])
            else:
                cp(DVE, DVs(("Ax2", l)), Vs("ln2_g", l), ["VEC"], ["DV"])
                cp(DVE, DVs(("Bx2", l)), Vs("ln2_b", l), ["VEC"], ["DV"])
        assert dvc[0] <= 512

        dump("d_dv", DV[:], ["DV"], [128, 512], F32)
        xakeys = [("xa", m) for m in range(8)]
        ukeys = [("u", m) for m in range(8)]

        def ln_accum(m):
            rb = lnb[:, m % 2, :]
            rq = lnb[:, 2 + (m % 2), :]
            act(rb, xa[:, m, :], AF.Identity, [("xa", m)], [("lnb", m % 2)])
            act(rq, xa[:, m, :], AF.Square, [("xa", m)], [("lnb", 2 + m % 2)])
            mm(ps[4][:], onesb[:], rb, m == 0, m == 7, ["onesb", ("lnb", m % 2)], [PK(4)])
            mm(ps[5][:], onesb[:], rq, m == 0, m == 7, ["onesb", ("lnb", 2 + m % 2)], [PK(5)])

        def ln_finish(Ax, Bx, Au, Bu):
            mean_sb, msq, rstd = F(12), F(13), F(14)
            act(mean_sb, ps[4][:], AF.Identity, [PK(4)], [Fk(12)])
            tt(DVE, msq, mean_sb, mean_sb, ALU.mult, [Fk(12)], [Fk(13)])
            tt(DVE, msq, ps[5][:], msq, ALU.subtract, [PK(5), Fk(13)], [Fk(13)])
            act(rstd, msq, AF.Ln, [Fk(13)], [Fk(14)], bias=LN_EPS)
            act(rstd, rstd, AF.Exp, [Fk(14)], [Fk(14)], scale=-0.5)
            for m in range(8):
                xm = xa[:, m, :]
                tt(DVE, xm, xm, mean_sb, ALU.subtract, [("xa", m), Fk(12)], [("xa", m)])
                tt(POOL, xm, xm, rstd, ALU.mult, [("xa", m), Fk(14)], [("xa", m)])
                if Au is not None:
                    act(u[:, m, :], xm, AF.Identity, [("xa", m), "DV"], [("u", m)], scale=Au[:, m:m + 1], bias=Bu[:, m:m + 1])
                act(xm, xm, AF.Identity, [("xa", m), "DV"], [("xa", m)], scale=Ax[:, m:m + 1], bias=Bx[:, m:m + 1])

        for t in range(NT):
            t0 = t * T
            xin = FA[:, 0:8, :].rearrange("p a t -> p (a t)").rearrange("p (s f) -> p s f", f=1024)
            fk8 = [Fk(i) for i in range(8)]
            for s_ in range(NSUB):
                dma(POOL, xin[:, s_, :], x_d[t0 + s_ * 128:t0 + (s_ + 1) * 128, :], [], fk8, "xld%d" % s_)
            dma(POOL, post[:], pos_d[:, t0:t0 + T], [], ["post"], "pld")
            for j in range(8):
                pb = nextproj()
                for s_ in range(NSUB):
                    tr(ps[pb][:, s_ * 128:(s_ + 1) * 128], xin[:, s_, j * 128:(j + 1) * 128], identf, fk8 + ["CST"], [PK(pb)])
                act(xa[:, j, :], ps[pb][:], AF.Identity, [PK(pb)], [("xa", j)], scale=ALPHA)
                act(u[:, j, :], ps[pb][:], AF.Identity, [PK(pb), "DV"], [("u", j)],
                    scale=DV[:, DVO["A0u"] + j:DVO["A0u"] + j + 1], bias=DV[:, DVO["B0u"] + j:DVO["B0u"] + j + 1])

            ang, kf, ki, mk_ = F(8), F(9), F(10), F(11)
            cp(DVE, ang, post[:], ["post"], [Fk(8)])
            ts1(DVE, ang, ang, CST[:, C_THETA:C_THETA + 1], ALU.mult, [Fk(8), "CST"], [Fk(8)])
            TWO_PI = 2.0 * math.pi
            PI_HI = 6.28125
            PI_LO = TWO_PI - PI_HI
            for which in (1, 0):
                src = ang
                if which == 0:
                    ts1(DVE, kf, ang, math.pi / 2.0, ALU.add, [Fk(8)], [Fk(9)])
                    cp(DVE, ang, kf, [Fk(9)], [Fk(8)])
                ts1(DVE, kf, src, 1.0 / TWO_PI, ALU.mult, [Fk(8)], [Fk(9)])
                kiv = ki.bitcast(I32)
                cp(DVE, kiv, kf, [Fk(9)], [Fk(10)])
                cp(DVE, kf, kiv, [Fk(10)], [Fk(9)])
                r_ = F(15)
                stt(r_, kf, -PI_HI, src, ALU.mult, ALU.add, [Fk(9), Fk(8)], [Fk(15)])
                stt(r_, kf, -PI_LO, r_, ALU.mult, ALU.add, [Fk(9), Fk(15)], [Fk(15)])
                ts1(DVE, mk_, r_, math.pi, ALU.is_gt, [Fk(15)], [Fk(11)])
                stt(r_, mk_, -TWO_PI, r_, ALU.mult, ALU.add, [Fk(11), Fk(15)], [Fk(15)])
                ts1(DVE, mk_, r_, -math.pi, ALU.is_lt, [Fk(15)], [Fk(11)])
                stt(r_, mk_, TWO_PI, r_, ALU.mult, ALU.add, [Fk(11), Fk(15)], [Fk(15)])
                ts(DVE, r_, r_, 3.1415925, -3.1415925, ALU.min, ALU.max, [Fk(15)], [Fk(15)])
                act(cs[:, which, :], r_, AF.Sin, [Fk(15)], [("cs", which)])
            cosT, sinT = cs[:, 0, :], cs[:, 1, :]
            if t == 0:
                dump("d_cs", cs[:], [("cs", 0), ("cs", 1)], [128, 2, T], F32)
                dump("d_u0", u[:], ukeys, [128, 8, T], BF16)
                dump("d_xa0", xa[:], xakeys, [128, 8, T], F32)

            for l in range(NL):
                ub = l * UPL
                OGF = big[:, 0:8, :]
                RGOF = big[:, 8:24, :]
                sbuf_s = big[:, 24:32, :]
                def hg_names(h):
                    par = h % 2
                    fo, bo, smo = par * 8, par * 8, par * 32
                    Fs = [F(fo + i) for i in range(8)]
                    Fks = [Fk(fo + i) for i in range(8)]
                    Bs = [B(bo + i) for i in range(8)]
                    Bks = [Bk(bo + i) for i in range(8)]
                    return par, fo, Fs, Fks, Bs, Bks, smo

                def hg_proj(h):
                    wv, wk = next_unit(ub + h)
                    pf, pq = nextproj(), nextproj()
                    for kc in range(8):
                        mm(ps[pf][:], wv[:, kc, 128:256], u[:, kc, :], kc == 0, kc == 7, [wk, ("u", kc)], [PK(pf)])
                    for kc in range(8):
                        mm(ps[pq][:], wv[:, kc, 0:128], u[:, kc, :], kc == 0, kc == 7, [wk, ("u", kc)], [PK(pq)])
                    pv0, pv1 = nextproj(), nextproj()
                    for s_ in range(NSUB):
                        pv = pv0 if s_ < 2 else pv1
                        for kc in range(8):
                            mm(ps[pv][:, (s_ % 2) * 256:(s_ % 2 + 1) * 256], u[:, kc, s_ * 128:(s_ + 1) * 128], wv[:, kc, 256:512],
                               kc == 0, kc == 7, [wk, ("u", kc)], [PK(pv)])
                    return pf, pq, pv0, pv1

                def hg_wave1(h, banks):
                    pf, pq, pv0, pv1 = banks
                    par, fo, Fs, Fks, Bs, Bks, smo = hg_names(h)
                    E, L1, L2, NZ, EQ, QC, BT, EG = Fs
                    kE, kL1, kL2, kNZ, kEQ, kQC, kBT, kEG = Fks
                    V, GC = Bs[3], Bs[7]
                    kV, kGC = Bks[3], Bks[7]
                    lbc = DV[:, DVO["lb"] + l * 8 + h:DVO["lb"] + l * 8 + h + 1]
                    V3 = V.rearrange("p (s e) -> p s e", e=128)
                    GC3 = GC.rearrange("p (s e) -> p s e", e=128)
                    EG3 = EG.rearrange("p (s e) -> p s e", e=128)
                    act(E, ps[pf][:], AF.Exp, [PK(pf)], [kE], scale=-1.0)
                    act(EQ, ps[pq][:], AF.Exp, [PK(pq)], [kEQ], scale=-1.0)
                    act(QC, ps[pq][:], AF.Identity, [PK(pq)], [kQC])
                    for half, pv in ((0, pv0), (1, pv1)):
                        pvv = ps[pv][:].rearrange("p (s c) -> p s c", c=256)
                        act(V3[:, half * 2:half * 2 + 2, :], pvv[:, :, 0:128], AF.Identity, [PK(pv)], [kV])
                        act(EG3[:, half * 2:half * 2 + 2, :], pvv[:, :, 128:256], AF.Exp, [PK(pv)], [kEG], scale=-1.0)
                        act(GC3[:, half * 2:half * 2 + 2, :], pvv[:, :, 128:256], AF.Identity, [PK(pv)], [kGC])
                    act(L1, E, AF.Ln, [kE], [kL1], bias=1.0)
                    act(L2, E, AF.Ln, [kE, "DV"], [kL2], scale=lbc, bias=1.0)
                    act(NZ, E, AF.Ln, [kE], [kNZ])
                    act(EQ, EQ, AF.Ln, [kEQ], [kEQ], bias=1.0)
                    act(EG, EG, AF.Ln, [kEG], [kEG], bias=1.0)
                    act(EG, EG, AF.Exp, [kEG], [kEG], scale=-1.0)

                def hg_mid(h):
                    par, fo, Fs, Fks, Bs, Bks, smo = hg_names(h)
                    E, L1, L2, NZ, EQ, QC, BT, EG = Fs
                    kE, kL1, kL2, kNZ, kEQ, kQC, kBT, kEG = Fks
                    SG, GC = Bs[4], Bs[7]
                    kSG, kGC = Bks[4], Bks[7]
                    smk = ("sm", par)
                    nbm = sm[:, smo + 0:smo + 4]
                    bl = sm[:, smo + 24:smo + 28]
                    tt(POOL, L2, L2, L1, ALU.subtract, [kL2, kL1], [kL2])
                    tt(POOL, NZ, L1, NZ, ALU.subtract, [kL1, kNZ], [kNZ])
                    S.op(DVE, lambda e, BT=BT, L2=L2: e.tensor_tensor_scan(out=BT, data0=CST[:, C_SCANM:C_SCANM + 512], data1=L2, initial=0.0, op0=ALU.mult, op1=ALU.add),
                         ["CST", kL2], [kBT])
                    BT3 = BT.rearrange("p (c t) -> p c t", t=128)
                    ts1(DVE, nbm, BT3[:, :, 63], -1.0, ALU.mult, [kBT], [smk])
                    tt(DVE, BT3, BT3, nbm.unsqueeze(2).to_broadcast([128, 4, 128]), ALU.add, [kBT, smk], [kBT])
                    tt(DVE, bl, BT3[:, :, 127], nbm, ALU.subtract, [kBT, smk], [smk])
                    tt(POOL, EQ, BT, EQ, ALU.subtract, [kBT, kEQ], [kEQ])
                    tt(POOL, NZ, NZ, BT, ALU.add, [kNZ, kBT], [kNZ])
                    tt(DVE, SG, GC, EG, ALU.mult, [kGC, kEG], [kSG])

                def hg_wave2(h):
                    par, fo, Fs, Fks, Bs, Bks, smo = hg_names(h)
                    E, L1, L2, NZ, EQ, QC, BT, EG = Fs
                    kE, kL1, kL2, kNZ, kEQ, kQC, kBT, kEG = Fks
                    Kp, kKp = Bs[1], Bks[1]
                    smk = ("sm", par)
                    lnomc = DV[:, DVO["lnom"] + l * 8 + h:DVO["lnom"] + l * 8 + h + 1]
                    BT3 = BT.rearrange("p (c t) -> p c t", t=128)
                    act(EQ, EQ, AF.Exp, [kEQ], [kEQ])
                    act(Kp, NZ, AF.Exp, [kNZ, "DV"], [kKp], scale=-1.0, bias=lnomc)
                    act(sm[:, smo + 4:smo + 8], sm[:, smo + 0:smo + 4], AF.Exp, [smk], [smk], scale=-1.0)
                    act(sm[:, smo + 12:smo + 16], BT3[:, :, 127], AF.Exp, [kBT], [smk])
                    act(sm[:, smo + 8:smo + 12], sm[:, smo + 24:smo + 28], AF.Exp, [smk], [smk])

                def hg_post(h):
                    par, fo, Fs, Fks, Bs, Bks, smo = hg_names(h)
                    E, L1, L2, NZ, EQ, QC, BT, EG = Fs
                    kE, kL1, kL2, kNZ, kEQ, kQC, kBT, kEG = Fks
                    Qp, Kp, KT, V, SG, SCM, OG, GC = Bs
                    kQp, kKp, kKT, kV, kSG, kSCM, kOG, kGC = Bks
                    smk = ("sm", par)
                    ebm = sm[:, smo + 4:smo + 8]
                    dl = sm[:, smo + 8:smo + 12]
                    dlm = sm[:, smo + 12:smo + 16]
                    ss = sm[:, smo + 16:smo + 20]
                    rs = sm[:, smo + 20:smo + 24]
                    V3 = V.rearrange("p (s e) -> p s e", e=128)
                    SG3 = SG.rearrange("p (s e) -> p s e", e=128)
                    tt(DVE, Qp, QC, EQ, ALU.mult, [kQC, kEQ], [kQp])
                    p4b = ps[4][:].bitcast(BF16)
                    for c in range(4):
                        tr(p4b[:, c * 128:(c + 1) * 128], Kp[:, c * 128:(c + 1) * 128], identb[:], [kKp, "identb"], [PK(4)])
                    act(KT, p4b[:, 0:512], AF.Identity, [PK(4)], [kKT])
                    KT3 = KT.rearrange("p (c d) -> p c d", d=128)
                    for c in range(4):
                        c0 = c * 128
                        mm(ps[5][0:64, c0:c0 + 128], Kp[:, c0:c0 + 64], Qp[:, c0:c0 + 128], True, True, [kKp, kQp], [PK(5)])
                        mm(ps[5][64:128, c0 + 64:c0 + 128], Kp[:, c0 + 64:c0 + 128], Qp[:, c0 + 64:c0 + 128], True, True, [kKp, kQp], [PK(5)])
                    S.op(POOL, lambda e, SCM=SCM: e.memset(SCM, 0.0), [], [kSCM])
                    S.op(DVE, lambda e, SCM=SCM: e.copy_predicated(out=SCM, mask=maski[:].rearrange("p c i -> p (c i)"), data=ps[5][:]), [PK(5), "maski", kSCM], [kSCM])
                    SCM3 = SCM.rearrange("p (c i) -> p c i", i=128)
                    for c in range(4):
                        mm(ps[6][:, c * 128:(c + 1) * 128], KT3[:, c, :], V3[:, c, :], True, True, [kKT, kV], [PK(6)])
                    Sst = SH[:, l * 8 + h, :]
                    kS = ("SH", l * 8 + h)
                    for c in range(4):
                        ts1(DVE, HSB[:, c, :], Sst, ebm[:, c:c + 1], ALU.mult, [kS, smk], [("HSB", c)])
                        dst = FA[:, fo + 7, c * 128:(c + 1) * 128]
                        ts1(DVE, dst, ps[6][:, c * 128:(c + 1) * 128], dlm[:, c:c + 1], ALU.mult, [PK(6), smk, kEG], [kEG])
                        stt(Sst, Sst, dl[:, c:c + 1], dst, ALU.mult, ALU.add, [kS, smk, kEG], [kS])
                    for c in range(4):
                        mm(ps[7][:, c * 128:(c + 1) * 128], SCM3[:, c, :], V3[:, c, :], True, False, [kSCM, kV], [PK(7)])
                        mm(ps[7][:, c * 128:(c + 1) * 128], Qp[:, c * 128:(c + 1) * 128], HSB[:, c, :], False, True, [kQp, ("HSB", c)], [PK(7)])
                    act(E, ps[7][:], AF.Square, [PK(7)], [kE])
                    S.op(DVE, lambda e, ss=ss, E=E: e.tensor_reduce(out=ss, in_=E.rearrange("p (c t) -> p c t", t=128), axis=AX.X, op=ALU.add), [kE], [smk])
                    act(rs, ss, AF.Ln, [smk], [smk], scale=1.0 / 128.0, bias=HN_EPS)
                    act(rs, rs, AF.Exp, [smk], [smk], scale=-0.5)
                    OG3 = OG.rearrange("p (c e) -> p c e", e=128)
                    for c in range(4):
                        stt(OG3[:, c, :], ps[7][:, c * 128:(c + 1) * 128], rs[:, c:c + 1], SG3[:, c, :], ALU.mult, ALU.mult, [PK(7), smk, kSG], [kOG])
                    for c in range(4):
                        tr(p4b[:, c * 128:(c + 1) * 128], OG3[:, c, :], identb[:], [kOG, "identb"], [PK(4)])
                    act(OGF[:, h, :], p4b[:, 0:512], AF.Identity, [PK(4)], [("big", h)])

                hbanks = hg_proj(0)
                hg_wave1(0, hbanks)
                for h in range(8):
                    if h + 1 < 8:
                        hbanks = hg_proj(h + 1)
                    hg_mid(h)
                    if h + 1 < 8:
                        hg_wave1(h + 1, hbanks)
                    hg_wave2(h)
                    hg_post(h)

                for h in range(4):
                    gam = GAMMAS[h]
                    fo = 0 if h % 2 == 0 else 8
                    sidx = (l * 4 + h) * 2
                    kSR = [("SR", sidx), ("SR", sidx + 1)]
                    for dh in range(2):
                        act(RSB[:, 0, dh, :], SR[:, sidx + dh, :], AF.Identity, [kSR[dh]], [("RSB", 0, dh)])
                    wvA, wkA = next_unit(ub + 8 + h * 3)
                    QR = [B(14), B(15)]
                    KR = [B(16), B(17)]
                    kQR = [Bk(14), Bk(15)]
                    kKR = [Bk(16), Bk(17)]
                    for qk in range(2):
                        p1, p2 = nextproj(), nextproj()
                        for half, pb in ((0, p1), (1, p2)):
                            cb = qk * 256 + half * 128
                            for kc in range(8):
                                mm(ps[pb][:], wvA[:, kc, cb:cb + 128], u[:, kc, :], kc == 0, kc == 7, [wkA, ("u", kc)], [PK(pb)])
                        t1, t2_, t3, t4 = F(fo + 0), F(fo + 1), F(fo + 2), F(fo + 3)
                        k1, k2, k3, k4 = Fk(fo + 0), Fk(fo + 1), Fk(fo + 2), Fk(fo + 3)
                        R = QR if qk == 0 else KR
                        kR = kQR if qk == 0 else kKR
                        tt(DVE, t1, ps[p1][:], cosT, ALU.mult, [PK(p1), ("cs", 0)], [k1])
                        tt(DVE, t2_, ps[p2][:], sinT, ALU.mult, [PK(p2), ("cs", 1)], [k2])
                        tt(POOL, R[0], t1, t2_, ALU.subtract, [k1, k2], [kR[0]])
                        tt(DVE, t3, ps[p1][:], sinT, ALU.mult, [PK(p1), ("cs", 1)], [k3])
                        tt(DVE, t4, ps[p2][:], cosT, ALU.mult, [PK(p2), ("cs", 0)], [k4])
                        tt(POOL, R[1], t3, t4, ALU.add, [k3, k4], [kR[1]])
                    wvB, wkB = next_unit(ub + 8 + h * 3 + 1)
                    Vr = BA[:, 0:4, :]
                    SRG = BA[:, 4:8, :]
                    for s_ in range(NSUB):
                        pb = nextproj()
                        for kc in range(8):
                            mm(ps[pb][:], u[:, kc, s_ * 128:(s_ + 1) * 128], wvB[:, kc, :], kc == 0, kc == 7, [wkB, ("u", kc)], [PK(pb)])
                        cp(DVE, Vr[:, s_, :], ps[pb][:], [PK(pb)], [Bk(s_)])
                    wvC, wkC = next_unit(ub + 8 + h * 3 + 2)
                    for s_ in range(NSUB):
                        pb = nextproj()
                        for kc in range(8):
                            mm(ps[pb][:], u[:, kc, s_ * 128:(s_ + 1) * 128], wvC[:, kc, :], kc == 0, kc == 7, [wkC, ("u", kc)], [PK(pb)])
                        eg = F(fo + 4 + s_)
                        keg = Fk(fo + 4 + s_)
                        act(eg, ps[pb][:], AF.Exp, [PK(pb)], [keg], scale=-1.0)
                        act(eg, eg, AF.Ln, [keg], [keg], bias=1.0)
                        act(eg, eg, AF.Exp, [keg], [keg], scale=-1.0)
                        tt(DVE, SRG[:, s_, :], ps[pb][:], eg, ALU.mult, [PK(pb), keg], [Bk(4 + s_)])
                    KTr = BA[:, 8:10, :].rearrange("p a t -> p (a t)").rearrange("p (c d) -> p c d", d=256)
                    kKTr = [Bk(8), Bk(9)]
                    p4b = ps[4][:].bitcast(BF16).rearrange("p (c d) -> p c d", d=256)
                    for c in range(4):
                        for dh in range(2):
                            tr(p4b[:, c, dh * 128:(dh + 1) * 128], KR[dh][:, c * 128:(c + 1) * 128], identb[:], [kKR[dh], "identb"], [PK(4)])
                    for c in range(4):
                        act(KTr[:, c, :], p4b[:, c, :], AF.Identity, [PK(4), "CST"], kKTr, scale=CST[:, C_KDEC + h * 4 + c:C_KDEC + h * 4 + c + 1])
                    sbank = {0: (5, 0), 1: (6, 0), 2: (7, 0), 3: (7, 256)}
                    for cq in range(4):
                        bk_, o0 = sbank[cq]
                        for c in range(cq, 4):
                            oc = o0 + (c - cq) * 128
                            for dh in range(2):
                                mm(ps[bk_][:, oc:oc + 128], KR[dh][:, cq * 128:(cq + 1) * 128], QR[dh][:, c * 128:(c + 1) * 128], dh == 0, dh == 1,
                                   [kKR[dh], kQR[dh]], [PK(bk_)])
                    SCB = BA[:, 10:13, :].rearrange("p a t -> p (a t)")
                    kSCB = [Bk(10), Bk(11), Bk(12)]
                    bi0 = {0: 0, 1: 4, 2: 7, 3: 9}
                    Mh_ = CST[:, C_MH + h * 128:C_MH + (h + 1) * 128]
                    for cq in range(4):
                        bk_, o0 = sbank[cq]
                        b0 = bi0[cq] * 128
                        stt(SCB[:, b0:b0 + 128], ps[bk_][:, o0:o0 + 128], float(gam ** (-(cq * 128.0))), Mh_, ALU.mult, ALU.mult, [PK(bk_), "CST"], kSCB)
                        nof = 3 - cq
                        if nof > 0:
                            ts1(DVE, SCB[:, b0 + 128:b0 + 128 + nof * 128], ps[bk_][:, o0 + 128:o0 + 128 + nof * 128],
                                CST[:, C_KSC + h * 4 + cq:C_KSC + h * 4 + cq + 1], ALU.mult, [PK(bk_), "CST"], kSCB)
                    for dh in range(2):
                        for c in range(4):
                            mm(ps[5 + dh][:], KTr[:, c, dh * 128:(dh + 1) * 128], Vr[:, c, :], c == 0, c == 3, kKTr + [Bk(c)], [PK(5 + dh)])
                    for dh in range(2):
                        stt(SR[:, sidx + dh, :], SR[:, sidx + dh, :], float(gam ** 512.0), ps[5 + dh][:], ALU.mult, ALU.add, [kSR[dh], PK(5 + dh)], [kSR[dh]])
                    stat6 = sm[:, 96:120].rearrange("p (c s) -> p c s", s=6)
                    mv = sm[:, 64:72].rearrange("p (c s) -> p c s", s=2)
                    smr = ("sm", 2)
                    OB = [7, 0, 1, 2]
                    for c in range(4):
                        po = OB[c]
                        for cq in range(c + 1):
                            b0 = (bi0[cq] + (c - cq)) * 128
                            mm(ps[po][:], SCB[:, b0:b0 + 128], Vr[:, cq, :], cq == 0, False, kSCB + [Bk(cq)], [PK(po)])
                        for dh in range(2):
                            mm(ps[po][:], QR[dh][:, c * 128:(c + 1) * 128], RSB[:, 0, dh, :], False, dh == 1, [kQR[dh], ("RSB", 0, dh)], [PK(po)])
                        S.op(DVE, lambda e, c=c, po=po: e.bn_stats(out=stat6[:, c, :], in_=ps[po][:]), [PK(po)], [smr])
                        S.op(DVE, lambda e, c=c: e.bn_aggr(out=mv[:, c, :], in_=stat6[:, c, :]), [smr], [smr])
                    rsp4 = sm[:, 72:76]
                    nb4 = sm[:, 80:84]
                    tt(DVE, rsp4, mv[:, :, 1], CST[:, C_QD2 + h * 4:C_QD2 + h * 4 + 4], ALU.mult, [smr, "CST"], [smr])
                    act(rsp4, rsp4, AF.Ln, [smr], [smr], bias=HN_EPS)
                    act(rsp4, rsp4, AF.Exp, [smr], [smr], scale=-0.5)
                    tt(DVE, rsp4, rsp4, CST[:, C_QD + h * 4:C_QD + h * 4 + 4], ALU.mult, [smr, "CST"], [smr])
                    stt(nb4, mv[:, :, 0], -1.0, rsp4, ALU.mult, ALU.mult, [smr], [smr])
                    for c in range(4):
                        po = OB[c]
                        ON = FA[:, fo + 4 + (c % 2), :].bitcast(BF16)[:, 0:512]
                        kON = Fk(fo + 4 + (c % 2))
                        act(ON, ps[po][:], AF.Identity, [PK(po), smr], [kON], scale=rsp4[:, c:c + 1], bias=nb4[:, c:c + 1])
                        OGr = B(18 + (c % 2))
                        kOGr = Bk(18 + (c % 2))
                        tt(POOL, OGr, ON, SRG[:, c, :], ALU.mult, [kON, Bk(4 + c)], [kOGr])
                        pt = 4 if c % 2 == 0 else 3
                        ptb = ps[pt][:].bitcast(BF16)
                        for ec in range(4):
                            tr(ptb[:, ec * 128:(ec + 1) * 128], OGr[:, ec * 128:(ec + 1) * 128], identb[:], [kOGr, "identb"], [PK(pt)])
                        act(RGOF[:, h * 4:(h + 1) * 4, c * 128:(c + 1) * 128], ptb[:, 0:512].rearrange("p (a i) -> p a i", i=128), AF.Identity,
                            [PK(pt)], [("big", 8 + h * 4 + ec) for ec in range(4)])

                gate_sb = {}
                for gi in range(4):
                    wv, wk = next_unit(ub + 20 + gi)
                    for blk in range(4):
                        m = (gi % 2) * 4 + blk
                        pb = nextproj()
                        for kc in range(8):
                            mm(ps[pb][:], wv[:, kc, blk * 128:(blk + 1) * 128], u[:, kc, :], kc == 0, kc == 7, [wk, ("u", kc)], [PK(pb)])
                        si = (gi // 2) * 8 + m
                        g_ = F(si)
                        act(g_, ps[pb][:], AF.Exp, [PK(pb)], [Fk(si)], scale=-1.0)
                        act(g_, g_, AF.Ln, [Fk(si)], [Fk(si)], bias=1.0)
                        act(g_, g_, AF.Exp, [Fk(si)], [Fk(si)], scale=-1.0)
                        gate_sb[(gi // 2, m)] = si
                for ch in range(2):
                    wv, wk = next_unit(ub + 24 + ch)
                    for blk in range(4):
                        m = ch * 4 + blk
                        pb = nextproj()
                        for kc in range(8):
                            mm(ps[pb][:], wv[:, kc, blk * 128:(blk + 1) * 128], OGF[:, kc, :], kc == 0, kc == 7, [wk, ("big", kc)], [PK(pb)])
                        si = gate_sb[(0, m)]
                        tt(DVE, F(si), ps[pb][:], F(si), ALU.mult, [PK(pb), Fk(si)], [Fk(si)])
                for ch in range(2):
                    units = [next_unit(ub + 26 + ch * 2 + kh) for kh in range(2)]
                    pbs = [nextproj() for _ in range(4)]
                    for kh in range(2):
                        wv, wk = units[kh]
                        for blk in range(4):
                            for kc in range(8):
                                kk = kh * 8 + kc
                                mm(ps[pbs[blk]][:], wv[:, kc, blk * 128:(blk + 1) * 128], RGOF[:, kk, :], kk == 0, kk == 15, [wk, ("big", 8 + kk)], [PK(pbs[blk])])
                    for blk in range(4):
                        m = ch * 4 + blk
                        sa, sb_ = gate_sb[(0, m)], gate_sb[(1, m)]
                        tt(DVE, F(sb_), ps[pbs[blk]][:], F(sb_), ALU.mult, [PK(pbs[blk]), Fk(sb_)], [Fk(sb_)])
                        tt(POOL, sbuf_s[:, m, :], F(sa), F(sb_), ALU.add, [Fk(sa), Fk(sb_)], [("big", 24 + m)])
                if t == 0:
                    dump("d_big%d" % l, big[:], bigkeys, [128, 32, T], BF16)
                G1h = DVs(("G1h", l))
                for ch in range(2):
                    wv, wk = next_unit(ub + 30 + ch)
                    for blk in range(4):
                        m = ch * 4 + blk
                        pb = nextproj()
                        for kc in range(8):
                            mm(ps[pb][:], wv[:, kc, blk * 128:(blk + 1) * 128], sbuf_s[:, kc, :], kc == 0, kc == 7, [wk, ("big", 24 + kc)], [PK(pb)])
                        stt(xa[:, m, :], ps[pb][:], G1h[:, m:m + 1], xa[:, m, :], ALU.mult, ALU.add, [PK(pb), "DV", ("xa", m)], [("xa", m)])
                        ln_accum(m)
                ln_finish(DVs(("Ax1", l)), DVs(("Bx1", l)), DVs(("Au1", l)), DVs(("Bu1", l)))

                if t == 0:
                    dump("d_xa1_%d" % l, xa[:], xakeys, [128, 8, T], F32)
                    dump("d_u1_%d" % l, u[:], ukeys, [128, 8, T], BF16)
                bup = Vs("b_up", l, 32)
                for un in range(8):
                    wv, wk = next_unit(ub + 32 + un)
                    for blk in range(4):
                        j = un * 4 + blk
                        pb = nextproj()
                        for kc in range(8):
                            mm(ps[pb][:], wv[:, kc, blk * 128:(blk + 1) * 128], u[:, kc, :], kc == 0, kc == 7, [wk, ("u", kc)], [PK(pb)])
                        hr = F(j % 4)
                        act(hr, ps[pb][:], AF.Relu, [PK(pb), "VEC"], [Fk(j % 4)], bias=bup[:, j:j + 1])
                        tt(DVE if j % 2 == 0 else POOL, big[:, j, :], hr, hr, ALU.mult, [Fk(j % 4)], [("big", j)])
                G2 = DVs(("G2", l))
                for ch in range(2):
                    units = [next_unit(ub + 40 + ch * 4 + kq) for kq in range(4)]
                    pbs = [0, 1, 2, 3] if ch == 0 else [6, 7, 0, 1]
                    for kq in range(4):
                        wv, wk = units[kq]
                        for blk in range(4):
                            for kc in range(8):
                                kk = kq * 8 + kc
                                mm(ps[pbs[blk]][:], wv[:, kc, blk * 128:(blk + 1) * 128], big[:, kk, :], kk == 0, kk == 31, [wk, ("big", kk)], [PK(pbs[blk])])
                    for blk in range(4):
                        m = ch * 4 + blk
                        stt(xa[:, m, :], ps[pbs[blk]][:], G2[:, m:m + 1], xa[:, m, :], ALU.mult, ALU.add, [PK(pbs[blk]), "DV", ("xa", m)], [("xa", m)])
                        ln_accum(m)
                last = (l == NL - 1)
                ln_finish(DVs(("Ax2", l)), DVs(("Bx2", l)), None if last else DVs(("Au2", l)), None if last else DVs(("Bu2", l)))

            for s_ in range(NSUB):
                st_ = FA[:, (s_ % 2) * 2:(s_ % 2) * 2 + 2, :].rearrange("p a t -> p (a t)")
                stk = [Fk((s_ % 2) * 2), Fk((s_ % 2) * 2 + 1)]
                for jh in range(2):
                    pb = nextproj()
                    for jj in range(4):
                        j = jh * 4 + jj
                        tr(ps[pb][:, jj * 128:(jj + 1) * 128], xa[:, j, s_ * 128:(s_ + 1) * 128], identf, [("xa", j), "CST"], [PK(pb)])
                    act(st_[:, jh * 512:(jh + 1) * 512], ps[pb][:], AF.Identity, [PK(pb)], [stk[jh]])
                dma(POOL, out_d[t0 + s_ * 128:t0 + (s_ + 1) * 128, :], st_, stk, [], "ost%d" % (s_ % 2))

        fin = S.op(POOL, lambda e: e.nop(), [], [])
        for ch in list(S.dma_last):
            if S.dma_last[ch] is not fin:
                fin.deps.append(S.dma_last[ch])

        S.finalize()
        engsem = {e: es.enter_context(nc.semaphore("sem_" + e)) for e in ENGS}
        dmasem = {ch: es.enter_context(nc.semaphore("dsem_" + ch)) for ch in S.dma_count}
        with nc.Block() as block:
            @block.sync
            def _(e):
                S.emit(e, SP, engsem, dmasem)

            @block.tensor
            def _(e):
                S.emit(e, PE, engsem, dmasem)

            @block.scalar
            def _(e):
                S.emit(e, ACT, engsem, dmasem)

            @block.vector
            def _(e):
                S.emit(e, DVE, engsem, dmasem)

            @block.gpsimd
            def _(e):
                S.emit(e, POOL, engsem, dmasem)
    return nc


def _unit(Wm, rows, cols):
    sub = Wm[rows][:, cols]
    return np.ascontiguousarray(sub.reshape(8, 128, 512).transpose(1, 0, 2)).reshape(128, 4096)


def _pack_weights(w_in, w_pa, w_pb, w_o, w_up, w_down):
    units = []
    r1024 = np.arange(1024)
    for l in range(DEPTH):
        Wi = w_in[l]
        for h in range(8):
            cols = np.concatenate([0 + h * 128 + np.arange(128), 1024 + h * 128 + np.arange(128),
                                   2048 + h * 128 + np.arange(128), 3072 + h * 128 + np.arange(128)])
            units.append(_unit(Wi, r1024, cols))
        for h in range(4):
            qb = 4096 + h * 256
            kb = 5120 + h * 256
            ev = np.arange(0, 256, 2)
            od = np.arange(1, 256, 2)
            units.append(_unit(Wi, r1024, np.concatenate([qb + ev, qb + od, kb + ev, kb + od])))
            units.append(_unit(Wi, r1024, 6144 + h * 512 + np.arange(512)))
            units.append(_unit(Wi, r1024, 8192 + h * 512 + np.arange(512)))
        for gi in range(4):
            units.append(_unit(Wi, r1024, 10240 + gi * 512 + np.arange(512)))
        for ch in range(2):
            units.append(_unit(w_pa[l], r1024, ch * 512 + np.arange(512)))
        for ch in range(2):
            for kh in range(2):
                units.append(_unit(w_pb[l], kh * 1024 + r1024, ch * 512 + np.arange(512)))
        for ch in range(2):
            units.append(_unit(w_o[l], r1024, ch * 512 + np.arange(512)))
        for un in range(8):
            units.append(_unit(w_up[l], r1024, un * 512 + np.arange(512)))
        for ch in range(2):
            for kq in range(4):
                units.append(_unit(w_down[l], kq * 1024 + r1024, ch * 512 + np.arange(512)))
    return np.stack(units, axis=0)


def _fm(v, n):
    return np.ascontiguousarray(np.asarray(v, np.float32).reshape(n, 128).T)


def _consts():
    C = np.zeros((128, NCONST), np.float32)
    C[:, C_IDENT:C_IDENT + 128] = np.eye(128, dtype=np.float32)
    j = np.arange(128)[:, None]
    i = np.arange(128)[None, :]
    C[:, C_MASKT:C_MASKT + 128] = (j <= i).astype(np.float32)
    jj = np.arange(128, dtype=np.float64)
    for h in range(4):
        g = np.float64(GAMMAS[h])
        C[:, C_MH + h * 128:C_MH + (h + 1) * 128] = ((j <= i) * (g ** (-(j + 1.0))) / 16.0).astype(np.float32)
        for c in range(4):
            C[:, C_KSC + h * 4 + c] = (g ** (-(c * 128.0 + jj + 1.0)) / 16.0).astype(np.float32)
            C[:, C_KDEC + h * 4 + c] = (g ** (511.0 - c * 128.0 - jj) / 16.0).astype(np.float32)
            C[:, C_QD + h * 4 + c] = (g ** (c * 128.0 + jj + 1.0)).astype(np.float32)
            C[:, C_QD2 + h * 4 + c] = (g ** (2.0 * (c * 128.0 + jj + 1.0))).astype(np.float32)
    sc = np.ones(512, np.float32)
    sc[::128] = 0.0
    C[:, C_SCANM:C_SCANM + 512] = sc[None, :]
    C[:, C_THETA] = (10000.0 ** (-np.linspace(0.0, 1.0, 128, dtype=np.float32))).astype(np.float32)
    return C


_NC_CACHE = {}


def kernel(x, c, positions, lb_logits, w_ada, b_ada, w_in, w_pa, w_pb, w_o,
           ln1_g, ln1_b, w_up, b_up, w_down, b_down, ln2_g, ln2_b, _NT=8, _cores=8, _dbg=False):
    x = np.asarray(x, np.float32)
    wall = _pack_weights(*[np.asarray(a, np.float32) for a in (w_in, w_pa, w_pb, w_o, w_up, w_down)])
    w_ada = np.asarray(w_ada, np.float32)
    wada = np.stack([np.ascontiguousarray(w_ada[l][:, g * 1024:(g + 1) * 1024].reshape(8, 128, 1024).transpose(1, 0, 2)).reshape(128, 8192)
                     for l in range(DEPTH) for g in range(6)], axis=0)
    consts = _consts()
    in_maps = []
    for b in range(_cores):
        V = np.zeros((128, NVEC), np.float32)
        for l in range(DEPTH):
            for nm, arr, n in (("ln1_g", ln1_g, 8), ("ln1_b", ln1_b, 8), ("ln2_g", ln2_g, 8), ("ln2_b", ln2_b, 8),
                               ("b_down", b_down, 8), ("b_up", b_up, 32), ("b_ada", b_ada, 48), ("lbl", lb_logits, 8)):
                V[:, VOFF[(nm, l)]:VOFF[(nm, l)] + n] = _fm(np.asarray(arr)[l], n)
        V[:, VOFF["c"]:VOFF["c"] + 8] = _fm(np.asarray(c)[b], 8)
        pos = np.ascontiguousarray(np.broadcast_to(np.asarray(positions)[b].astype(np.int32)[None, :], (128, SEQ)))
        in_maps.append({"x": np.ascontiguousarray(x[b]), "pos": pos, "wall": wall, "wada": wada, "vecs": V, "consts": consts})
    key = (_NT, _dbg)
    if key not in _NC_CACHE:
        _NC_CACHE[key] = build_nc(NT=_NT, dbg=_dbg)
    nc = _NC_CACHE[key]
    res = run_bass_kernel_spmd(nc, in_maps, core_ids=list(range(_cores)))
    out = np.stack([np.asarray(r["out"], np.float32) for r in res.results], axis=0)
    if _dbg:
        return out, res.results[0]
    if _cores < 8:
        return out
    return out.reshape(8, SEQ, D)
```
